# Optimizing a Trainium2 kernel written in Bass

```python
import jax, jax.numpy as jnp
from jax import lax
import numpy as np

D_MODEL = 1024
BATCH = 4
SEQ = 8192
DEPTH = 2

N_MIXERS = 2
N_ATT_LAYERS = (DEPTH + 1) // 2
N_HGRN_LAYERS = DEPTH // 2
FOX_HEADS = 16
FOX_HEAD_DIM = D_MODEL // FOX_HEADS
FOX_Q_BLOCK = 128
HGRN_EXPAND = 128
HGRN_HEADS = D_MODEL // HGRN_EXPAND
HGRN_DK = HGRN_EXPAND
HGRN_DV = D_MODEL // HGRN_HEADS
HGRN_CHUNK = 64
FFN_DIM = 2816
CONV_WIDTH = 3
RMS_EPS = 1e-6

kernel_name = "fox_hgrn2_convffn_hybrid"


def rmsnorm(x, g):
    x32 = x.astype(jnp.float32)
    y = x32 * lax.rsqrt(jnp.mean(x32 * x32, axis=-1, keepdims=True) + RMS_EPS)
    return (y * g.astype(jnp.float32)).astype(x.dtype)


def fox_mixer(h, w_in, b_f, w_out):
    B, S, D = h.shape
    H, hd, QB = FOX_HEADS, FOX_HEAD_DIM, FOX_Q_BLOCK
    nqb = S // QB
    proj = h @ w_in
    q, k, v, fl = jnp.split(proj, [D, 2 * D, 3 * D], axis=-1)
    logf = jax.nn.log_sigmoid((fl + b_f).astype(jnp.float32))
    c = jnp.cumsum(logf, axis=1).transpose(0, 2, 1)
    qb = q.reshape(B, nqb, QB, H, hd).transpose(1, 0, 3, 2, 4)
    k = k.reshape(B, S, H, hd).transpose(0, 2, 1, 3)
    v = v.reshape(B, S, H, hd).transpose(0, 2, 1, 3)
    cqb = c.reshape(B, H, nqb, QB).transpose(2, 0, 1, 3)
    starts = jnp.arange(nqb, dtype=jnp.int32) * QB
    s_pos = jnp.arange(S, dtype=jnp.int32)
    scale = 1.0 / np.sqrt(hd)

    def block(args):
        qi, cqi, st = args
        logits = jnp.einsum('bhqd,bhkd->bhqk', qi, k).astype(jnp.float32) * scale
        logits = logits + (cqi[..., :, None] - c[:, :, None, :])
        t_pos = st + jnp.arange(QB, dtype=jnp.int32)
        logits = jnp.where(s_pos[None, :] <= t_pos[:, None], logits, -jnp.inf)
        p = jax.nn.softmax(logits, axis=-1)
        return jnp.einsum('bhqk,bhkd->bhqd', p.astype(v.dtype), v)

    o = lax.map(block, (qb, cqb, starts))
    o = o.transpose(1, 0, 3, 2, 4).reshape(B, S, D)
    return o @ w_out


def hgrn2_mixer(h, w_in, lb, onorm_g, w_out):
    B, S, D = h.shape
    H, Dk, Dv, C = HGRN_HEADS, HGRN_DK, HGRN_DV, HGRN_CHUNK
    nc = S // C
    proj = h @ w_in
    q, fl, i, g = jnp.split(proj, 4, axis=-1)
    q = jax.nn.silu(q).astype(jnp.float32)
    f = lb + (1.0 - lb) * jax.nn.sigmoid(fl.astype(jnp.float32))
    k = 1.0 - f
    logf = jnp.log(f)

    def to_chunks(t, d):
        return t.reshape(B, nc, C, H, d).transpose(0, 3, 1, 2, 4)

    q, k, logf = to_chunks(q, Dk), to_chunks(k, Dk), to_chunks(logf, Dk)
    iv = to_chunks(i.astype(jnp.float32), Dv)
    b = jnp.cumsum(logf, axis=3)
    b_last = b[..., -1:, :]
    q_dec = q * jnp.exp(b)
    k_inv = k * jnp.exp(-b)
    k_state = k * jnp.exp(b_last - b)
    A = jnp.einsum('bhncd,bhnsd->bhncs', q_dec, k_inv)
    causal = jnp.tril(jnp.ones((C, C), dtype=bool))
    A = jnp.where(causal, A, 0.0)
    o_intra = jnp.einsum('bhncs,bhnsv->bhncv', A, iv)
    U = jnp.einsum('bhnsd,bhnsv->bhndv', k_state, iv)
    decay = jnp.exp(b_last[..., 0, :])

    def step(state, xs):
        dec, u = xs
        return dec[..., None] * state + u, state

    s0 = jnp.zeros((B, H, Dk, Dv), jnp.float32)
    _, s_prev = lax.scan(step, s0, (jnp.moveaxis(decay, 2, 0), jnp.moveaxis(U, 2, 0)))
    s_prev = jnp.moveaxis(s_prev, 0, 2)
    o = o_intra + jnp.einsum('bhncd,bhndv->bhncv', q_dec, s_prev)
    o = o.transpose(0, 2, 3, 1, 4).reshape(B, S, H, Dv)
    o = o * lax.rsqrt(jnp.mean(o * o, axis=-1, keepdims=True) + RMS_EPS)
    o = o * onorm_g.astype(jnp.float32).reshape(H, Dv)
    o = o.reshape(B, S, D).astype(h.dtype) * jax.nn.silu(g)
    return o @ w_out


def conv_ffn(h, w_up, conv_w, conv_b, w_down):
    S = h.shape[1]
    u = h @ w_up
    up = jnp.pad(u, ((0, 0), (CONV_WIDTH - 1, 0), (0, 0)))
    y = conv_b
    for j in range(CONV_WIDTH):
        y = y + conv_w[j] * up[:, j:j + S]
    gate, val = jnp.split(y, 2, axis=-1)
    return (jax.nn.silu(gate) * val) @ w_down


def setup_inputs(seed: int = 0) -> dict:
    key = jax.random.key(seed)
    ks = jax.random.split(key, 16)
    D, F = D_MODEL, FFN_DIM
    nrm = lambda k, shape, s: jax.random.normal(k, shape, jnp.float32) * s
    return {
        "x": nrm(ks[0], (BATCH, SEQ, D), 1.0),
        "att_norm_g": 1.0 + nrm(ks[1], (N_ATT_LAYERS, D), 0.02),
        "att_w_in": nrm(ks[2], (N_ATT_LAYERS, D, 3 * D + FOX_HEADS), D ** -0.5),
        "att_b_f": 3.0 + nrm(ks[3], (N_ATT_LAYERS, FOX_HEADS), 0.5),
        "att_w_out": nrm(ks[4], (N_ATT_LAYERS, D, D), D ** -0.5),
        "hgrn_norm_g": 1.0 + nrm(ks[5], (N_HGRN_LAYERS, D), 0.02),
        "hgrn_w_in": nrm(ks[6], (N_HGRN_LAYERS, D, 4 * D), D ** -0.5),
        "hgrn_lb_logits": nrm(ks[7], (DEPTH, D), 0.1),
        "hgrn_onorm_g": 1.0 + nrm(ks[8], (N_HGRN_LAYERS, D), 0.02),
        "hgrn_w_out": nrm(ks[9], (N_HGRN_LAYERS, D, D), D ** -0.5),
        "ffn_norm_g": 1.0 + nrm(ks[10], (DEPTH, D), 0.02),
        "ffn_w_up": nrm(ks[11], (DEPTH, D, 2 * F), D ** -0.5),
        "ffn_conv_w": nrm(ks[12], (DEPTH, CONV_WIDTH, 2 * F), CONV_WIDTH ** -0.5),
        "ffn_conv_b": nrm(ks[13], (DEPTH, 2 * F), 0.02),
        "ffn_w_down": nrm(ks[14], (DEPTH, F, D), F ** -0.5),
        "final_norm_g": 1.0 + nrm(ks[15], (D,), 0.02),
    }


def reference(x, att_norm_g, att_w_in, att_b_f, att_w_out, hgrn_norm_g, hgrn_w_in,
              hgrn_lb_logits, hgrn_onorm_g, hgrn_w_out, ffn_norm_g, ffn_w_up,
              ffn_conv_w, ffn_conv_b, ffn_w_down, final_norm_g):
    sm = jax.nn.softmax(hgrn_lb_logits.astype(jnp.float32), axis=0)
    lower_bounds = jnp.cumsum(sm, axis=0) - sm[0:1]
    for layer in range(DEPTH):
        j = layer // N_MIXERS
        if layer % N_MIXERS == 0:
            x = x + fox_mixer(rmsnorm(x, att_norm_g[j]), att_w_in[j], att_b_f[j], att_w_out[j])
        else:
            x = x + hgrn2_mixer(rmsnorm(x, hgrn_norm_g[j]), hgrn_w_in[j],
                                lower_bounds[layer], hgrn_onorm_g[j], hgrn_w_out[j])
        x = x + conv_ffn(rmsnorm(x, ffn_norm_g[layer]), ffn_w_up[layer], ffn_conv_w[layer],
                         ffn_conv_b[layer], ffn_w_down[layer])
    return rmsnorm(x, final_norm_g)
```

```python
import contextlib
import numpy as np
import concourse.bass as bass
import concourse.mybir as mybir
from concourse.bass_utils import run_bass_kernel_spmd

F32 = mybir.dt.float32
BF16 = mybir.dt.bfloat16
AF = mybir.ActivationFunctionType
ALU = mybir.AluOpType

D = 1024
S = 8192
H = 16
HD = 64
FF = 2816
NFC = 22
EPS = 1e-6
NT = S // 512
NB = S // 128

C_ATTG, C_FFNG0, C_HGRNG, C_FFNG1 = 0, 8, 16, 24
C_CONVW = 32
C_CONVB = C_CONVW + 264
C_LB = C_CONVB + 88
C_ONORM = C_LB + 16
C_BF = C_ONORM + 8
C_PM = C_BF + 1
C_M0 = C_PM + 1
C_NEG = C_M0 + 1
C_NEG8 = C_NEG + 1
NCOL = C_NEG8 + 1
T0 = 8
SO = S // 2
PAIRS = [[0, 1], [2, 3], [4, 5], [6, 7]]
K_ID, K_TRI, K_BD, K_RM = 0, 128, 256, 384
NCONST = 896


class Buf:
    __slots__ = ("w", "r", "excl")

    def __init__(self, excl=False):
        self.w = None
        self.r = {}
        self.excl = excl


class Ring:
    def __init__(self, items):
        self.items = items
        self.i = 0

    def next(self):
        it = self.items[self.i % len(self.items)]
        self.i += 1
        return it


class Sched:
    def __init__(self, nc, es, nds=40):
        self.nc = nc
        self.E = {"pe": nc.tensor, "act": nc.scalar, "dve": nc.vector, "pool": nc.gpsimd, "sp": nc.sync}
        self.cnt = {e: 0 for e in ("pe", "act", "dve", "pool")}
        self.sem = {e: es.enter_context(nc.semaphore("sm_" + e)) for e in self.cnt}
        self.dsem = [es.enter_context(nc.semaphore("sd%d" % i)) for i in range(nds)]
        self.dval = [0] * nds
        self.dn = 0
        self.known = {e: {} for e in self.E}
        self.nwait = 0
        self.nops = 0
        self.ncc = 0
        self.ccev = []

    def _wait(self, eng, ev):
        sem, val = ev
        k = self.known[eng]
        key = id(sem)
        if k.get(key, 0) >= val:
            return
        k[key] = val
        self.E[eng].wait_ge(sem, val)
        self.nwait += 1

    def _deps(self, eng, r, w, own):
        for b in r:
            ev = b.w
            if ev is not None and not (ev[0] is own and eng == "pe"):
                self._wait(eng, ev)
        for b in w:
            ev = b.w
            if ev is not None and ev[0] is not own:
                self._wait(eng, ev)
            for ev in b.r.values():
                if ev[0] is not own:
                    self._wait(eng, ev)

    @staticmethod
    def _mark(ev, r, w):
        for b in r:
            b.r[id(ev[0])] = ev
        for b in w:
            b.w = ev
            b.r = {}

    def op(self, eng, fn, r=(), w=()):
        if any(b.excl for b in r):
            w = list(w) + [b for b in r if b.excl]
            r = [b for b in r if not b.excl]
        own = self.sem[eng]
        self._deps(eng, r, w, own)
        self.cnt[eng] += 1
        ev = (own, self.cnt[eng])
        fn(self.E[eng]).then_inc(own, 1)
        self._mark(ev, r, w)
        self.nops += 1
        return ev

    def dma(self, out, in_, r=(), w=(), q="sp"):
        i = self.dn % len(self.dsem)
        self.dn += 1
        sem = self.dsem[i]
        if self.dval[i] > 0:
            self._wait(q, (sem, self.dval[i]))
        self._deps(q, r, w, None)
        self.dval[i] += 16
        ev = (sem, self.dval[i])
        self.E[q].dma_start(out=out, in_=in_).then_inc(sem, 16)
        self._mark(ev, r, w)
        self.nops += 1
        return ev

    def collective(self, es, in_ap, out_ap, r=(), w=()):
        sem = es.enter_context(self.nc.semaphore("cc%d" % self.ncc))
        self.ncc += 1
        self._deps("pool", r, w, None)
        self.nc.gpsimd.collective_compute("AllReduce", ALU.add, replica_groups=PAIRS,
                                          ins=[in_ap], outs=[out_ap]).then_inc(sem)
        ev = (sem, 1)
        self._mark(ev, r, w)
        self.ccev.append(ev)
        return ev

    def barrier(self):
        evs = [(self.sem[e], self.cnt[e]) for e in self.cnt if self.cnt[e] > 0]
        evs += [(sem, self.dval[i]) for i, sem in enumerate(self.dsem) if self.dval[i] > 0]
        evs += self.ccev
        for eng in self.E:
            own = self.sem.get(eng)
            for ev in evs:
                if ev[0] is not own:
                    self._wait(eng, ev)

    def finish(self):
        for i, sem in enumerate(self.dsem):
            if self.dval[i] > 0:
                self._wait("sp", (sem, self.dval[i]))


def build(dbg=False, upto=99):
    nc = bass.Bass("TRN2", target_bir_lowering=False)

    def I(n, shp):
        return nc.dram_tensor(n, shp, F32, kind="ExternalInput").ap()

    x_in = I("x", [S, D])
    w_att_in = I("att_w_in", [D, 3 * D + H])
    w_att_out = I("att_w_out", [D, D])
    w_h_in = I("hgrn_w_in", [D, 4 * D])
    w_h_out = I("hgrn_w_out", [D, D])
    w_up = I("ffn_w_up", [2, D, 2 * FF])
    w_dn = I("ffn_w_down", [2, FF, D])
    cols_in = I("cols", [128, NCOL])
    consts_in = I("consts", [128, NCONST])
    fgain_in = I("final_norm_g", [1, D])
    out = nc.dram_tensor("out", [SO, D], F32, kind="ExternalOutput").ap()
    sk = "ExternalOutput" if dbg else "Internal"

    def Sc(n, shp, dt):
        return nc.dram_tensor(n, shp, dt, kind=sk).ap()

    qT = Sc("qT", [D, S], BF16)
    kT = Sc("kT", [D, S], BF16)
    vv = Sc("vv", [S, D], BF16)
    caug = Sc("caug", [H, S], BF16)
    caugk = Sc("caugk", [3, H, S], BF16)
    oT = Sc("oT", [D, S], BF16)
    hTs = Sc("hTs", [D, S], BF16)
    xs = Sc("xs", [S, D], F32)
    cn_d = Sc("cn_d", [H, S], F32) if dbg else None
    halo_a = [nc.dram_tensor("halo_a%d" % l, [128, 88], F32).ap() for l in range(2)]
    halo_g = [nc.dram_tensor("halo_g%d" % l, [128, 88], F32).ap() for l in range(2)]
    eb_s = nc.dram_tensor("eb_s", [8 * 8, 128, 512], F32).ap()
    ki_s = nc.dram_tensor("ki_s", [8 * 8, 128, 512], BF16).ap()
    kit_s = nc.dram_tensor("kit_s", [8 * 8, 128, 512], BF16).ap()
    iv_s = nc.dram_tensor("iv_s", [8, 128, 4096], BF16).ap()
    st_a = nc.dram_tensor("st_a", [128, 1024], F32).ap()
    st_g = nc.dram_tensor("st_g", [128, 1024], F32).ap()

    wb_ao = nc.dram_tensor("wb_ao", [D, D], BF16).ap()
    wb_hi = nc.dram_tensor("wb_hi", [D, 4 * D], BF16).ap()
    wb_ho = nc.dram_tensor("wb_ho", [D, D], BF16).ap()
    wb_up = nc.dram_tensor("wb_up", [2, D, 2 * FF], BF16).ap()
    wb_dn = nc.dram_tensor("wb_dn", [2, FF, D], BF16).ap()
    b_wb = Buf()
    b_qT, b_kT, b_vv, b_caug, b_oT = Buf(), Buf(), Buf(), Buf(), Buf()
    b_caugk = Buf()
    b_hTs = [Buf() for _ in range(NT)]
    b_xs = [Buf() for _ in range(NT)]
    b_out = Buf()

    with contextlib.ExitStack() as es:
        sch = Sched(nc, es)

        def TS(stack, name, shape, dt):
            return stack.enter_context(nc.sbuf_tensor(name, shape, dt))

        cols = TS(es, "cols_sb", [128, NCOL], F32)
        identb = TS(es, "identb", [128, 128], BF16)
        trib = TS(es, "trib", [128, 128], BF16)
        ones_b = TS(es, "ones_b", [128, 128], BF16)
        b_cols, b_cb = Buf(), Buf()
        sch.dma(cols[:, :], cols_in[:, :], w=[b_cols])
        with contextlib.ExitStack() as ph0:
            cst0 = TS(ph0, "cst0", [128, 256], F32)
            b_cst0 = Buf()
            sch.dma(cst0[:, :], consts_in[:, 0:256], w=[b_cst0])
            sch.op("dve", lambda e: e.tensor_copy(identb[:, :], cst0[:, K_ID:K_ID + 128]), r=[b_cst0], w=[b_cb])
            sch.op("dve", lambda e: e.tensor_copy(trib[:, :], cst0[:, K_TRI:K_TRI + 128]), r=[b_cst0], w=[b_cb])
            sch.op("dve", lambda e: e.memset(ones_b[:, :], 1.0), w=[b_cb])
            sch.barrier()

        pc_jobs = []
        for (dst_, src_, kch_, nco_) in ((wb_ao, w_att_out, 8, D), (wb_up[0], w_up[0], 8, 2 * FF), (wb_dn[0], w_dn[0], NFC, D),
                                         (wb_hi, w_h_in, 8, 4 * D), (wb_ho, w_h_out, 8, D),
                                         (wb_up[1], w_up[1], 8, 2 * FF), (wb_dn[1], w_dn[1], NFC, D)):
            sv = src_.rearrange("(c p) n -> p c n", p=128)
            dv = dst_.rearrange("(c p) n -> p c n", p=128)
            for k0 in range(0, kch_, 4):
                kk = min(4, kch_ - k0)
                for c0 in range(0, nco_, 512):
                    wd = min(512, nco_ - c0)
                    pc_jobs.append((sv[:, k0:k0 + kk, c0:c0 + wd], dv[:, k0:k0 + kk, c0:c0 + wd], kk, wd))

        def load_wb(dst, bdst, src2d, kch):
            v = src2d.rearrange("(c p) n -> p c n", p=128)
            for k0 in range(0, kch, 4):
                kk = min(4, kch - k0)
                sch.dma(dst[:, k0:k0 + kk, :], v[:, k0:k0 + kk, :], r=[b_wb], w=[bdst])

        def load_cst(stack, tag, c0, c1):
            t_ = TS(stack, "cst" + tag, [128, c1 - c0], F32)
            b_ = Buf()
            sch.dma(t_[:, :], consts_in[:, c0:c1], w=[b_])
            return t_, b_

        pbs = [es.enter_context(nc.psum_tensor("pb%d" % i, [128, 512], F32)) for i in range(8)]
        pbuf = [Buf(excl=True) for _ in range(8)]

        wcnt = [0]

        def load_w(dst, bdst, src2d, kch, ncols, stg, bstg):
            v = src2d.rearrange("(c p) n -> p c n", p=128)
            for k0 in range(0, kch, 4):
                kk = min(4, kch - k0)
                for c0 in range(0, ncols, 512):
                    wd = min(512, ncols - c0)
                    i = wcnt[0] % len(stg)
                    wcnt[0] += 1
                    sch.dma(stg[i][:, :kk, :wd], v[:, k0:k0 + kk, c0:c0 + wd], w=[bstg[i]])
                    eng = ("pool", "dve", "act")[(wcnt[0] - 1) % 3] if len(stg) > 2 else "pool"
                    if eng == "act":
                        sch.op("act", lambda e, i=i, k0=k0, kk=kk, c0=c0, wd=wd: e.copy(
                            out=dst[:, k0:k0 + kk, c0:c0 + wd], in_=stg[i][:, :kk, :wd]), r=[bstg[i]], w=[bdst])
                    else:
                        sch.op(eng, lambda e, i=i, k0=k0, kk=kk, c0=c0, wd=wd: e.tensor_copy(
                            dst[:, k0:k0 + kk, c0:c0 + wd], stg[i][:, :kk, :wd]), r=[bstg[i]], w=[bdst])

        def norm_tile(ns, xtile, bx, gc0, hT, bh, psr, nb=4):
            junk, bjunk, ssq, bssq, rstd, brstd, xn, bxn = ns
            TK = nb * 128
            for b in range(nb):
                sch.op("act", lambda e, b=b: e.activation(out=junk[:, :], in_=xtile[:, b, :], func=AF.Square,
                                                         accum_out=ssq[:, b:b + 1]), r=[bx], w=[bjunk, bssq])
            sch.op("act", lambda e: e.activation(out=rstd[:, 0:nb], in_=ssq[:, 0:nb], func=AF.Sqrt,
                                                 scale=1.0 / D, bias=EPS), r=[bssq], w=[brstd])
            sch.op("dve", lambda e: e.reciprocal(rstd[:, 0:nb], rstd[:, 0:nb]), r=[brstd], w=[brstd])
            for b in range(nb):
                sch.op("dve", lambda e, b=b: e.tensor_scalar(xn[:, b, :], xtile[:, b, :], rstd[:, b:b + 1], None,
                                                            ALU.mult), r=[bx, brstd], w=[bxn])
            for cp in range(4):
                pbk, bpb = psr.next()
                pv = pbk[:, :].bitcast(BF16)
                for b in range(nb):
                    for cc in range(2):
                        c = cp * 2 + cc
                        sch.op("pe", lambda e, b=b, cc=cc, c=c, pv=pv: e.transpose(
                            out=pv[:, cc * TK + b * 128:cc * TK + (b + 1) * 128],
                            in_=xn[:, b, c * 128:(c + 1) * 128], identity=identb[:, :]),
                            r=[bxn, b_cb], w=[bpb])
                for cc in range(2):
                    c = cp * 2 + cc
                    sch.op("dve", lambda e, cc=cc, c=c, pv=pv: e.tensor_scalar(
                        hT[:, c, :], pv[:, cc * TK:(cc + 1) * TK], cols[:, gc0 + c:gc0 + c + 1], None, ALU.mult),
                        r=[bpb, b_cols], w=[bh])

        def norm_scratch(stack, tag):
            junk = TS(stack, "junk" + tag, [128, 1024], BF16)
            ssq = TS(stack, "ssq" + tag, [128, 4], F32)
            rstd = TS(stack, "rstd" + tag, [128, 4], F32)
            xn = TS(stack, "xn" + tag, [128, 4, 1024], BF16)
            return (junk, Buf(), ssq, Buf(), rstd, Buf(), xn, Buf())

        ac_stack = contextlib.ExitStack()
        cnt_keep = TS(ac_stack, "CNT", [128, NB, H], F32)
        b_cnt = Buf()
        with contextlib.ExitStack() as phB, contextlib.ExitStack() as ph:
            NL = TS(phB, "NL", [16, S], F32)
            b_nl = Buf()
            Wqk = TS(ph, "Wqk", [128, 8, 2048], BF16)
            Wv = TS(ph, "Wv", [128, 8, 1024], BF16)
            Wf = TS(ph, "Wf", [128, 8, 16], BF16)
            bWqk, bWv, bWf = Buf(), Buf(), Buf()
            stg = [TS(ph, "stgA%d" % i, [128, 4, 512], F32) for i in range(2)]
            bstg = [Buf(), Buf()]
            negb = TS(ph, "negb", [16, 1], F32)
            b_negb = Buf()
            sch.op("dve", lambda e: e.tensor_scalar(negb[:, :], cols[0:16, C_BF:C_BF + 1], -1.0, None, ALU.mult),
                   r=[b_cols], w=[b_negb])
            load_w(Wqk, bWqk, w_att_in[:, 0:2048], 8, 2048, stg, bstg)
            load_w(Wv, bWv, w_att_in[:, 2048:3072], 8, 1024, stg, bstg)
            load_w(Wf, bWf, w_att_in[:, 3072:3088], 8, 16, stg, bstg)
            xts = Ring([(TS(ph, "xtA%d" % i, [128, 4, 1024], F32), Buf()) for i in range(2)])
            hTr = Ring([(TS(ph, "hTA%d" % i, [128, 8, 512], BF16), Buf()) for i in range(2)])
            ns = norm_scratch(ph, "A")
            qst = Ring([(TS(ph, "qst%d" % i, [128, 8, 512], BF16), Buf()) for i in range(1)])
            kst = Ring([(TS(ph, "kst%d" % i, [128, 8, 512], BF16), Buf()) for i in range(1)])
            vst = Ring([(TS(ph, "vst%d" % i, [128, 4, 1024], BF16), Buf()) for i in range(1)])
            tmpE = TS(ph, "tmpE", [16, 512], F32)
            b_tmpE = Buf()
            psT = Ring([(pbs[i], pbuf[i]) for i in (0, 1)])
            psM = Ring([(pbs[i], pbuf[i]) for i in (2, 3, 4, 5, 6, 7)])
            qTv = qT.rearrange("(g p) t -> p g t", p=128)
            kTv = kT.rearrange("(g p) t -> p g t", p=128)
            vvv = vv.rearrange("(n p) c -> p n c", p=128)
            xv = x_in.rearrange("(n p) d -> p n d", p=128)

            def load_x(t):
                xt, bx = xts.next()
                sch.dma(xt[:, :, :], xv[:, t * 4:(t + 1) * 4, :], w=[bx])
                return xt, bx

            nxt = load_x(0)
            evi = 0
            for t in range(NT):
                xt, bx = nxt
                if t + 1 < NT:
                    nxt = load_x(t + 1)
                hT, bh = hTr.next()
                norm_tile(ns, xt, bx, C_ATTG, hT, bh, psT)
                qs_, bqs = qst.next()
                ks_, bks = kst.next()
                for grp in range(16):
                    if grp < 8 and t < T0:
                        continue
                    pbk, bpb = psM.next()
                    for c in range(8):
                        sch.op("pe", lambda e, c=c, grp=grp, pbk=pbk, hT=hT: e.matmul(
                            pbk[:, :], lhsT=Wqk[:, c, grp * 128:(grp + 1) * 128], rhs=hT[:, c, :],
                            start=(c == 0), stop=(c == 7)), r=[bWqk, bh], w=[bpb])
                    dst, bd = (qs_, bqs) if grp < 8 else (ks_, bks)
                    sch.op("act", lambda e, dst=dst, grp=grp, pbk=pbk: e.copy(out=dst[:, grp % 8, :], in_=pbk[:, :]),
                           r=[bpb], w=[bd])
                if t >= T0:
                    sch.dma(qTv[:, :, t * 512:(t + 1) * 512], qs_[:, :, :], r=[bqs], w=[b_qT])
                sch.dma(kTv[:, :, t * 512:(t + 1) * 512], ks_[:, :, :], r=[bks], w=[b_kT])
                vs_, bvs = vst.next()
                for b in range(4):
                    for hf in range(2):
                        pbk, bpb = psM.next()
                        for c in range(8):
                            sch.op("pe", lambda e, c=c, b=b, hf=hf, pbk=pbk, hT=hT: e.matmul(
                                pbk[:, :], lhsT=hT[:, c, b * 128:(b + 1) * 128], rhs=Wv[:, c, hf * 512:(hf + 1) * 512],
                                start=(c == 0), stop=(c == 7)), r=[bWv, bh], w=[bpb])
                        eng = "dve" if (evi % 2 == 0) else "act"
                        evi += 1
                        if eng == "dve":
                            sch.op("dve", lambda e, b=b, hf=hf, pbk=pbk, vs_=vs_: e.tensor_copy(
                                vs_[:, b, hf * 512:(hf + 1) * 512], pbk[:, :]), r=[bpb], w=[bvs])
                        else:
                            sch.op("act", lambda e, b=b, hf=hf, pbk=pbk, vs_=vs_: e.copy(
                                out=vs_[:, b, hf * 512:(hf + 1) * 512], in_=pbk[:, :]), r=[bpb], w=[bvs])
                sch.dma(vvv[:, t * 4:(t + 1) * 4, :], vs_[:, :, :], r=[bvs], w=[b_vv])
                pbk, bpb = psM.next()
                for c in range(8):
                    sch.op("pe", lambda e, c=c, pbk=pbk, hT=hT: e.matmul(
                        pbk[0:16, :], lhsT=Wf[:, c, :], rhs=hT[:, c, :], start=(c == 0), stop=(c == 7)),
                        r=[bWf, bh], w=[bpb])
                sch.op("act", lambda e, pbk=pbk: e.activation(out=tmpE[:, :], in_=pbk[0:16, :], func=AF.Exp,
                                                            scale=-1.0, bias=negb[:, 0:1]),
                       r=[bpb, b_negb], w=[b_tmpE])
                sch.op("act", lambda e, t=t: e.activation(out=NL[:, t * 512:(t + 1) * 512], in_=tmpE[:, :],
                                                          func=AF.Ln, scale=1.0, bias=1.0), r=[b_tmpE], w=[b_nl])

            ph.close()
            ph = phB
            CN = NL
            b_cn = b_nl
            ones16 = TS(ph, "ones16", [16, 512], F32)
            b_o16 = Buf()
            qa = TS(ph, "qa", [16, S], BF16)
            b_qa = Buf()
            identf, b_cst = load_cst(ph, "B", K_ID, K_ID + 128)
            sch.op("dve", lambda e: e.memset(ones16[:, :], 1.0), w=[b_o16])
            for t in range(NT):
                init = 0.0 if t == 0 else CN[:, t * 512 - 1:t * 512]
                sch.op("dve", lambda e, t=t, init=init: e.tensor_tensor_scan(
                    CN[:, t * 512:(t + 1) * 512], ones16[:, :], NL[:, t * 512:(t + 1) * 512], init,
                    ALU.mult, ALU.add), r=[b_o16, b_cn], w=[b_cn])
            for t in range(4):
                sch.op("dve", lambda e, t=t: e.tensor_scalar(qa[:, t * 2048:(t + 1) * 2048], CN[:, t * 2048:(t + 1) * 2048],
                                                            -8.0, None, ALU.mult), r=[b_cn], w=[b_qa])
            sch.dma(caug[:, :], qa[:, :], r=[b_qa], w=[b_caug])
            if dbg:
                sch.dma(cn_d[:, :], CN[:, :], r=[b_cn])
            Wt = TS(ph, "Wt", [16, 2048], F32)
            Wr = [TS(ph, "Wr%d" % k, [16, 2048], BF16) for k in range(3)]
            b_wt, b_wr = Buf(), Buf()
            for t4 in range(4):
                sl = slice(t4 * 2048, (t4 + 1) * 2048)
                if t4 < 2:
                    sch.op("dve", lambda e, sl=sl: e.tensor_scalar(Wt[:, :], CN[:, sl], 8.0, cols[0:16, C_NEG8:C_NEG8 + 1],
                                                                  ALU.mult, ALU.add), r=[b_cn, b_cols], w=[b_wt])
                else:
                    sch.op("dve", lambda e, sl=sl: e.tensor_scalar(Wt[:, :], CN[:, sl], 8.0, None, ALU.mult),
                           r=[b_cn], w=[b_wt])
                for k in range(3):
                    sch.op("dve", lambda e, k=k: e.tensor_copy(Wr[k][:, :], Wt[:, :]), r=[b_wt], w=[b_wr])
                    if k < 2:
                        sch.op("dve", lambda e, k=k: e.tensor_tensor(Wt[:, :], Wt[:, :], Wr[k][:, :], ALU.subtract),
                               r=[b_wt, b_wr], w=[b_wt])
                    sch.dma(caugk[k, :, sl], Wr[k][:, :], r=[b_wr], w=[b_caugk])

        sch.barrier()
        if upto >= 2:
            with contextlib.ExitStack() as ph:
                KTr = Ring([(TS(ph, "KT%d" % i, [68, S], BF16), Buf()) for i in range(2)])
                QTr = Ring([(TS(ph, "QT%d" % i, [68, S], BF16), Buf()) for i in range(2)])
                VTr = Ring([(TS(ph, "VT%d" % i, [128, NB, 128], BF16), Buf()) for i in range(2)])
                OTr = Ring([(TS(ph, "OT%d" % i, [64, S], BF16), Buf()) for i in range(2)])
                PTr = Ring([(TS(ph, "PT%d" % i, [128, 512], BF16), Buf()) for i in range(6)])
                SrF = TS(ph, "SrF", [128, 512], F32)
                b_sr = Buf()
                Ssh = TS(ph, "Ssh", [64, 512], F32)
                b_ssh = Buf()
                shiftF = TS(ph, "shiftF", [128, 128], F32)
                b_shift = Buf()
                cstC = TS(ph, "cstC", [128, 128], F32)
                b_cstC = Buf()
                sch.dma(cstC[:, :], consts_in[:, K_ID:K_ID + 128], w=[b_cstC])
                sch.op("dve", lambda e: e.memset(shiftF[:, :], 0.0), w=[b_shift])
                sch.op("dve", lambda e: e.tensor_copy(shiftF[:, 0:64], cstC[:, 64:128]), r=[b_cstC], w=[b_shift])
                sch.op("dve", lambda e: e.memset(SrF[:, :], 0.0), w=[b_sr])
                for (kt, bk) in KTr.items:
                    sch.op("dve", lambda e, kt=kt: e.memset(kt[64:65, :], 1.0), w=[bk])
                for (qt_, bq_) in QTr.items:
                    sch.op("dve", lambda e, qt_=qt_: e.memset(qt_[64:68, SO:S], 1.0), w=[bq_])
                for (vt_, bv_) in VTr.items:
                    sch.op("pool", lambda e, vt_=vt_: e.memset(vt_[:, :, 64:128], 1.0), w=[bv_])
                psS = Ring([(pbs[i], pbuf[i]) for i in (0, 1, 2, 3, 4, 5)])
                psO = Ring([(pbs[i], pbuf[i]) for i in (6,)])
                psH, bpsH = pbs[7], pbuf[7]
                Ocp = TS(ph, "Ocp", [128, 512], F32)
                b_ocp = Buf()
                GK = 3
                pcs = Ring([(TS(ph, "pcs%d" % i, [128, 4, 512], F32), Buf(), TS(ph, "pcb%d" % i, [128, 4, 512], BF16), Buf())
                            for i in range(2)])
                per_head = (len(pc_jobs) + H - 1) // H
                vhv = vv.rearrange("(n p) (h d) -> p n h d", p=128, d=HD)

                def load_head(h):
                    kt, bk = KTr.next()
                    qt, bq = QTr.next()
                    vt, bv = VTr.next()
                    sch.dma(kt[0:64, :], kT[h * 64:(h + 1) * 64, :], r=[b_kT], w=[bk])
                    sch.dma(kt[65:68, :], caugk[:, h, :], r=[b_caugk], w=[bk])
                    sch.dma(qt[0:64, SO:S], qT[h * 64:(h + 1) * 64, SO:S], r=[b_qT], w=[bq])
                    sch.dma(qt[64:65, SO:S], caug[h:h + 1, SO:S], r=[b_caug], w=[bq])
                    for part in range(4):
                        sch.dma(vt[:, part * 16:(part + 1) * 16, 0:64], vhv[:, part * 16:(part + 1) * 16, h, :],
                                r=[b_vv], w=[bv])
                    return kt, bk, qt, bq, vt, bv

                nh = load_head(0)
                for h in range(H):
                    kt, bk, qt, bq, vt, bv = nh
                    if h + 1 < H:
                        nh = load_head(h + 1)
                    ot, bo = OTr.next()
                    for j in range(T0, NT):
                        nkb = 4 * j + 4
                        pso, bpo = psO.next()

                        def qk(i):
                            m = i - 4 * j
                            c0 = 128 * max(m, 0)
                            pss, bps = psS.next()
                            sch.op("pe", lambda e, i=i, c0=c0, pss=pss: e.matmul(
                                pss[:, c0:512], lhsT=kt[0:68, i * 128:(i + 1) * 128],
                                rhs=qt[0:68, j * 512 + c0:(j + 1) * 512], start=True, stop=True),
                                r=[bk, bq], w=[bps])
                            return pss, bps, c0, m

                        groups = [list(range(g0, min(g0 + GK, nkb))) for g0 in range(0, nkb, GK)]
                        cur = [qk(i) for i in groups[0]]
                        for gi, grp in enumerate(groups):
                            mine = cur
                            if gi + 1 < len(groups):
                                cur = [qk(i) for i in groups[gi + 1]]
                            pts = []
                            for i, (pss, bps, c0, m) in zip(grp, mine):
                                pt, bpt = PTr.next()
                                sch.op("act", lambda e, i=i, c0=c0, pss=pss, pt=pt: e.activation(
                                    out=pt[:, c0:512], in_=pss[:, c0:512], func=AF.Exp, scale=0.125),
                                    r=[bps], w=[bpt])
                                if m >= 0:
                                    sch.op("dve", lambda e, c0=c0, pt=pt: e.tensor_tensor(
                                        pt[:, c0:c0 + 128], pt[:, c0:c0 + 128], trib[:, :], ALU.mult),
                                        r=[bpt, b_cb], w=[bpt])
                                pts.append((i, c0, pt, bpt))
                            for (i, c0, pt, bpt) in pts:
                                sch.op("pe", lambda e, i=i, c0=c0, pt=pt: e.matmul(
                                    pso[:, c0:512], lhsT=vt[:, i, :], rhs=pt[:, c0:512],
                                    start=(i == 0), stop=(i == nkb - 1)), r=[bv, bpt], w=[bpo])
                        sch.op("act", lambda e, pso=pso: e.copy(out=Ocp[:, :], in_=pso[:, :]), r=[bpo], w=[b_ocp])
                        sch.op("dve", lambda e: e.reciprocal(SrF[64:128, :], Ocp[64:128, :]), r=[b_ocp], w=[b_sr])
                        sch.dma(Ssh[:, :], SrF[64:128, :], r=[b_sr], w=[b_ssh])
                        sch.op("dve", lambda e, j=j, ot=ot: e.tensor_tensor(
                            ot[:, j * 512:(j + 1) * 512], Ocp[0:64, :], Ssh[:, :], ALU.mult),
                            r=[b_ocp, b_ssh], w=[bo])
                    sch.dma(oT[h * 64:(h + 1) * 64, SO:S], ot[:, SO:S], r=[bo], w=[b_oT])
                    for _ in range(per_head):
                        if not pc_jobs:
                            break
                        sv_, dv_, kk_, wd_ = pc_jobs.pop(0)
                        st32, b32, st16, b16 = pcs.next()
                        sch.dma(st32[:, :kk_, :wd_], sv_, w=[b32])
                        sch.op("pool", lambda e, st32=st32, st16=st16, kk_=kk_, wd_=wd_: e.tensor_copy(
                            st16[:, :kk_, :wd_], st32[:, :kk_, :wd_]), r=[b32], w=[b16])
                        sch.dma(dv_, st16[:, :kk_, :wd_], r=[b16], w=[b_wb])

        sch.barrier()
        ac_stack.close()

        hTsv = hTs.rearrange("(c p) t -> p c t", p=128)
        xsv = xs.rearrange("(n p) d -> p n d", p=128)
        outv = out.rearrange("(n p) d -> p n d", p=128)

        def out_proj_phase(src_x_view, lhs_loader, w_src, gc0, tagp):
            with contextlib.ExitStack() as ph:
                Wo = TS(ph, "Wo" + tagp, [128, 8, 1024], BF16)
                bWo = Buf()
                load_wb(Wo, bWo, w_src, 8)
                lts = Ring([(TS(ph, "lt%s%d" % (tagp, i), [128, 8, 512], BF16), Buf()) for i in range(2)])
                xts = Ring([(TS(ph, "xtD%s%d" % (tagp, i), [128, 4, 1024], F32), Buf()) for i in range(2)])
                hto = Ring([(TS(ph, "hto%s%d" % (tagp, i), [128, 8, 512], BF16), Buf()) for i in range(2)])
                ns = norm_scratch(ph, "D" + tagp)
                psT = Ring([(pbs[i], pbuf[i]) for i in (0, 1)])
                psM = Ring([(pbs[i], pbuf[i]) for i in (2, 3, 4, 5)])

                def loads(t):
                    lt, bl = lts.next()
                    xt, bx = xts.next()
                    lhs_loader(t, lt, bl)
                    sch.dma(xt[:, :, :], src_x_view[:, t * 4:(t + 1) * 4, :], w=[bx])
                    return lt, bl, xt, bx

                nxt = loads(T0)
                for t in range(T0, NT):
                    lt, bl, xt, bx = nxt
                    if t + 1 < NT:
                        nxt = loads(t + 1)
                    for b in range(4):
                        for hf in range(2):
                            pbk, bpb = psM.next()
                            for c in range(8):
                                sch.op("pe", lambda e, c=c, b=b, hf=hf, pbk=pbk, lt=lt: e.matmul(
                                    pbk[:, :], lhsT=lt[:, c, b * 128:(b + 1) * 128],
                                    rhs=Wo[:, c, hf * 512:(hf + 1) * 512], start=(c == 0), stop=(c == 7)),
                                    r=[bWo, bl], w=[bpb])
                            sch.op("dve", lambda e, b=b, hf=hf, pbk=pbk, xt=xt: e.tensor_tensor(
                                xt[:, b, hf * 512:(hf + 1) * 512], pbk[:, :], xt[:, b, hf * 512:(hf + 1) * 512], ALU.add),
                                r=[bpb, bx], w=[bx])
                    sch.dma(xsv[:, t * 4:(t + 1) * 4, :], xt[:, :, :], r=[bx], w=[b_xs[t]])
                    ho, bho = hto.next()
                    norm_tile(ns, xt, bx, gc0, ho, bho, psT)
                    sch.dma(hTsv[:, :, t * 512:(t + 1) * 512], ho[:, :, :], r=[bho], w=[b_hTs[t]])
            sch.barrier()

        def ffn_phase(l, gc_next, final):
            TF = 512
            with contextlib.ExitStack() as ph:
                Wup = TS(ph, "Wup%d" % l, [128, 8, 2 * FF], BF16)
                Wdn = TS(ph, "Wdn%d" % l, [128, NFC, 1024], BF16)
                bWup, bWdn = Buf(), Buf()
                load_wb(Wup, bWup, wb_up[l], 8)
                load_wb(Wdn, bWdn, wb_dn[l], NFC)
                hin = Ring([(TS(ph, "hin%d%d" % (l, i), [128, 8, TF], BF16), Buf()) for i in range(2)])
                xt = TS(ph, "xtF%d" % l, [128, 4, 1024], F32)
                bx = Buf()
                R = TS(ph, "R%d" % l, [128, NFC * TF], BF16)
                bR = Buf()
                actv = R[:, :].rearrange("p (c t) -> p c t", t=TF)
                xn = R[:, 0:4096].rearrange("p (b d) -> p b d", d=1024)
                hto = R[:, 4096:8192].rearrange("p (c t) -> p c t", t=TF)
                junk = R[:, 8192:9216]
                ssq = TS(ph, "ssqF%d" % l, [128, 4], F32)
                rstd = TS(ph, "rstdF%d" % l, [128, 4], F32)
                ns = (junk, bR, ssq, Buf(), rstd, Buf(), xn, bR)
                Ur = Ring([(TS(ph, "U%d%d" % (l, i), [128, TF + 2], BF16), Buf()) for i in range(4)])
                yr = Ring([(TS(ph, "y%d%d" % (l, i), [128, TF], F32), Buf()) for i in range(4)])
                carry = TS(ph, "carry%d" % l, [128, 2 * NFC, 2], BF16)
                bcar = [Buf() for _ in range(2 * NFC)]
                if final:
                    gfin = TS(ph, "gfin", [128, 1024], F32)
                    b_gfin = Buf()
                    sch.dma(gfin[:, :], fgain_in.partition_broadcast(128), w=[b_gfin])
                psT = Ring([(pbs[i], pbuf[i]) for i in (0, 1)])
                psU = Ring([(pbs[i], pbuf[i]) for i in (2, 3, 4, 5)])
                psD = Ring([(pbs[i], pbuf[i]) for i in (6, 7)])
                NTF = S // TF
                cw0 = C_CONVW + l * 3 * 44
                cb0 = C_CONVB + l * 44
                hl = TS(ph, "hl%d" % l, [128, 8, 128], BF16)
                b_hl = Buf()
                halo = TS(ph, "halo%d" % l, [128, 2 * NFC, 2], F32)
                b_halo, b_ha, b_hg = Buf(), Buf(), Buf()
                sch.dma(hl[:, :, :], hTsv[:, :, S - 128:S], r=[b_hTs[NT - 1]], w=[b_hl])
                for ci in range(2 * NFC):
                    pbk, bpb = psU.next()
                    for c in range(8):
                        sch.op("pe", lambda e, c=c, ci=ci, pbk=pbk: e.matmul(
                            pbk[:, 0:128], lhsT=Wup[:, c, ci * 128:(ci + 1) * 128], rhs=hl[:, c, :],
                            start=(c == 0), stop=(c == 7)), r=[bWup, b_hl], w=[bpb])
                    sch.op("act", lambda e, ci=ci, pbk=pbk: e.copy(out=halo[:, ci, :], in_=pbk[:, 126:128]),
                           r=[bpb], w=[b_halo])
                sch.op("dve", lambda e: e.tensor_scalar(halo[:, :, :], halo[:, :, :], cols[:, C_M0:C_M0 + 1], None, ALU.mult),
                       r=[b_halo, b_cols], w=[b_halo])
                sch.dma(halo_a[l][:, :], halo[:, :, :].rearrange("p c k -> p (c k)"), r=[b_halo], w=[b_ha])
                sch.collective(es, halo_a[l][:, :], halo_g[l][:, :], r=[b_ha], w=[b_hg])
                sch.dma(halo[:, :, :].rearrange("p c k -> p (c k)"), halo_g[l][:, :], r=[b_hg], w=[b_halo])
                sch.op("dve", lambda e: e.tensor_scalar(carry[:, :, :], halo[:, :, :], cols[:, C_PM:C_PM + 1], None, ALU.mult),
                       r=[b_halo, b_cols], w=bcar)

                def load_h(t):
                    hi, bhi = hin.next()
                    sch.dma(hi[:, :, :], hTsv[:, :, t * TF:(t + 1) * TF], r=[b_hTs[t]], w=[bhi])
                    return hi, bhi

                def up(j, hi, bhi):
                    res = []
                    for ci in (j, NFC + j):
                        pbk, bpb = psU.next()
                        for c in range(8):
                            sch.op("pe", lambda e, c=c, ci=ci, pbk=pbk: e.matmul(
                                pbk[:, :], lhsT=Wup[:, c, ci * 128:(ci + 1) * 128], rhs=hi[:, c, :],
                                start=(c == 0), stop=(c == 7)), r=[bWup, bhi], w=[bpb])
                        res.append((ci, pbk, bpb))
                    return res

                def conv(ci, pbk, bpb):
                    U, bU = Ur.next()
                    y, by = yr.next()
                    wc = lambda tp: cols[:, cw0 + tp * 44 + ci:cw0 + tp * 44 + ci + 1]
                    sch.op("pool", lambda e: e.tensor_copy(U[:, 0:2], carry[:, ci, :]), r=[bcar[ci]], w=[bU])
                    sch.op("act", lambda e: e.copy(out=U[:, 2:TF + 2], in_=pbk[:, :]), r=[bpb], w=[bU])
                    sch.op("act", lambda e: e.activation(out=y[:, :], in_=pbk[:, :], func=AF.Identity, scale=wc(2),
                                                         bias=cols[:, cb0 + ci:cb0 + ci + 1]), r=[bpb, b_cols], w=[by])
                    sch.op("pool", lambda e: e.tensor_copy(carry[:, ci, :], U[:, TF:TF + 2]), r=[bU], w=[bcar[ci]])
                    sch.op("dve", lambda e: e.scalar_tensor_tensor(y[:, :], U[:, 1:TF + 1], wc(1), y[:, :],
                                                                   ALU.mult, ALU.add), r=[bU, by, b_cols], w=[by])
                    sch.op("dve", lambda e: e.scalar_tensor_tensor(y[:, :], U[:, 0:TF], wc(0), y[:, :],
                                                                   ALU.mult, ALU.add), r=[bU, by, b_cols], w=[by])
                    return y, by

                nxt = load_h(T0)
                sch.dma(xt[:, :, :], xsv[:, T0 * 4:(T0 + 1) * 4, :], r=[b_xs[T0]], w=[bx])
                for t in range(T0, NTF):
                    hi, bhi = nxt
                    if t + 1 < NTF:
                        nxt = load_h(t + 1)
                    cur = up(0, hi, bhi)
                    for j in range(NFC):
                        mine = cur
                        if j + 1 < NFC:
                            cur = up(j + 1, hi, bhi)
                        (cg, pg, bpg), (cv, pv_, bpv) = mine
                        yg, byg = conv(cg, pg, bpg)
                        yv, byv = conv(cv, pv_, bpv)
                        sch.op("act", lambda e, yg=yg: e.activation(out=yg[:, :], in_=yg[:, :], func=AF.Silu),
                               r=[byg], w=[byg])
                        sch.op("dve", lambda e, j=j, yg=yg, yv=yv: e.tensor_tensor(actv[:, j, :], yg[:, :], yv[:, :], ALU.mult),
                               r=[byg, byv], w=[bR])
                    for b in range(4):
                        for hf in range(2):
                            pbk, bpb = psD.next()
                            for j in range(NFC):
                                sch.op("pe", lambda e, j=j, b=b, hf=hf, pbk=pbk: e.matmul(
                                    pbk[:, :], lhsT=actv[:, j, b * 128:(b + 1) * 128], rhs=Wdn[:, j, hf * 512:(hf + 1) * 512],
                                    start=(j == 0), stop=(j == NFC - 1)), r=[bWdn, bR], w=[bpb])
                            sch.op("dve", lambda e, b=b, hf=hf, pbk=pbk: e.tensor_tensor(
                                xt[:, b, hf * 512:(hf + 1) * 512], pbk[:, :], xt[:, b, hf * 512:(hf + 1) * 512], ALU.add),
                                r=[bpb, bx], w=[bx])
                    if not final:
                        sch.dma(xsv[:, t * 4:(t + 1) * 4, :], xt[:, :, :], r=[bx], w=[b_xs[t]])
                        norm_tile(ns, xt, bx, gc_next, hto, bR, psT)
                        sch.dma(hTsv[:, :, t * TF:(t + 1) * TF], hto, r=[bR], w=[b_hTs[t]])
                    else:
                        for b in range(4):
                            sch.op("act", lambda e, b=b: e.activation(out=junk, in_=xt[:, b, :], func=AF.Square,
                                                                      accum_out=ssq[:, b:b + 1]), r=[bx], w=[bR, ns[3]])
                        sch.op("act", lambda e: e.activation(out=rstd[:, 0:4], in_=ssq[:, 0:4], func=AF.Sqrt,
                                                             scale=1.0 / D, bias=EPS), r=[ns[3]], w=[ns[5]])
                        sch.op("dve", lambda e: e.reciprocal(rstd[:, 0:4], rstd[:, 0:4]), r=[ns[5]], w=[ns[5]])
                        for b in range(4):
                            sch.op("dve", lambda e, b=b: e.scalar_tensor_tensor(
                                xt[:, b, :], xt[:, b, :], rstd[:, b:b + 1], gfin[:, :], ALU.mult, ALU.mult),
                                r=[bx, ns[5], b_gfin], w=[bx])
                        sch.dma(outv[:, (t - T0) * 4:(t - T0 + 1) * 4, :], xt[:, :, :], r=[bx], w=[b_out])
                    if t + 1 < NTF:
                        sch.dma(xt[:, :, :], xsv[:, (t + 1) * 4:(t + 2) * 4, :], r=[b_xs[t + 1]], w=[bx])
            sch.barrier()

        if upto >= 3:
            oTv = oT.rearrange("(c p) t -> p c t", p=128)
            xv0 = x_in.rearrange("(n p) d -> p n d", p=128)
            out_proj_phase(xv0, lambda t, lt, bl: sch.dma(lt[:, :, :], oTv[:, :, t * 512:(t + 1) * 512], r=[b_oT], w=[bl]),
                           wb_ao, C_FFNG0, "a")
        if upto >= 4:
            ffn_phase(0, C_HGRNG, final=(upto == 4))

        def hgrn_phase():
            with contextlib.ExitStack() as ph:
                Wh = TS(ph, "Wh", [128, 8, 4096], BF16)
                Woh = TS(ph, "Woh", [128, 8, 1024], BF16)
                bWh, bWoh = Buf(), Buf()
                load_wb(Wh, bWh, wb_hi, 8)
                load_wb(Woh, bWoh, wb_ho, 8)
                hin = Ring([(TS(ph, "hinH%d" % i, [128, 8, 512], BF16), Buf()) for i in range(2)])
                xtl = Ring([(TS(ph, "xtH%d" % i, [128, 4, 1024], F32), Buf()) for i in range(1)])
                IV = TS(ph, "IV", [128, 4, 1024], BF16)
                bIV = Buf()
                OGT = TS(ph, "OGT", [128, 8, 512], BF16)
                bOGT = Buf()
                junk = TS(ph, "junkH", [128, 1024], BF16)
                ssq = TS(ph, "ssqH", [128, 4], F32)
                rstd = TS(ph, "rstdH", [128, 4], F32)
                ns = (junk, Buf(), ssq, Buf(), rstd, Buf(), IV, bIV)
                NSET = 2
                A = [[(TS(ph, "A%d_%d" % (k, i), [128, 512], F32), Buf()) for k in range(5)] for i in range(NSET)]
                Bt = [[(TS(ph, "B%d_%d" % (k, i), [128, 512], BF16), Buf()) for k in range(6)] for i in range(NSET)]
                Ct = [[(TS(ph, "C%d_%d" % (k, i), [128, 512], F32), Buf()) for k in range(2)] for i in range(NSET)]
                Tst = TS(ph, "Tst", [128, 8, 128], F32)
                bT = [Buf() for _ in range(8)]
                Sbf = TS(ph, "Sbf", [128, 8, 2, 128], BF16)
                bS = [[Buf(), Buf()] for _ in range(8)]
                dprev = TS(ph, "dprev", [128, 8], F32)
                bdp = [Buf() for _ in range(8)]
                lbc = TS(ph, "lbc", [128, 8], F32)
                omlc = TS(ph, "omlc", [128, 8], F32)
                nomlc = TS(ph, "nomlc", [128, 8], F32)
                b_lb = Buf()
                bd4 = TS(ph, "bd4", [128, 512], BF16)
                b_bd4 = Buf()
                sch.op("dve", lambda e: e.tensor_tensor(lbc[:, :], cols[:, C_LB + 8:C_LB + 16], cols[:, C_LB:C_LB + 8],
                                                        ALU.subtract), r=[b_cols], w=[b_lb])
                sch.op("act", lambda e: e.activation(out=lbc[:, :], in_=lbc[:, :], func=AF.Sigmoid), r=[b_lb], w=[b_lb])
                sch.op("dve", lambda e: e.tensor_scalar(omlc[:, :], lbc[:, :], -1.0, 1.0, ALU.mult, ALU.add),
                       r=[b_lb], w=[b_lb])
                sch.op("dve", lambda e: e.tensor_scalar(omlc[:, :], omlc[:, :], 0.5, None, ALU.mult),
                       r=[b_lb], w=[b_lb])
                sch.op("dve", lambda e: e.tensor_tensor(lbc[:, :], lbc[:, :], omlc[:, :], ALU.add),
                       r=[b_lb], w=[b_lb])
                sch.op("dve", lambda e: e.tensor_scalar(nomlc[:, :], omlc[:, :], -1.0, None, ALU.mult),
                       r=[b_lb], w=[b_lb])
                cstH, b_cst = load_cst(ph, "H", K_BD, K_RM + 512)
                for k in range(4):
                    sch.op("dve", lambda e, k=k: e.tensor_copy(bd4[:, k * 128:(k + 1) * 128], cstH[:, 0:128]),
                           r=[b_cst], w=[b_bd4])
                rmask = cstH[:, K_RM - K_BD:K_RM - K_BD + 512]
                psP = Ring([(pbs[i], pbuf[i]) for i in (0, 1, 2, 3)])
                psA, bpsA = pbs[5], pbuf[5]
                psO, bpsO = pbs[6], pbuf[6]
                psXv = pbs[5][:, 0:256].bitcast(BF16)
                bpsX = pbuf[5]
                psUr = Ring([(pbs[4][:, 0:128], pbuf[4]), (pbs[7][:, 0:128], pbuf[7])])
                psT = Ring([(pbs[i], pbuf[i]) for i in (5, 6)])
                evi = [0]

                def loads(t):
                    hi, bhi = hin.next()
                    sch.dma(hi[:, :, :], hTsv[:, :, t * 512:(t + 1) * 512], r=[b_hTs[t]], w=[bhi])
                    return hi, bhi

                def proj(hi, bhi, hh, pre):
                    res = []
                    for base in ((1024,) if pre else (0, 3072)):
                        pbk, bpb = psP.next()
                        for c in range(8):
                            sch.op("pe", lambda e, c=c, base=base, pbk=pbk: e.matmul(
                                pbk[:, :], lhsT=Wh[:, c, base + hh * 128:base + (hh + 1) * 128], rhs=hi[:, c, :],
                                start=(c == 0), stop=(c == 7)), r=[bWh, bhi], w=[bpb])
                        res.append((pbk, bpb))
                    return res

                def elem_gen(pr, hh, st, pre, t):
                    (A1, b1), (A2, b2), (A3, b3), (A4, b4), (A5, b5) = A[st]
                    (QD, bqd), (KI, bki), (GS, bgs), (KIT, bkit) = Bt[st][0], Bt[st][1], Bt[st][2], Bt[st][3]
                    slot = (t - T0) * 8 + hh
                    if not pre:
                        (pq, bpq), (pg, bpg) = pr[0], pr[1]
                        sch.dma(A2[:, :], eb_s[slot], r=[b_ebs], w=[b2])
                        sch.dma(KI[:, :], ki_s[slot], r=[b_kis], w=[bki])
                        sch.dma(KIT[:, :], kit_s[slot], r=[b_kits], w=[bkit])
                        sch.op("act", lambda e: e.activation(out=A5[:, :], in_=pq[:, :], func=AF.Silu), r=[bpq], w=[b5])
                        sch.op("act", lambda e: e.activation(out=GS[:, :], in_=pg[:, :], func=AF.Silu), r=[bpg], w=[bgs])
                        yield
                        sch.op("dve", lambda e: e.tensor_tensor(QD[:, :], A5[:, :], A2[:, :], ALU.mult), r=[b5, b2], w=[bqd])
                        return
                    (pf, bpf) = pr[0]
                    sch.op("act", lambda e: e.activation(out=A1[:, :], in_=pf[:, :], func=AF.Tanh, scale=0.5), r=[bpf], w=[b1])
                    yield
                    sch.op("dve", lambda e: e.tensor_scalar(A2[:, :], A1[:, :], omlc[:, hh:hh + 1], lbc[:, hh:hh + 1],
                                                            ALU.mult, ALU.add), r=[b1, b_lb], w=[b2])
                    sch.op("dve", lambda e: e.tensor_scalar(A1[:, :], A1[:, :], nomlc[:, hh:hh + 1], omlc[:, hh:hh + 1],
                                                            ALU.mult, ALU.add), r=[b1, b_lb], w=[b1])
                    yield
                    sch.op("act", lambda e: e.activation(out=A2[:, :], in_=A2[:, :], func=AF.Ln), r=[b2], w=[b2])
                    yield
                    sch.op("dve", lambda e: e.tensor_tensor_scan(A3[:, :], rmask, A2[:, :], 0.0, ALU.mult, ALU.add),
                           r=[b2, b_cst], w=[b3])
                    yield
                    sch.op("act", lambda e: e.activation(out=A2[:, :], in_=A3[:, :], func=AF.Exp), r=[b3], w=[b2])
                    sch.op("act", lambda e: e.activation(out=A4[:, :], in_=A3[:, :], func=AF.Exp, scale=-1.0),
                           r=[b3], w=[b4])
                    yield
                    sch.op("dve", lambda e: e.tensor_tensor(KI[:, :], A1[:, :], A4[:, :], ALU.mult), r=[b1, b4], w=[bki])
                    sch.dma(eb_s[slot], A2[:, :], r=[b2], w=[b_ebs])
                    sch.dma(ki_s[slot], KI[:, :], r=[bki], w=[b_kis])

                def chunks(t, hh, st, pre, g=None):
                    def pull(k=1):
                        for _ in range(k):
                            if g is not None:
                                next(g, None)

                    (A2, b2) = A[st][1]
                    (QD, bqd), (KI, bki), (GS, bgs), (KIT, bkit), (AT, bat), (SQ, bsq) = Bt[st]
                    (RS, brs), (OG, bog) = Ct[st]
                    if pre:
                        for b in range(4):
                            sch.op("pe", lambda e, b=b: e.transpose(out=psXv[:, b * 128:(b + 1) * 128],
                                                                    in_=KI[:, b * 128:(b + 1) * 128], identity=identb[:, :]),
                                   r=[bki, b_cb], w=[bpsX])
                        sch.op("act", lambda e: e.copy(out=KIT[:, :], in_=psXv[:, :]), r=[bpsX], w=[bkit])
                        sch.dma(kit_s[(t - T0) * 8 + hh], KIT[:, :], r=[bkit], w=[b_kits])
                        pull()
                    if not pre:
                        for b in range(4):
                            sch.op("pe", lambda e, b=b: e.matmul(psA[:, b * 128:(b + 1) * 128], lhsT=KI[:, b * 128:(b + 1) * 128],
                                                                 rhs=QD[:, b * 128:(b + 1) * 128], start=True, stop=True),
                                   r=[bki, bqd], w=[bpsA])
                        sch.op("dve", lambda e: e.tensor_tensor(AT[:, :], psA[:, :], bd4[:, :], ALU.mult),
                               r=[bpsA, b_bd4], w=[bat])
                    for idx in range(8):
                        b, ee = idx // 2, idx % 2
                        n = 8 * (t - T0) + idx
                        loc = idx * 64
                        pu, bpu = psUr.next()
                        sch.op("pe", lambda e, b=b, ee=ee, pu=pu: e.matmul(
                            pu, lhsT=KIT[ee * 64:(ee + 1) * 64, b * 128:(b + 1) * 128],
                            rhs=IV[ee * 64:(ee + 1) * 64, b, hh * 128:(hh + 1) * 128], start=True, stop=True),
                            r=[bkit, bIV], w=[bpu])
                        if ee == 0 and not pre:
                            sch.op("pe", lambda e, b=b: e.matmul(psO[:, b * 128:(b + 1) * 128],
                                                                 lhsT=IV[:, b, hh * 128:(hh + 1) * 128],
                                                                 rhs=AT[:, b * 128:(b + 1) * 128],
                                                                 start=True, stop=False), r=[bIV, bat], w=[bpsO])
                        if not pre:
                            c0 = loc
                            sch.op("pe", lambda e, c0=c0, n=n, ee=ee: e.matmul(
                                psO[:, c0:c0 + 64], lhsT=Sbf[:, hh, (n - 1) % 2, :], rhs=QD[:, c0:c0 + 64],
                                start=False, stop=(ee == 1)), r=[bS[hh][(n - 1) % 2], bqd], w=[bpsO])
                        if n == 0 and pre:
                            sch.op("dve", lambda e, pu=pu: e.tensor_copy(Tst[:, hh, :], pu), r=[bpu], w=[bT[hh]])
                        elif n == 0:
                            sch.op("dve", lambda e, pu=pu: e.tensor_tensor(Tst[:, hh, :], pu, Sinit[:, hh, :], ALU.add),
                                   r=[bpu, b_sinit], w=[bT[hh]])
                        else:
                            dec = A2[:, loc - 1:loc] if loc > 0 else dprev[:, hh:hh + 1]
                            rr = [bpu, bT[hh], b2] if loc > 0 else [bpu, bT[hh], bdp[hh]]
                            sch.op("dve", lambda e, pu=pu, dec=dec: e.scalar_tensor_tensor(
                                Tst[:, hh, :], Tst[:, hh, :], dec, pu, ALU.mult, ALU.add), r=rr, w=[bT[hh]])
                        if not pre:
                            sch.op("dve", lambda e, loc=loc, n=n: e.tensor_scalar(
                                Sbf[:, hh, n % 2, :], Tst[:, hh, :], A2[:, loc + 63:loc + 64], None, ALU.mult),
                                r=[bT[hh], b2], w=[bS[hh][n % 2]])
                        if pre and idx % 2 == 1:
                            pull()
                        if (not pre) and idx in (3, 7):
                            pull()
                    if pre:
                        if t == NT - 1:
                            sch.op("dve", lambda e: e.tensor_scalar(Sinit[:, hh, :], Tst[:, hh, :], A2[:, 511:512], None,
                                                                    ALU.mult), r=[bT[hh], b2], w=[b_sinit])
                        sch.op("pool", lambda e: e.tensor_copy(dprev[:, hh:hh + 1], A2[:, 511:512]), r=[b2], w=[bdp[hh]])
                        return
                    sch.op("act", lambda e: e.activation(out=SQ[:, :], in_=psO[:, :], func=AF.Square), r=[bpsO], w=[bsq])
                    sch.op("pe", lambda e: e.matmul(psA[:, :], lhsT=ones_b[:, :], rhs=SQ[:, :], start=True, stop=True),
                           r=[bsq, b_cb], w=[bpsA])
                    sch.op("act", lambda e: e.activation(out=RS[:, :], in_=psA[:, :], func=AF.Ln, scale=1.0 / 128.0,
                                                         bias=EPS), r=[bpsA], w=[brs])
                    sch.op("act", lambda e: e.activation(out=RS[:, :], in_=RS[:, :], func=AF.Exp, scale=-0.5),
                           r=[brs], w=[brs])
                    sch.op("dve", lambda e: e.scalar_tensor_tensor(OG[:, :], psO[:, :], cols[:, C_ONORM + hh:C_ONORM + hh + 1],
                                                                   RS[:, :], ALU.mult, ALU.mult),
                           r=[bpsO, brs, b_cols], w=[bog])
                    sch.op("dve", lambda e: e.tensor_tensor(OGT[:, hh, :], OG[:, :], GS[:, :], ALU.mult),
                           r=[bog, bgs], w=[bOGT])
                    sch.op("pool", lambda e: e.tensor_copy(dprev[:, hh:hh + 1], A2[:, 511:512]), r=[b2], w=[bdp[hh]])

                b_ebs, b_kis, b_kits, b_ivs = Buf(), Buf(), Buf(), Buf()
                Sinit = TS(ph, "Sinit", [128, 8, 128], F32)
                b_sinit, b_sta, b_stg = Buf(), Buf(), Buf()

                def iv_proj(hi, bhi):
                    for b in range(4):
                        for hf in range(2):
                            pbk, bpb = psP.next()
                            for c in range(8):
                                sch.op("pe", lambda e, c=c, b=b, hf=hf, pbk=pbk: e.matmul(
                                    pbk[:, :], lhsT=hi[:, c, b * 128:(b + 1) * 128],
                                    rhs=Wh[:, c, 2048 + hf * 512:2048 + (hf + 1) * 512], start=(c == 0), stop=(c == 7)),
                                    r=[bWh, bhi], w=[bpb])
                            if evi[0] % 2 == 0:
                                sch.op("dve", lambda e, b=b, hf=hf, pbk=pbk: e.tensor_copy(
                                    IV[:, b, hf * 512:(hf + 1) * 512], pbk[:, :]), r=[bpb], w=[bIV])
                            else:
                                sch.op("act", lambda e, b=b, hf=hf, pbk=pbk: e.copy(
                                    out=IV[:, b, hf * 512:(hf + 1) * 512], in_=pbk[:, :]), r=[bpb], w=[bIV])
                            evi[0] += 1

                def heads(t, hi, bhi, pre):
                    pr = proj(hi, bhi, 0, pre)
                    for _ in elem_gen(pr, 0, 0, pre, t):
                        pass
                    for hh in range(8):
                        st = hh % NSET
                        g = None
                        if hh + 1 < 8:
                            pr = proj(hi, bhi, hh + 1, pre)
                            g = elem_gen(pr, hh + 1, (hh + 1) % NSET, pre, t)
                        chunks(t, hh, st, pre, g)
                        if g is not None:
                            for _ in g:
                                pass

                nxt = loads(T0)
                for t in range(T0, NT):
                    hi, bhi = nxt
                    if t + 1 < NT:
                        nxt = loads(t + 1)
                    iv_proj(hi, bhi)
                    sch.dma(iv_s[t - T0], IV[:, :, :].rearrange("p b c -> p (b c)"), r=[bIV], w=[b_ivs])
                    heads(t, hi, bhi, True)
                sch.op("dve", lambda e: e.tensor_scalar(Sinit[:, :, :], Sinit[:, :, :], cols[:, C_M0:C_M0 + 1], None, ALU.mult),
                       r=[b_sinit, b_cols], w=[b_sinit])
                sch.dma(st_a[:, :], Sinit[:, :, :].rearrange("p h v -> p (h v)"), r=[b_sinit], w=[b_sta])
                sch.collective(es, st_a[:, :], st_g[:, :], r=[b_sta], w=[b_stg])
                sch.dma(Sinit[:, :, :].rearrange("p h v -> p (h v)"), st_g[:, :], r=[b_stg], w=[b_sinit])
                sch.op("dve", lambda e: e.tensor_scalar(Sinit[:, :, :], Sinit[:, :, :], cols[:, C_PM:C_PM + 1], None, ALU.mult),
                       r=[b_sinit, b_cols], w=[b_sinit])
                sch.op("dve", lambda e: e.tensor_copy(Sbf[:, :, 1, :], Sinit[:, :, :]), r=[b_sinit], w=[bS[hh][1] for hh in range(8)])

                nxt = loads(T0)
                for t in range(T0, NT):
                    hi, bhi = nxt
                    if t + 1 < NT:
                        nxt = loads(t + 1)
                    xt, bx = xtl.next()
                    sch.dma(xt[:, :, :], xsv[:, t * 4:(t + 1) * 4, :], r=[b_xs[t]], w=[bx])
                    sch.dma(IV[:, :, :].rearrange("p b c -> p (b c)"), iv_s[t - T0], r=[b_ivs], w=[bIV])
                    heads(t, hi, bhi, False)
                    for b in range(4):
                        for hf in range(2):
                            pbk, bpb = psP.next()
                            for hh in range(8):
                                sch.op("pe", lambda e, hh=hh, b=b, hf=hf, pbk=pbk: e.matmul(
                                    pbk[:, :], lhsT=OGT[:, hh, b * 128:(b + 1) * 128],
                                    rhs=Woh[:, hh, hf * 512:(hf + 1) * 512], start=(hh == 0), stop=(hh == 7)),
                                    r=[bWoh, bOGT], w=[bpb])
                            sch.op("dve", lambda e, b=b, hf=hf, pbk=pbk, xt=xt: e.tensor_tensor(
                                xt[:, b, hf * 512:(hf + 1) * 512], pbk[:, :], xt[:, b, hf * 512:(hf + 1) * 512], ALU.add),
                                r=[bpb, bx], w=[bx])
                    sch.dma(xsv[:, t * 4:(t + 1) * 4, :], xt[:, :, :], r=[bx], w=[b_xs[t]])
                    norm_tile(ns, xt, bx, C_FFNG1, OGT, bOGT, psT)
                    sch.dma(hTsv[:, :, t * 512:(t + 1) * 512], OGT[:, :, :], r=[bOGT], w=[b_hTs[t]])
            sch.barrier()

        if upto >= 5:
            hgrn_phase()
        if upto >= 6:
            ffn_phase(1, 0, final=True)

        sch.finish()
        print("sched: ops", sch.nops, "waits", sch.nwait, "cnt", sch.cnt, flush=True)
    return nc


def host_tables(inp):
    f = np.float32

    def colize(v):
        v = np.asarray(v, f)
        return np.ascontiguousarray(v.reshape(-1, 128).T)

    cols = np.zeros((128, NCOL), f)
    cols[:, C_ATTG:C_ATTG + 8] = colize(inp["att_norm_g"][0])
    cols[:, C_FFNG0:C_FFNG0 + 8] = colize(inp["ffn_norm_g"][0])
    cols[:, C_HGRNG:C_HGRNG + 8] = colize(inp["hgrn_norm_g"][0])
    cols[:, C_FFNG1:C_FFNG1 + 8] = colize(inp["ffn_norm_g"][1])
    cw = np.asarray(inp["ffn_conv_w"], f)
    cb = np.asarray(inp["ffn_conv_b"], f)
    for l in range(2):
        for tp in range(3):
            o = C_CONVW + (l * 3 + tp) * 44
            cols[:, o:o + 44] = colize(cw[l, tp])
        o = C_CONVB + l * 44
        cols[:, o:o + 44] = colize(cb[l])
    lbl = np.asarray(inp["hgrn_lb_logits"], f)
    for l in range(2):
        cols[:, C_LB + l * 8:C_LB + (l + 1) * 8] = colize(lbl[l])
    cols[:, C_ONORM:C_ONORM + 8] = colize(inp["hgrn_onorm_g"][0])
    cols[0:16, C_BF] = np.asarray(inp["att_b_f"], f)[0]
    cst = np.zeros((128, NCONST), f)
    cst[:, K_ID:K_ID + 128] = np.eye(128, dtype=f)
    ii = np.arange(128)
    cst[:, K_TRI:K_TRI + 128] = (ii[:, None] <= ii[None, :]).astype(f)
    bd = (ii[:, None] <= ii[None, :]) & ((ii[:, None] // 64) == (ii[None, :] // 64))
    cst[:, K_BD:K_BD + 128] = bd.astype(f)
    rm = np.ones((128, 512), f)
    rm[:, 0::64] = 0.0
    cst[:, K_RM:K_RM + 512] = rm
    return cols, cst


_NC_CACHE = {}


def make_in_maps(inp, n_cores=8):
    cols, cst = host_tables(inp)
    x = np.asarray(inp["x"], np.float32)
    shared = {
        "att_w_in": np.ascontiguousarray(np.asarray(inp["att_w_in"], np.float32)[0]),
        "att_w_out": np.ascontiguousarray(np.asarray(inp["att_w_out"], np.float32)[0]),
        "hgrn_w_in": np.ascontiguousarray(np.asarray(inp["hgrn_w_in"], np.float32)[0]),
        "hgrn_w_out": np.ascontiguousarray(np.asarray(inp["hgrn_w_out"], np.float32)[0]),
        "ffn_w_up": np.ascontiguousarray(np.asarray(inp["ffn_w_up"], np.float32)),
        "ffn_w_down": np.ascontiguousarray(np.asarray(inp["ffn_w_down"], np.float32)),
        "cols": cols, "consts": cst,
        "final_norm_g": np.ascontiguousarray(np.asarray(inp["final_norm_g"], np.float32).reshape(1, D)),
    }
    maps = []
    for c in range(n_cores):
        b, p = c // 2, c % 2
        m = dict(shared)
        m["x"] = np.ascontiguousarray(np.concatenate([x[b, 0:SO], x[b, p * SO:(p + 1) * SO]], axis=0))
        cc = cols.copy()
        cc[:, C_PM] = float(p)
        cc[:, C_M0] = float(1 - p)
        cc[:, C_NEG] = float((p - 1) * 30000)
        cc[:, C_NEG8] = float((p - 1) * 240000)
        m["cols"] = cc
        maps.append(m)
    return maps


def kernel(**inputs):
    if "nc" not in _NC_CACHE:
        _NC_CACHE["nc"] = build()
    nc = _NC_CACHE["nc"]
    maps = make_in_maps(inputs, 8)
    res = run_bass_kernel_spmd(nc, maps, core_ids=list(range(8)))
    full = np.empty((4, S, D), np.float32)
    for c in range(8):
        b, p = c // 2, c % 2
        full[b, p * SO:(p + 1) * SO] = np.asarray(res.results[c]["out"], np.float32)
    return full
```

```python
import contextlib
import numpy as np
import concourse.bass as bass
import concourse.mybir as mybir
from concourse.bass_utils import run_bass_kernel_spmd

F32 = mybir.dt.float32
BF16 = mybir.dt.bfloat16
AF = mybir.ActivationFunctionType
ALU = mybir.AluOpType

D = 1024
S = 8192
H = 16
HD = 64
FF = 2816
NFC = 22
EPS = 1e-6
NT = S // 512
NB = S // 128

C_ATTG, C_FFNG0, C_HGRNG, C_FFNG1 = 0, 8, 16, 24
C_CONVW = 32
C_CONVB = C_CONVW + 264
C_LB = C_CONVB + 88
C_ONORM = C_LB + 16
C_BF = C_ONORM + 8
C_PM = C_BF + 1
C_M0 = C_PM + 1
C_NEG = C_M0 + 1
C_NEG8 = C_NEG + 1
NCOL = C_NEG8 + 1
T0 = 8
SO = S // 2
PAIRS = [[0, 1], [2, 3], [4, 5], [6, 7]]
K_ID, K_TRI, K_BD, K_RM = 0, 128, 256, 384
NCONST = 896


class Buf:
    __slots__ = ("w", "r", "excl")

    def __init__(self, excl=False):
        self.w = None
        self.r = {}
        self.excl = excl


class Ring:
    def __init__(self, items):
        self.items = items
        self.i = 0

    def next(self):
        it = self.items[self.i % len(self.items)]
        self.i += 1
        return it


class Sched:
    def __init__(self, nc, es, nds=40):
        self.nc = nc
        self.E = {"pe": nc.tensor, "act": nc.scalar, "dve": nc.vector, "pool": nc.gpsimd, "sp": nc.sync}
        self.cnt = {e: 0 for e in ("pe", "act", "dve", "pool")}
        self.sem = {e: es.enter_context(nc.semaphore("sm_" + e)) for e in self.cnt}
        self.dsem = [es.enter_context(nc.semaphore("sd%d" % i)) for i in range(nds)]
        self.dval = [0] * nds
        self.dn = 0
        self.known = {e: {} for e in self.E}
        self.nwait = 0
        self.nops = 0
        self.ncc = 0
        self.ccev = []

    def _wait(self, eng, ev):
        sem, val = ev
        k = self.known[eng]
        key = id(sem)
        if k.get(key, 0) >= val:
            return
        k[key] = val
        self.E[eng].wait_ge(sem, val)
        self.nwait += 1

    def _deps(self, eng, r, w, own):
        skip_own = (eng == "pe")
        for b in r:
            ev = b.w
            if ev is not None and not (ev[0] is own and skip_own):
                self._wait(eng, ev)
        for b in w:
            ev = b.w
            if ev is not None and not (ev[0] is own and skip_own):
                self._wait(eng, ev)
            for ev in b.r.values():
                if not (ev[0] is own and skip_own):
                    self._wait(eng, ev)

    @staticmethod
    def _mark(ev, r, w):
        for b in r:
            b.r[id(ev[0])] = ev
        for b in w:
            b.w = ev
            b.r = {}

    def op(self, eng, fn, r=(), w=()):
        if any(b.excl for b in r):
            w = list(w) + [b for b in r if b.excl]
            r = [b for b in r if not b.excl]
        own = self.sem[eng]
        self._deps(eng, r, w, own)
        self.cnt[eng] += 1
        ev = (own, self.cnt[eng])
        fn(self.E[eng]).then_inc(own, 1)
        self._mark(ev, r, w)
        self.nops += 1
        return ev

    def dma(self, out, in_, r=(), w=(), q="sp"):
        i = self.dn % len(self.dsem)
        self.dn += 1
        sem = self.dsem[i]
        if self.dval[i] > 0:
            self._wait(q, (sem, self.dval[i]))
        self._deps(q, r, w, None)
        self.dval[i] += 16
        ev = (sem, self.dval[i])
        self.E[q].dma_start(out=out, in_=in_).then_inc(sem, 16)
        self._mark(ev, r, w)
        self.nops += 1
        return ev

    def collective(self, es, in_ap, out_ap, r=(), w=()):
        sem = es.enter_context(self.nc.semaphore("cc%d" % self.ncc))
        self.ncc += 1
        self._deps("pool", r, w, None)
        self.nc.gpsimd.collective_compute("AllReduce", ALU.add, replica_groups=PAIRS,
                                          ins=[in_ap], outs=[out_ap]).then_inc(sem)
        ev = (sem, 1)
        self._mark(ev, r, w)
        self.ccev.append(ev)
        return ev

    def barrier(self):
        evs = [(self.sem[e], self.cnt[e]) for e in self.cnt if self.cnt[e] > 0]
        evs += [(sem, self.dval[i]) for i, sem in enumerate(self.dsem) if self.dval[i] > 0]
        evs += self.ccev
        for eng in self.E:
            own = self.sem.get(eng)
            for ev in evs:
                if ev[0] is not own:
                    self._wait(eng, ev)

    def finish(self):
        for i, sem in enumerate(self.dsem):
            if self.dval[i] > 0:
                self._wait("sp", (sem, self.dval[i]))


def build(dbg=False, upto=99):
    nc = bass.Bass("TRN2", target_bir_lowering=False)

    def I(n, shp):
        return nc.dram_tensor(n, shp, F32, kind="ExternalInput").ap()

    x_in = I("x", [S, D])
    w_att_in = I("att_w_in", [D, 3 * D + H])
    w_att_out = I("att_w_out", [D, D])
    w_h_in = I("hgrn_w_in", [D, 4 * D])
    w_h_out = I("hgrn_w_out", [D, D])
    w_up = I("ffn_w_up", [2, D, 2 * FF])
    w_dn = I("ffn_w_down", [2, FF, D])
    cols_in = I("cols", [128, NCOL])
    consts_in = I("consts", [128, NCONST])
    fgain_in = I("final_norm_g", [1, D])
    out = nc.dram_tensor("out", [SO, D], F32, kind="ExternalOutput").ap()
    sk = "ExternalOutput" if dbg else "Internal"

    def Sc(n, shp, dt):
        return nc.dram_tensor(n, shp, dt, kind=sk).ap()

    qT = Sc("qT", [D, S], BF16)
    kT = Sc("kT", [D, S], BF16)
    vv = Sc("vv", [S, D], BF16)
    caug = Sc("caug", [H, S], BF16)
    caugk = Sc("caugk", [3, H, S], BF16)
    oT = Sc("oT", [D, S], BF16)
    hTs = Sc("hTs", [D, S], BF16)
    xs = Sc("xs", [S, D], F32)
    cn_d = Sc("cn_d", [H, S], F32) if dbg else None
    halo_a = [nc.dram_tensor("halo_a%d" % l, [128, 88], F32).ap() for l in range(2)]
    halo_g = [nc.dram_tensor("halo_g%d" % l, [128, 88], F32).ap() for l in range(2)]
    eb_s = nc.dram_tensor("eb_s", [8 * 8, 128, 512], F32).ap()
    ki_s = nc.dram_tensor("ki_s", [8 * 8, 128, 512], BF16).ap()
    kit_s = nc.dram_tensor("kit_s", [8 * 8, 128, 512], BF16).ap()
    iv_s = nc.dram_tensor("iv_s", [8, 128, 4096], BF16).ap()
    st_a = nc.dram_tensor("st_a", [128, 1024], F32).ap()
    st_g = nc.dram_tensor("st_g", [128, 1024], F32).ap()

    b_qT, b_kT, b_vv, b_caug, b_oT = Buf(), Buf(), Buf(), Buf(), Buf()
    b_caugk = Buf()
    b_hTs = [Buf() for _ in range(NT)]
    b_xs = [Buf() for _ in range(NT)]
    b_out = Buf()

    with contextlib.ExitStack() as es:
        sch = Sched(nc, es)

        def TS(stack, name, shape, dt):
            return stack.enter_context(nc.sbuf_tensor(name, shape, dt))

        cols = TS(es, "cols_sb", [128, NCOL], F32)
        identb = TS(es, "identb", [128, 128], BF16)
        trib = TS(es, "trib", [128, 128], BF16)
        ones_b = TS(es, "ones_b", [128, 128], BF16)
        b_cols, b_cb = Buf(), Buf()
        sch.dma(cols[:, :], cols_in[:, :], w=[b_cols])
        with contextlib.ExitStack() as ph0:
            cst0 = TS(ph0, "cst0", [128, 256], F32)
            b_cst0 = Buf()
            sch.dma(cst0[:, :], consts_in[:, 0:256], w=[b_cst0])
            sch.op("dve", lambda e: e.tensor_copy(identb[:, :], cst0[:, K_ID:K_ID + 128]), r=[b_cst0], w=[b_cb])
            sch.op("dve", lambda e: e.tensor_copy(trib[:, :], cst0[:, K_TRI:K_TRI + 128]), r=[b_cst0], w=[b_cb])
            sch.op("dve", lambda e: e.memset(ones_b[:, :], 1.0), w=[b_cb])
            sch.barrier()

        def load_cst(stack, tag, c0, c1):
            t_ = TS(stack, "cst" + tag, [128, c1 - c0], F32)
            b_ = Buf()
            sch.dma(t_[:, :], consts_in[:, c0:c1], w=[b_])
            return t_, b_

        pbs = [es.enter_context(nc.psum_tensor("pb%d" % i, [128, 512], F32)) for i in range(8)]
        pbuf = [Buf(excl=True) for _ in range(8)]

        wcnt = [0]

        def load_w(dst, bdst, src2d, kch, ncols, stg, bstg):
            v = src2d.rearrange("(c p) n -> p c n", p=128)
            for k0 in range(0, kch, 4):
                kk = min(4, kch - k0)
                for c0 in range(0, ncols, 512):
                    wd = min(512, ncols - c0)
                    i = wcnt[0] % len(stg)
                    wcnt[0] += 1
                    sch.dma(stg[i][:, :kk, :wd], v[:, k0:k0 + kk, c0:c0 + wd], w=[bstg[i]])
                    eng = ("pool", "dve", "act")[(wcnt[0] - 1) % 3] if len(stg) > 2 else "pool"
                    if eng == "act":
                        sch.op("act", lambda e, i=i, k0=k0, kk=kk, c0=c0, wd=wd: e.copy(
                            out=dst[:, k0:k0 + kk, c0:c0 + wd], in_=stg[i][:, :kk, :wd]), r=[bstg[i]], w=[bdst])
                    else:
                        sch.op(eng, lambda e, i=i, k0=k0, kk=kk, c0=c0, wd=wd: e.tensor_copy(
                            dst[:, k0:k0 + kk, c0:c0 + wd], stg[i][:, :kk, :wd]), r=[bstg[i]], w=[bdst])

        def norm_tile(ns, xtile, bx, gc0, hT, bh, psr, nb=4):
            junk, bjunk, ssq, bssq, rstd, brstd, xn, bxn = ns
            TK = nb * 128
            for b in range(nb):
                sch.op("act", lambda e, b=b: e.activation(out=junk[:, :], in_=xtile[:, b, :], func=AF.Square,
                                                         accum_out=ssq[:, b:b + 1]), r=[bx], w=[bjunk, bssq])
            sch.op("act", lambda e: e.activation(out=rstd[:, 0:nb], in_=ssq[:, 0:nb], func=AF.Sqrt,
                                                 scale=1.0 / D, bias=EPS), r=[bssq], w=[brstd])
            sch.op("dve", lambda e: e.reciprocal(rstd[:, 0:nb], rstd[:, 0:nb]), r=[brstd], w=[brstd])
            for b in range(nb):
                sch.op("dve", lambda e, b=b: e.tensor_scalar(xn[:, b, :], xtile[:, b, :], rstd[:, b:b + 1], None,
                                                            ALU.mult), r=[bx, brstd], w=[bxn])
            for cp in range(4):
                pbk, bpb = psr.next()
                pv = pbk[:, :].bitcast(BF16)
                for b in range(nb):
                    for cc in range(2):
                        c = cp * 2 + cc
                        sch.op("pe", lambda e, b=b, cc=cc, c=c, pv=pv: e.transpose(
                            out=pv[:, cc * TK + b * 128:cc * TK + (b + 1) * 128],
                            in_=xn[:, b, c * 128:(c + 1) * 128], identity=identb[:, :]),
                            r=[bxn, b_cb], w=[bpb])
                for cc in range(2):
                    c = cp * 2 + cc
                    sch.op("dve", lambda e, cc=cc, c=c, pv=pv: e.tensor_scalar(
                        hT[:, c, :], pv[:, cc * TK:(cc + 1) * TK], cols[:, gc0 + c:gc0 + c + 1], None, ALU.mult),
                        r=[bpb, b_cols], w=[bh])

        def norm_scratch(stack, tag):
            junk = TS(stack, "junk" + tag, [128, 1024], BF16)
            ssq = TS(stack, "ssq" + tag, [128, 4], F32)
            rstd = TS(stack, "rstd" + tag, [128, 4], F32)
            xn = TS(stack, "xn" + tag, [128, 4, 1024], BF16)
            return (junk, Buf(), ssq, Buf(), rstd, Buf(), xn, Buf())

        ac_stack = contextlib.ExitStack()
        cnt_keep = TS(ac_stack, "CNT", [128, NB, H], F32)
        b_cnt = Buf()
        with contextlib.ExitStack() as phB, contextlib.ExitStack() as ph:
            NL = TS(phB, "NL", [16, S], F32)
            b_nl = Buf()
            Wqk = TS(ph, "Wqk", [128, 8, 2048], BF16)
            Wv = TS(ph, "Wv", [128, 8, 1024], BF16)
            Wf = TS(ph, "Wf", [128, 8, 16], BF16)
            bWqk, bWv, bWf = Buf(), Buf(), Buf()
            stg = [TS(ph, "stgA%d" % i, [128, 4, 512], F32) for i in range(2)]
            bstg = [Buf(), Buf()]
            negb = TS(ph, "negb", [16, 1], F32)
            b_negb = Buf()
            sch.op("dve", lambda e: e.tensor_scalar(negb[:, :], cols[0:16, C_BF:C_BF + 1], -1.0, None, ALU.mult),
                   r=[b_cols], w=[b_negb])
            load_w(Wqk, bWqk, w_att_in[:, 0:2048], 8, 2048, stg, bstg)
            load_w(Wv, bWv, w_att_in[:, 2048:3072], 8, 1024, stg, bstg)
            load_w(Wf, bWf, w_att_in[:, 3072:3088], 8, 16, stg, bstg)
            xts = Ring([(TS(ph, "xtA%d" % i, [128, 4, 1024], F32), Buf()) for i in range(2)])
            hTr = Ring([(TS(ph, "hTA%d" % i, [128, 8, 512], BF16), Buf()) for i in range(2)])
            ns = norm_scratch(ph, "A")
            qst = Ring([(TS(ph, "qst%d" % i, [128, 8, 512], BF16), Buf()) for i in range(1)])
            kst = Ring([(TS(ph, "kst%d" % i, [128, 8, 512], BF16), Buf()) for i in range(1)])
            vst = Ring([(TS(ph, "vst%d" % i, [128, 4, 1024], BF16), Buf()) for i in range(1)])
            tmpE = TS(ph, "tmpE", [16, 512], F32)
            b_tmpE = Buf()
            psT = Ring([(pbs[i], pbuf[i]) for i in (0, 1)])
            psM = Ring([(pbs[i], pbuf[i]) for i in (2, 3, 4, 5, 6, 7)])
            qTv = qT.rearrange("(g p) t -> p g t", p=128)
            kTv = kT.rearrange("(g p) t -> p g t", p=128)
            vvv = vv.rearrange("(n p) c -> p n c", p=128)
            xv = x_in.rearrange("(n p) d -> p n d", p=128)

            def load_x(t):
                xt, bx = xts.next()
                sch.dma(xt[:, :, :], xv[:, t * 4:(t + 1) * 4, :], w=[bx])
                return xt, bx

            nxt = load_x(0)
            evi = 0
            for t in range(NT):
                xt, bx = nxt
                if t + 1 < NT:
                    nxt = load_x(t + 1)
                hT, bh = hTr.next()
                norm_tile(ns, xt, bx, C_ATTG, hT, bh, psT)
                qs_, bqs = qst.next()
                ks_, bks = kst.next()
                for grp in range(16):
                    if grp < 8 and t < T0:
                        continue
                    pbk, bpb = psM.next()
                    for c in range(8):
                        sch.op("pe", lambda e, c=c, grp=grp, pbk=pbk, hT=hT: e.matmul(
                            pbk[:, :], lhsT=Wqk[:, c, grp * 128:(grp + 1) * 128], rhs=hT[:, c, :],
                            start=(c == 0), stop=(c == 7)), r=[bWqk, bh], w=[bpb])
                    dst, bd = (qs_, bqs) if grp < 8 else (ks_, bks)
                    sch.op("act", lambda e, dst=dst, grp=grp, pbk=pbk: e.copy(out=dst[:, grp % 8, :], in_=pbk[:, :]),
                           r=[bpb], w=[bd])
                if t >= T0:
                    sch.dma(qTv[:, :, t * 512:(t + 1) * 512], qs_[:, :, :], r=[bqs], w=[b_qT])
                sch.dma(kTv[:, :, t * 512:(t + 1) * 512], ks_[:, :, :], r=[bks], w=[b_kT])
                vs_, bvs = vst.next()
                for b in range(4):
                    for hf in range(2):
                        pbk, bpb = psM.next()
                        for c in range(8):
                            sch.op("pe", lambda e, c=c, b=b, hf=hf, pbk=pbk, hT=hT: e.matmul(
                                pbk[:, :], lhsT=hT[:, c, b * 128:(b + 1) * 128], rhs=Wv[:, c, hf * 512:(hf + 1) * 512],
                                start=(c == 0), stop=(c == 7)), r=[bWv, bh], w=[bpb])
                        eng = "dve" if (evi % 2 == 0) else "act"
                        evi += 1
                        if eng == "dve":
                            sch.op("dve", lambda e, b=b, hf=hf, pbk=pbk, vs_=vs_: e.tensor_copy(
                                vs_[:, b, hf * 512:(hf + 1) * 512], pbk[:, :]), r=[bpb], w=[bvs])
                        else:
                            sch.op("act", lambda e, b=b, hf=hf, pbk=pbk, vs_=vs_: e.copy(
                                out=vs_[:, b, hf * 512:(hf + 1) * 512], in_=pbk[:, :]), r=[bpb], w=[bvs])
                sch.dma(vvv[:, t * 4:(t + 1) * 4, :], vs_[:, :, :], r=[bvs], w=[b_vv])
                pbk, bpb = psM.next()
                for c in range(8):
                    sch.op("pe", lambda e, c=c, pbk=pbk, hT=hT: e.matmul(
                        pbk[0:16, :], lhsT=Wf[:, c, :], rhs=hT[:, c, :], start=(c == 0), stop=(c == 7)),
                        r=[bWf, bh], w=[bpb])
                sch.op("act", lambda e, pbk=pbk: e.activation(out=tmpE[:, :], in_=pbk[0:16, :], func=AF.Exp,
                                                            scale=-1.0, bias=negb[:, 0:1]),
                       r=[bpb, b_negb], w=[b_tmpE])
                sch.op("act", lambda e, t=t: e.activation(out=NL[:, t * 512:(t + 1) * 512], in_=tmpE[:, :],
                                                          func=AF.Ln, scale=1.0, bias=1.0), r=[b_tmpE], w=[b_nl])

            ph.close()
            ph = phB
            CN = NL
            b_cn = b_nl
            ones16 = TS(ph, "ones16", [16, 512], F32)
            b_o16 = Buf()
            qa = TS(ph, "qa", [16, S], BF16)
            b_qa = Buf()
            identf, b_cst = load_cst(ph, "B", K_ID, K_ID + 128)
            sch.op("dve", lambda e: e.memset(ones16[:, :], 1.0), w=[b_o16])
            for t in range(NT):
                init = 0.0 if t == 0 else CN[:, t * 512 - 1:t * 512]
                sch.op("dve", lambda e, t=t, init=init: e.tensor_tensor_scan(
                    CN[:, t * 512:(t + 1) * 512], ones16[:, :], NL[:, t * 512:(t + 1) * 512], init,
                    ALU.mult, ALU.add), r=[b_o16, b_cn], w=[b_cn])
            for t in range(4):
                sch.op("dve", lambda e, t=t: e.tensor_scalar(qa[:, t * 2048:(t + 1) * 2048], CN[:, t * 2048:(t + 1) * 2048],
                                                            -8.0, None, ALU.mult), r=[b_cn], w=[b_qa])
            sch.dma(caug[:, :], qa[:, :], r=[b_qa], w=[b_caug])
            if dbg:
                sch.dma(cn_d[:, :], CN[:, :], r=[b_cn])
            Wt = TS(ph, "Wt", [16, 2048], F32)
            Wr = [TS(ph, "Wr%d" % k, [16, 2048], BF16) for k in range(3)]
            b_wt, b_wr = Buf(), Buf()
            for t4 in range(4):
                sl = slice(t4 * 2048, (t4 + 1) * 2048)
                if t4 < 2:
                    sch.op("dve", lambda e, sl=sl: e.tensor_scalar(Wt[:, :], CN[:, sl], 8.0, cols[0:16, C_NEG8:C_NEG8 + 1],
                                                                  ALU.mult, ALU.add), r=[b_cn, b_cols], w=[b_wt])
                else:
                    sch.op("dve", lambda e, sl=sl: e.tensor_scalar(Wt[:, :], CN[:, sl], 8.0, None, ALU.mult),
                           r=[b_cn], w=[b_wt])
                for k in range(3):
                    sch.op("dve", lambda e, k=k: e.tensor_copy(Wr[k][:, :], Wt[:, :]), r=[b_wt], w=[b_wr])
                    if k < 2:
                        sch.op("dve", lambda e, k=k: e.tensor_tensor(Wt[:, :], Wt[:, :], Wr[k][:, :], ALU.subtract),
                               r=[b_wt, b_wr], w=[b_wt])
                    sch.dma(caugk[k, :, sl], Wr[k][:, :], r=[b_wr], w=[b_caugk])

        sch.barrier()
        if upto >= 2:
            with contextlib.ExitStack() as ph:
                KTr = Ring([(TS(ph, "KT%d" % i, [68, S], BF16), Buf()) for i in range(2)])
                QTr = Ring([(TS(ph, "QT%d" % i, [68, S], BF16), Buf()) for i in range(2)])
                VTr = Ring([(TS(ph, "VT%d" % i, [128, NB, 128], BF16), Buf()) for i in range(2)])
                OTr = Ring([(TS(ph, "OT%d" % i, [64, S], BF16), Buf()) for i in range(2)])
                PTr = Ring([(TS(ph, "PT%d" % i, [128, 512], BF16), Buf()) for i in range(6)])
                SrF = TS(ph, "SrF", [128, 512], F32)
                b_sr = Buf()
                Ssh = TS(ph, "Ssh", [64, 512], F32)
                b_ssh = Buf()
                shiftF = TS(ph, "shiftF", [128, 128], F32)
                b_shift = Buf()
                cstC = TS(ph, "cstC", [128, 128], F32)
                b_cstC = Buf()
                sch.dma(cstC[:, :], consts_in[:, K_ID:K_ID + 128], w=[b_cstC])
                sch.op("dve", lambda e: e.memset(shiftF[:, :], 0.0), w=[b_shift])
                sch.op("dve", lambda e: e.tensor_copy(shiftF[:, 0:64], cstC[:, 64:128]), r=[b_cstC], w=[b_shift])
                sch.op("dve", lambda e: e.memset(SrF[:, :], 0.0), w=[b_sr])
                for (kt, bk) in KTr.items:
                    sch.op("dve", lambda e, kt=kt: e.memset(kt[64:65, :], 1.0), w=[bk])
                for (qt_, bq_) in QTr.items:
                    sch.op("dve", lambda e, qt_=qt_: e.memset(qt_[64:68, SO:S], 1.0), w=[bq_])
                for (vt_, bv_) in VTr.items:
                    sch.op("pool", lambda e, vt_=vt_: e.memset(vt_[:, :, 64:128], 1.0), w=[bv_])
                psS = Ring([(pbs[i], pbuf[i]) for i in (0, 1, 2, 3, 4, 5)])
                psO = Ring([(pbs[i], pbuf[i]) for i in (6,)])
                psH, bpsH = pbs[7], pbuf[7]
                Ocp = TS(ph, "Ocp", [128, 512], F32)
                b_ocp = Buf()
                GK = 3
                vhv = vv.rearrange("(n p) (h d) -> p n h d", p=128, d=HD)

                def load_head(h):
                    kt, bk = KTr.next()
                    qt, bq = QTr.next()
                    vt, bv = VTr.next()
                    sch.dma(kt[0:64, :], kT[h * 64:(h + 1) * 64, :], r=[b_kT], w=[bk])
                    sch.dma(kt[65:68, :], caugk[:, h, :], r=[b_caugk], w=[bk])
                    sch.dma(qt[0:64, SO:S], qT[h * 64:(h + 1) * 64, SO:S], r=[b_qT], w=[bq])
                    sch.dma(qt[64:65, SO:S], caug[h:h + 1, SO:S], r=[b_caug], w=[bq])
                    for part in range(4):
                        sch.dma(vt[:, part * 16:(part + 1) * 16, 0:64], vhv[:, part * 16:(part + 1) * 16, h, :],
                                r=[b_vv], w=[bv])
                    return kt, bk, qt, bq, vt, bv

                nh = load_head(0)
                for h in range(H):
                    kt, bk, qt, bq, vt, bv = nh
                    if h + 1 < H:
                        nh = load_head(h + 1)
                    ot, bo = OTr.next()
                    for j in range(T0, NT):
                        nkb = 4 * j + 4
                        pso, bpo = psO.next()

                        def qk(i):
                            m = i - 4 * j
                            c0 = 128 * max(m, 0)
                            pss, bps = psS.next()
                            sch.op("pe", lambda e, i=i, c0=c0, pss=pss: e.matmul(
                                pss[:, c0:512], lhsT=kt[0:68, i * 128:(i + 1) * 128],
                                rhs=qt[0:68, j * 512 + c0:(j + 1) * 512], start=True, stop=True),
                                r=[bk, bq], w=[bps])
                            return pss, bps, c0, m

                        groups = [list(range(g0, min(g0 + GK, nkb))) for g0 in range(0, nkb, GK)]
                        cur = [qk(i) for i in groups[0]]
                        for gi, grp in enumerate(groups):
                            mine = cur
                            if gi + 1 < len(groups):
                                cur = [qk(i) for i in groups[gi + 1]]
                            pts = []
                            for i, (pss, bps, c0, m) in zip(grp, mine):
                                pt, bpt = PTr.next()
                                sch.op("act", lambda e, i=i, c0=c0, pss=pss, pt=pt: e.activation(
                                    out=pt[:, c0:512], in_=pss[:, c0:512], func=AF.Exp, scale=0.125),
                                    r=[bps], w=[bpt])
                                if m >= 0:
                                    sch.op("dve", lambda e, c0=c0, pt=pt: e.tensor_tensor(
                                        pt[:, c0:c0 + 128], pt[:, c0:c0 + 128], trib[:, :], ALU.mult),
                                        r=[bpt, b_cb], w=[bpt])
                                pts.append((i, c0, pt, bpt))
                            for (i, c0, pt, bpt) in pts:
                                sch.op("pe", lambda e, i=i, c0=c0, pt=pt: e.matmul(
                                    pso[:, c0:512], lhsT=vt[:, i, :], rhs=pt[:, c0:512],
                                    start=(i == 0), stop=(i == nkb - 1)), r=[bv, bpt], w=[bpo])
                        sch.op("act", lambda e, pso=pso: e.copy(out=Ocp[:, :], in_=pso[:, :]), r=[bpo], w=[b_ocp])
                        sch.op("dve", lambda e: e.reciprocal(SrF[64:128, :], Ocp[64:128, :]), r=[b_ocp], w=[b_sr])
                        sch.dma(Ssh[:, :], SrF[64:128, :], r=[b_sr], w=[b_ssh])
                        sch.op("dve", lambda e, j=j, ot=ot: e.tensor_tensor(
                            ot[:, j * 512:(j + 1) * 512], Ocp[0:64, :], Ssh[:, :], ALU.mult),
                            r=[b_ocp, b_ssh], w=[bo])
                    sch.dma(oT[h * 64:(h + 1) * 64, SO:S], ot[:, SO:S], r=[bo], w=[b_oT])

        sch.barrier()
        ac_stack.close()

        hTsv = hTs.rearrange("(c p) t -> p c t", p=128)
        xsv = xs.rearrange("(n p) d -> p n d", p=128)
        outv = out.rearrange("(n p) d -> p n d", p=128)

        def out_proj_phase(src_x_view, lhs_loader, w_src, gc0, tagp):
            with contextlib.ExitStack() as ph:
                Wo = TS(ph, "Wo" + tagp, [128, 8, 1024], BF16)
                bWo = Buf()
                stg = [TS(ph, "stgD%s%d" % (tagp, i), [128, 4, 512], F32) for i in range(2)]
                bstg = [Buf(), Buf()]
                load_w(Wo, bWo, w_src, 8, 1024, stg, bstg)
                lts = Ring([(TS(ph, "lt%s%d" % (tagp, i), [128, 8, 512], BF16), Buf()) for i in range(2)])
                xts = Ring([(TS(ph, "xtD%s%d" % (tagp, i), [128, 4, 1024], F32), Buf()) for i in range(2)])
                hto = Ring([(TS(ph, "hto%s%d" % (tagp, i), [128, 8, 512], BF16), Buf()) for i in range(2)])
                ns = norm_scratch(ph, "D" + tagp)
                psT = Ring([(pbs[i], pbuf[i]) for i in (0, 1)])
                psM = Ring([(pbs[i], pbuf[i]) for i in (2, 3, 4, 5)])

                def loads(t):
                    lt, bl = lts.next()
                    xt, bx = xts.next()
                    lhs_loader(t, lt, bl)
                    sch.dma(xt[:, :, :], src_x_view[:, t * 4:(t + 1) * 4, :], w=[bx])
                    return lt, bl, xt, bx

                nxt = loads(T0)
                for t in range(T0, NT):
                    lt, bl, xt, bx = nxt
                    if t + 1 < NT:
                        nxt = loads(t + 1)
                    for b in range(4):
                        for hf in range(2):
                            pbk, bpb = psM.next()
                            for c in range(8):
                                sch.op("pe", lambda e, c=c, b=b, hf=hf, pbk=pbk, lt=lt: e.matmul(
                                    pbk[:, :], lhsT=lt[:, c, b * 128:(b + 1) * 128],
                                    rhs=Wo[:, c, hf * 512:(hf + 1) * 512], start=(c == 0), stop=(c == 7)),
                                    r=[bWo, bl], w=[bpb])
                            sch.op("dve", lambda e, b=b, hf=hf, pbk=pbk, xt=xt: e.tensor_tensor(
                                xt[:, b, hf * 512:(hf + 1) * 512], pbk[:, :], xt[:, b, hf * 512:(hf + 1) * 512], ALU.add),
                                r=[bpb, bx], w=[bx])
                    sch.dma(xsv[:, t * 4:(t + 1) * 4, :], xt[:, :, :], r=[bx], w=[b_xs[t]])
                    ho, bho = hto.next()
                    norm_tile(ns, xt, bx, gc0, ho, bho, psT)
                    sch.dma(hTsv[:, :, t * 512:(t + 1) * 512], ho[:, :, :], r=[bho], w=[b_hTs[t]])
            sch.barrier()

        def ffn_phase(l, gc_next, final):
            TF = 512
            with contextlib.ExitStack() as ph:
                Wup = TS(ph, "Wup%d" % l, [128, 8, 2 * FF], BF16)
                Wdn = TS(ph, "Wdn%d" % l, [128, NFC, 1024], BF16)
                bWup, bWdn = Buf(), Buf()
                with contextlib.ExitStack() as ph2:
                    stg = [TS(ph2, "stgF%d%d" % (l, i), [128, 4, 512], F32) for i in range(6)]
                    bstg = [Buf() for _ in range(6)]
                    load_w(Wup, bWup, w_up[l], 8, 2 * FF, stg, bstg)
                    load_w(Wdn, bWdn, w_dn[l], NFC, 1024, stg, bstg)
                    sch.barrier()
                hin = Ring([(TS(ph, "hin%d%d" % (l, i), [128, 8, TF], BF16), Buf()) for i in range(2)])
                xt = TS(ph, "xtF%d" % l, [128, 4, 1024], F32)
                bx = Buf()
                R = TS(ph, "R%d" % l, [128, NFC * TF], BF16)
                bR = Buf()
                actv = R[:, :].rearrange("p (c t) -> p c t", t=TF)
                xn = R[:, 0:4096].rearrange("p (b d) -> p b d", d=1024)
                hto = R[:, 4096:8192].rearrange("p (c t) -> p c t", t=TF)
                junk = R[:, 8192:9216]
                ssq = TS(ph, "ssqF%d" % l, [128, 4], F32)
                rstd = TS(ph, "rstdF%d" % l, [128, 4], F32)
                ns = (junk, bR, ssq, Buf(), rstd, Buf(), xn, bR)
                Ur = Ring([(TS(ph, "U%d%d" % (l, i), [128, TF + 2], BF16), Buf()) for i in range(4)])
                yr = Ring([(TS(ph, "y%d%d" % (l, i), [128, TF], F32), Buf()) for i in range(4)])
                carry = TS(ph, "carry%d" % l, [128, 2 * NFC, 2], BF16)
                bcar = [Buf() for _ in range(2 * NFC)]
                if final:
                    gfin = TS(ph, "gfin", [128, 1024], F32)
                    b_gfin = Buf()
                    sch.dma(gfin[:, :], fgain_in.partition_broadcast(128), w=[b_gfin])
                psT = Ring([(pbs[i], pbuf[i]) for i in (0, 1)])
                psU = Ring([(pbs[i], pbuf[i]) for i in (2, 3, 4, 5)])
                psD = Ring([(pbs[i], pbuf[i]) for i in (6, 7)])
                NTF = S // TF
                cw0 = C_CONVW + l * 3 * 44
                cb0 = C_CONVB + l * 44
                hl = TS(ph, "hl%d" % l, [128, 8, 128], BF16)
                b_hl = Buf()
                halo = TS(ph, "halo%d" % l, [128, 2 * NFC, 2], F32)
                b_halo, b_ha, b_hg = Buf(), Buf(), Buf()
                sch.dma(hl[:, :, :], hTsv[:, :, S - 128:S], r=[b_hTs[NT - 1]], w=[b_hl])
                for ci in range(2 * NFC):
                    pbk, bpb = psU.next()
                    for c in range(8):
                        sch.op("pe", lambda e, c=c, ci=ci, pbk=pbk: e.matmul(
                            pbk[:, 0:128], lhsT=Wup[:, c, ci * 128:(ci + 1) * 128], rhs=hl[:, c, :],
                            start=(c == 0), stop=(c == 7)), r=[bWup, b_hl], w=[bpb])
                    sch.op("act", lambda e, ci=ci, pbk=pbk: e.copy(out=halo[:, ci, :], in_=pbk[:, 126:128]),
                           r=[bpb], w=[b_halo])
                sch.op("dve", lambda e: e.tensor_scalar(halo[:, :, :], halo[:, :, :], cols[:, C_M0:C_M0 + 1], None, ALU.mult),
                       r=[b_halo, b_cols], w=[b_halo])
                sch.dma(halo_a[l][:, :], halo[:, :, :].rearrange("p c k -> p (c k)"), r=[b_halo], w=[b_ha])
                sch.collective(es, halo_a[l][:, :], halo_g[l][:, :], r=[b_ha], w=[b_hg])
                sch.dma(halo[:, :, :].rearrange("p c k -> p (c k)"), halo_g[l][:, :], r=[b_hg], w=[b_halo])
                sch.op("dve", lambda e: e.tensor_scalar(carry[:, :, :], halo[:, :, :], cols[:, C_PM:C_PM + 1], None, ALU.mult),
                       r=[b_halo, b_cols], w=bcar)

                def load_h(t):
                    hi, bhi = hin.next()
                    sch.dma(hi[:, :, :], hTsv[:, :, t * TF:(t + 1) * TF], r=[b_hTs[t]], w=[bhi])
                    return hi, bhi

                def up(j, hi, bhi):
                    res = []
                    for ci in (j, NFC + j):
                        pbk, bpb = psU.next()
                        for c in range(8):
                            sch.op("pe", lambda e, c=c, ci=ci, pbk=pbk: e.matmul(
                                pbk[:, :], lhsT=Wup[:, c, ci * 128:(ci + 1) * 128], rhs=hi[:, c, :],
                                start=(c == 0), stop=(c == 7)), r=[bWup, bhi], w=[bpb])
                        res.append((ci, pbk, bpb))
                    return res

                def conv(ci, pbk, bpb):
                    U, bU = Ur.next()
                    y, by = yr.next()
                    wc = lambda tp: cols[:, cw0 + tp * 44 + ci:cw0 + tp * 44 + ci + 1]
                    sch.op("pool", lambda e: e.tensor_copy(U[:, 0:2], carry[:, ci, :]), r=[bcar[ci]], w=[bU])
                    sch.op("act", lambda e: e.copy(out=U[:, 2:TF + 2], in_=pbk[:, :]), r=[bpb], w=[bU])
                    sch.op("act", lambda e: e.activation(out=y[:, :], in_=pbk[:, :], func=AF.Identity, scale=wc(2),
                                                         bias=cols[:, cb0 + ci:cb0 + ci + 1]), r=[bpb, b_cols], w=[by])
                    sch.op("pool", lambda e: e.tensor_copy(carry[:, ci, :], U[:, TF:TF + 2]), r=[bU], w=[bcar[ci]])
                    sch.op("dve", lambda e: e.scalar_tensor_tensor(y[:, :], U[:, 1:TF + 1], wc(1), y[:, :],
                                                                   ALU.mult, ALU.add), r=[bU, by, b_cols], w=[by])
                    sch.op("dve", lambda e: e.scalar_tensor_tensor(y[:, :], U[:, 0:TF], wc(0), y[:, :],
                                                                   ALU.mult, ALU.add), r=[bU, by, b_cols], w=[by])
                    return y, by

                nxt = load_h(T0)
                sch.dma(xt[:, :, :], xsv[:, T0 * 4:(T0 + 1) * 4, :], r=[b_xs[T0]], w=[bx])
                for t in range(T0, NTF):
                    hi, bhi = nxt
                    if t + 1 < NTF:
                        nxt = load_h(t + 1)
                    cur = up(0, hi, bhi)
                    for j in range(NFC):
                        mine = cur
                        if j + 1 < NFC:
                            cur = up(j + 1, hi, bhi)
                        (cg, pg, bpg), (cv, pv_, bpv) = mine
                        yg, byg = conv(cg, pg, bpg)
                        yv, byv = conv(cv, pv_, bpv)
                        sch.op("act", lambda e, yg=yg: e.activation(out=yg[:, :], in_=yg[:, :], func=AF.Silu),
                               r=[byg], w=[byg])
                        sch.op("dve", lambda e, j=j, yg=yg, yv=yv: e.tensor_tensor(actv[:, j, :], yg[:, :], yv[:, :], ALU.mult),
                               r=[byg, byv], w=[bR])
                    for b in range(4):
                        for hf in range(2):
                            pbk, bpb = psD.next()
                            for j in range(NFC):
                                sch.op("pe", lambda e, j=j, b=b, hf=hf, pbk=pbk: e.matmul(
                                    pbk[:, :], lhsT=actv[:, j, b * 128:(b + 1) * 128], rhs=Wdn[:, j, hf * 512:(hf + 1) * 512],
                                    start=(j == 0), stop=(j == NFC - 1)), r=[bWdn, bR], w=[bpb])
                            sch.op("dve", lambda e, b=b, hf=hf, pbk=pbk: e.tensor_tensor(
                                xt[:, b, hf * 512:(hf + 1) * 512], pbk[:, :], xt[:, b, hf * 512:(hf + 1) * 512], ALU.add),
                                r=[bpb, bx], w=[bx])
                    if not final:
                        sch.dma(xsv[:, t * 4:(t + 1) * 4, :], xt[:, :, :], r=[bx], w=[b_xs[t]])
                        norm_tile(ns, xt, bx, gc_next, hto, bR, psT)
                        sch.dma(hTsv[:, :, t * TF:(t + 1) * TF], hto, r=[bR], w=[b_hTs[t]])
                    else:
                        for b in range(4):
                            sch.op("act", lambda e, b=b: e.activation(out=junk, in_=xt[:, b, :], func=AF.Square,
                                                                      accum_out=ssq[:, b:b + 1]), r=[bx], w=[bR, ns[3]])
                        sch.op("act", lambda e: e.activation(out=rstd[:, 0:4], in_=ssq[:, 0:4], func=AF.Sqrt,
                                                             scale=1.0 / D, bias=EPS), r=[ns[3]], w=[ns[5]])
                        sch.op("dve", lambda e: e.reciprocal(rstd[:, 0:4], rstd[:, 0:4]), r=[ns[5]], w=[ns[5]])
                        for b in range(4):
                            sch.op("dve", lambda e, b=b: e.scalar_tensor_tensor(
                                xt[:, b, :], xt[:, b, :], rstd[:, b:b + 1], gfin[:, :], ALU.mult, ALU.mult),
                                r=[bx, ns[5], b_gfin], w=[bx])
                        sch.dma(outv[:, (t - T0) * 4:(t - T0 + 1) * 4, :], xt[:, :, :], r=[bx], w=[b_out])
                    if t + 1 < NTF:
                        sch.dma(xt[:, :, :], xsv[:, (t + 1) * 4:(t + 2) * 4, :], r=[b_xs[t + 1]], w=[bx])
            sch.barrier()

        if upto >= 3:
            oTv = oT.rearrange("(c p) t -> p c t", p=128)
            xv0 = x_in.rearrange("(n p) d -> p n d", p=128)
            out_proj_phase(xv0, lambda t, lt, bl: sch.dma(lt[:, :, :], oTv[:, :, t * 512:(t + 1) * 512], r=[b_oT], w=[bl]),
                           w_att_out, C_FFNG0, "a")
        if upto >= 4:
            ffn_phase(0, C_HGRNG, final=(upto == 4))

        def hgrn_phase():
            with contextlib.ExitStack() as ph:
                Wh = TS(ph, "Wh", [128, 8, 4096], BF16)
                Woh = TS(ph, "Woh", [128, 8, 1024], BF16)
                bWh, bWoh = Buf(), Buf()
                with contextlib.ExitStack() as ph2:
                    stg = [TS(ph2, "stgH%d" % i, [128, 4, 512], F32) for i in range(6)]
                    bstg = [Buf() for _ in range(6)]
                    load_w(Wh, bWh, w_h_in, 8, 4096, stg, bstg)
                    load_w(Woh, bWoh, w_h_out, 8, 1024, stg, bstg)
                    sch.barrier()
                hin = Ring([(TS(ph, "hinH%d" % i, [128, 8, 512], BF16), Buf()) for i in range(2)])
                xtl = Ring([(TS(ph, "xtH%d" % i, [128, 4, 1024], F32), Buf()) for i in range(1)])
                IV = TS(ph, "IV", [128, 4, 1024], BF16)
                bIV = Buf()
                OGT = TS(ph, "OGT", [128, 8, 512], BF16)
                bOGT = Buf()
                junk = TS(ph, "junkH", [128, 1024], BF16)
                ssq = TS(ph, "ssqH", [128, 4], F32)
                rstd = TS(ph, "rstdH", [128, 4], F32)
                ns = (junk, Buf(), ssq, Buf(), rstd, Buf(), IV, bIV)
                NSET = 2
                A = [[(TS(ph, "A%d_%d" % (k, i), [128, 512], F32), Buf()) for k in range(5)] for i in range(NSET)]
                Bt = [[(TS(ph, "B%d_%d" % (k, i), [128, 512], BF16), Buf()) for k in range(6)] for i in range(NSET)]
                Ct = [[(TS(ph, "C%d_%d" % (k, i), [128, 512], F32), Buf()) for k in range(2)] for i in range(NSET)]
                Tst = TS(ph, "Tst", [128, 8, 128], F32)
                bT = [Buf() for _ in range(8)]
                Sbf = TS(ph, "Sbf", [128, 8, 2, 128], BF16)
                bS = [[Buf(), Buf()] for _ in range(8)]
                dprev = TS(ph, "dprev", [128, 8], F32)
                bdp = [Buf() for _ in range(8)]
                lbc = TS(ph, "lbc", [128, 8], F32)
                omlc = TS(ph, "omlc", [128, 8], F32)
                nomlc = TS(ph, "nomlc", [128, 8], F32)
                b_lb = Buf()
                bd4 = TS(ph, "bd4", [128, 512], BF16)
                b_bd4 = Buf()
                sch.op("dve", lambda e: e.tensor_tensor(lbc[:, :], cols[:, C_LB + 8:C_LB + 16], cols[:, C_LB:C_LB + 8],
                                                        ALU.subtract), r=[b_cols], w=[b_lb])
                sch.op("act", lambda e: e.activation(out=lbc[:, :], in_=lbc[:, :], func=AF.Sigmoid), r=[b_lb], w=[b_lb])
                sch.op("dve", lambda e: e.tensor_scalar(omlc[:, :], lbc[:, :], -1.0, 1.0, ALU.mult, ALU.add),
                       r=[b_lb], w=[b_lb])
                sch.op("dve", lambda e: e.tensor_scalar(omlc[:, :], omlc[:, :], 0.5, None, ALU.mult),
                       r=[b_lb], w=[b_lb])
                sch.op("dve", lambda e: e.tensor_tensor(lbc[:, :], lbc[:, :], omlc[:, :], ALU.add),
                       r=[b_lb], w=[b_lb])
                sch.op("dve", lambda e: e.tensor_scalar(nomlc[:, :], omlc[:, :], -1.0, None, ALU.mult),
                       r=[b_lb], w=[b_lb])
                cstH, b_cst = load_cst(ph, "H", K_BD, K_RM + 512)
                for k in range(4):
                    sch.op("dve", lambda e, k=k: e.tensor_copy(bd4[:, k * 128:(k + 1) * 128], cstH[:, 0:128]),
                           r=[b_cst], w=[b_bd4])
                rmask = cstH[:, K_RM - K_BD:K_RM - K_BD + 512]
                psP = Ring([(pbs[i], pbuf[i]) for i in (0, 1, 2, 3)])
                psA, bpsA = pbs[5], pbuf[5]
                psO, bpsO = pbs[6], pbuf[6]
                psXv = pbs[5][:, 0:256].bitcast(BF16)
                bpsX = pbuf[5]
                psUr = Ring([(pbs[4][:, 0:128], pbuf[4]), (pbs[7][:, 0:128], pbuf[7])])
                psT = Ring([(pbs[i], pbuf[i]) for i in (5, 6)])
                evi = [0]

                def loads(t):
                    hi, bhi = hin.next()
                    sch.dma(hi[:, :, :], hTsv[:, :, t * 512:(t + 1) * 512], r=[b_hTs[t]], w=[bhi])
                    return hi, bhi

                def proj(hi, bhi, hh, pre):
                    res = []
                    for base in ((1024,) if pre else (0, 3072)):
                        pbk, bpb = psP.next()
                        for c in range(8):
                            sch.op("pe", lambda e, c=c, base=base, pbk=pbk: e.matmul(
                                pbk[:, :], lhsT=Wh[:, c, base + hh * 128:base + (hh + 1) * 128], rhs=hi[:, c, :],
                                start=(c == 0), stop=(c == 7)), r=[bWh, bhi], w=[bpb])
                        res.append((pbk, bpb))
                    return res

                def elem_gen(pr, hh, st, pre, t):
                    (A1, b1), (A2, b2), (A3, b3), (A4, b4), (A5, b5) = A[st]
                    (QD, bqd), (KI, bki), (GS, bgs), (KIT, bkit) = Bt[st][0], Bt[st][1], Bt[st][2], Bt[st][3]
                    slot = (t - T0) * 8 + hh
                    if not pre:
                        (pq, bpq), (pg, bpg) = pr[0], pr[1]
                        sch.dma(A2[:, :], eb_s[slot], r=[b_ebs], w=[b2])
                        sch.dma(KI[:, :], ki_s[slot], r=[b_kis], w=[bki])
                        sch.dma(KIT[:, :], kit_s[slot], r=[b_kits], w=[bkit])
                        sch.op("act", lambda e: e.activation(out=A5[:, :], in_=pq[:, :], func=AF.Silu), r=[bpq], w=[b5])
                        sch.op("act", lambda e: e.activation(out=GS[:, :], in_=pg[:, :], func=AF.Silu), r=[bpg], w=[bgs])
                        yield
                        sch.op("dve", lambda e: e.tensor_tensor(QD[:, :], A5[:, :], A2[:, :], ALU.mult), r=[b5, b2], w=[bqd])
                        return
                    (pf, bpf) = pr[0]
                    sch.op("act", lambda e: e.activation(out=A1[:, :], in_=pf[:, :], func=AF.Tanh, scale=0.5), r=[bpf], w=[b1])
                    yield
                    sch.op("dve", lambda e: e.tensor_scalar(A2[:, :], A1[:, :], omlc[:, hh:hh + 1], lbc[:, hh:hh + 1],
                                                            ALU.mult, ALU.add), r=[b1, b_lb], w=[b2])
                    sch.op("dve", lambda e: e.tensor_scalar(A1[:, :], A1[:, :], nomlc[:, hh:hh + 1], omlc[:, hh:hh + 1],
                                                            ALU.mult, ALU.add), r=[b1, b_lb], w=[b1])
                    yield
                    sch.op("act", lambda e: e.activation(out=A2[:, :], in_=A2[:, :], func=AF.Ln), r=[b2], w=[b2])
                    yield
                    sch.op("dve", lambda e: e.tensor_tensor_scan(A3[:, :], rmask, A2[:, :], 0.0, ALU.mult, ALU.add),
                           r=[b2, b_cst], w=[b3])
                    yield
                    sch.op("act", lambda e: e.activation(out=A2[:, :], in_=A3[:, :], func=AF.Exp), r=[b3], w=[b2])
                    sch.op("act", lambda e: e.activation(out=A4[:, :], in_=A3[:, :], func=AF.Exp, scale=-1.0),
                           r=[b3], w=[b4])
                    yield
                    sch.op("dve", lambda e: e.tensor_tensor(KI[:, :], A1[:, :], A4[:, :], ALU.mult), r=[b1, b4], w=[bki])
                    sch.dma(eb_s[slot], A2[:, :], r=[b2], w=[b_ebs])
                    sch.dma(ki_s[slot], KI[:, :], r=[bki], w=[b_kis])

                def chunks(t, hh, st, pre, g=None):
                    def pull(k=1):
                        for _ in range(k):
                            if g is not None:
                                next(g, None)

                    (A2, b2) = A[st][1]
                    (QD, bqd), (KI, bki), (GS, bgs), (KIT, bkit), (AT, bat), (SQ, bsq) = Bt[st]
                    (RS, brs), (OG, bog) = Ct[st]
                    if pre:
                        for b in range(4):
                            sch.op("pe", lambda e, b=b: e.transpose(out=psXv[:, b * 128:(b + 1) * 128],
                                                                    in_=KI[:, b * 128:(b + 1) * 128], identity=identb[:, :]),
                                   r=[bki, b_cb], w=[bpsX])
                        sch.op("act", lambda e: e.copy(out=KIT[:, :], in_=psXv[:, :]), r=[bpsX], w=[bkit])
                        sch.dma(kit_s[(t - T0) * 8 + hh], KIT[:, :], r=[bkit], w=[b_kits])
                        pull()
                    if not pre:
                        for b in range(4):
                            sch.op("pe", lambda e, b=b: e.matmul(psA[:, b * 128:(b + 1) * 128], lhsT=KI[:, b * 128:(b + 1) * 128],
                                                                 rhs=QD[:, b * 128:(b + 1) * 128], start=True, stop=True),
                                   r=[bki, bqd], w=[bpsA])
                        sch.op("dve", lambda e: e.tensor_tensor(AT[:, :], psA[:, :], bd4[:, :], ALU.mult),
                               r=[bpsA, b_bd4], w=[bat])
                    for idx in range(8):
                        b, ee = idx // 2, idx % 2
                        n = 8 * (t - T0) + idx
                        loc = idx * 64
                        pu, bpu = psUr.next()
                        sch.op("pe", lambda e, b=b, ee=ee, pu=pu: e.matmul(
                            pu, lhsT=KIT[ee * 64:(ee + 1) * 64, b * 128:(b + 1) * 128],
                            rhs=IV[ee * 64:(ee + 1) * 64, b, hh * 128:(hh + 1) * 128], start=True, stop=True),
                            r=[bkit, bIV], w=[bpu])
                        if ee == 0 and not pre:
                            sch.op("pe", lambda e, b=b: e.matmul(psO[:, b * 128:(b + 1) * 128],
                                                                 lhsT=IV[:, b, hh * 128:(hh + 1) * 128],
                                                                 rhs=AT[:, b * 128:(b + 1) * 128],
                                                                 start=True, stop=False), r=[bIV, bat], w=[bpsO])
                        if not pre:
                            c0 = loc
                            sch.op("pe", lambda e, c0=c0, n=n, ee=ee: e.matmul(
                                psO[:, c0:c0 + 64], lhsT=Sbf[:, hh, (n - 1) % 2, :], rhs=QD[:, c0:c0 + 64],
                                start=False, stop=(ee == 1)), r=[bS[hh][(n - 1) % 2], bqd], w=[bpsO])
                        if n == 0 and pre:
                            sch.op("dve", lambda e, pu=pu: e.tensor_copy(Tst[:, hh, :], pu), r=[bpu], w=[bT[hh]])
                        elif n == 0:
                            sch.op("dve", lambda e, pu=pu: e.tensor_tensor(Tst[:, hh, :], pu, Sinit[:, hh, :], ALU.add),
                                   r=[bpu, b_sinit], w=[bT[hh]])
                        else:
                            dec = A2[:, loc - 1:loc] if loc > 0 else dprev[:, hh:hh + 1]
                            rr = [bpu, bT[hh], b2] if loc > 0 else [bpu, bT[hh], bdp[hh]]
                            sch.op("dve", lambda e, pu=pu, dec=dec: e.scalar_tensor_tensor(
                                Tst[:, hh, :], Tst[:, hh, :], dec, pu, ALU.mult, ALU.add), r=rr, w=[bT[hh]])
                        if not pre:
                            sch.op("dve", lambda e, loc=loc, n=n: e.tensor_scalar(
                                Sbf[:, hh, n % 2, :], Tst[:, hh, :], A2[:, loc + 63:loc + 64], None, ALU.mult),
                                r=[bT[hh], b2], w=[bS[hh][n % 2]])
                        if pre and idx % 2 == 1:
                            pull()
                        if (not pre) and idx in (3, 7):
                            pull()
                    if pre:
                        if t == NT - 1:
                            sch.op("dve", lambda e: e.tensor_scalar(Sinit[:, hh, :], Tst[:, hh, :], A2[:, 511:512], None,
                                                                    ALU.mult), r=[bT[hh], b2], w=[b_sinit])
                        sch.op("pool", lambda e: e.tensor_copy(dprev[:, hh:hh + 1], A2[:, 511:512]), r=[b2], w=[bdp[hh]])
                        return
                    sch.op("act", lambda e: e.activation(out=SQ[:, :], in_=psO[:, :], func=AF.Square), r=[bpsO], w=[bsq])
                    sch.op("pe", lambda e: e.matmul(psA[:, :], lhsT=ones_b[:, :], rhs=SQ[:, :], start=True, stop=True),
                           r=[bsq, b_cb], w=[bpsA])
                    sch.op("act", lambda e: e.activation(out=RS[:, :], in_=psA[:, :], func=AF.Ln, scale=1.0 / 128.0,
                                                         bias=EPS), r=[bpsA], w=[brs])
                    sch.op("act", lambda e: e.activation(out=RS[:, :], in_=RS[:, :], func=AF.Exp, scale=-0.5),
                           r=[brs], w=[brs])
                    sch.op("dve", lambda e: e.scalar_tensor_tensor(OG[:, :], psO[:, :], cols[:, C_ONORM + hh:C_ONORM + hh + 1],
                                                                   RS[:, :], ALU.mult, ALU.mult),
                           r=[bpsO, brs, b_cols], w=[bog])
                    sch.op("dve", lambda e: e.tensor_tensor(OGT[:, hh, :], OG[:, :], GS[:, :], ALU.mult),
                           r=[bog, bgs], w=[bOGT])
                    sch.op("pool", lambda e: e.tensor_copy(dprev[:, hh:hh + 1], A2[:, 511:512]), r=[b2], w=[bdp[hh]])

                b_ebs, b_kis, b_kits, b_ivs = Buf(), Buf(), Buf(), Buf()
                Sinit = TS(ph, "Sinit", [128, 8, 128], F32)
                b_sinit, b_sta, b_stg = Buf(), Buf(), Buf()

                def iv_proj(hi, bhi):
                    for b in range(4):
                        for hf in range(2):
                            pbk, bpb = psP.next()
                            for c in range(8):
                                sch.op("pe", lambda e, c=c, b=b, hf=hf, pbk=pbk: e.matmul(
                                    pbk[:, :], lhsT=hi[:, c, b * 128:(b + 1) * 128],
                                    rhs=Wh[:, c, 2048 + hf * 512:2048 + (hf + 1) * 512], start=(c == 0), stop=(c == 7)),
                                    r=[bWh, bhi], w=[bpb])
                            if evi[0] % 2 == 0:
                                sch.op("dve", lambda e, b=b, hf=hf, pbk=pbk: e.tensor_copy(
                                    IV[:, b, hf * 512:(hf + 1) * 512], pbk[:, :]), r=[bpb], w=[bIV])
                            else:
                                sch.op("act", lambda e, b=b, hf=hf, pbk=pbk: e.copy(
                                    out=IV[:, b, hf * 512:(hf + 1) * 512], in_=pbk[:, :]), r=[bpb], w=[bIV])
                            evi[0] += 1

                def heads(t, hi, bhi, pre):
                    pr = proj(hi, bhi, 0, pre)
                    for _ in elem_gen(pr, 0, 0, pre, t):
                        pass
                    for hh in range(8):
                        st = hh % NSET
                        g = None
                        if hh + 1 < 8:
                            pr = proj(hi, bhi, hh + 1, pre)
                            g = elem_gen(pr, hh + 1, (hh + 1) % NSET, pre, t)
                        chunks(t, hh, st, pre, g)
                        if g is not None:
                            for _ in g:
                                pass

                nxt = loads(T0)
                for t in range(T0, NT):
                    hi, bhi = nxt
                    if t + 1 < NT:
                        nxt = loads(t + 1)
                    iv_proj(hi, bhi)
                    sch.dma(iv_s[t - T0], IV[:, :, :].rearrange("p b c -> p (b c)"), r=[bIV], w=[b_ivs])
                    heads(t, hi, bhi, True)
                sch.op("dve", lambda e: e.tensor_scalar(Sinit[:, :, :], Sinit[:, :, :], cols[:, C_M0:C_M0 + 1], None, ALU.mult),
                       r=[b_sinit, b_cols], w=[b_sinit])
                sch.dma(st_a[:, :], Sinit[:, :, :].rearrange("p h v -> p (h v)"), r=[b_sinit], w=[b_sta])
                sch.collective(es, st_a[:, :], st_g[:, :], r=[b_sta], w=[b_stg])
                sch.dma(Sinit[:, :, :].rearrange("p h v -> p (h v)"), st_g[:, :], r=[b_stg], w=[b_sinit])
                sch.op("dve", lambda e: e.tensor_scalar(Sinit[:, :, :], Sinit[:, :, :], cols[:, C_PM:C_PM + 1], None, ALU.mult),
                       r=[b_sinit, b_cols], w=[b_sinit])
                sch.op("dve", lambda e: e.tensor_copy(Sbf[:, :, 1, :], Sinit[:, :, :]), r=[b_sinit], w=[bS[hh][1] for hh in range(8)])

                nxt = loads(T0)
                for t in range(T0, NT):
                    hi, bhi = nxt
                    if t + 1 < NT:
                        nxt = loads(t + 1)
                    xt, bx = xtl.next()
                    sch.dma(xt[:, :, :], xsv[:, t * 4:(t + 1) * 4, :], r=[b_xs[t]], w=[bx])
                    sch.dma(IV[:, :, :].rearrange("p b c -> p (b c)"), iv_s[t - T0], r=[b_ivs], w=[bIV])
                    heads(t, hi, bhi, False)
                    for b in range(4):
                        for hf in range(2):
                            pbk, bpb = psP.next()
                            for hh in range(8):
                                sch.op("pe", lambda e, hh=hh, b=b, hf=hf, pbk=pbk: e.matmul(
                                    pbk[:, :], lhsT=OGT[:, hh, b * 128:(b + 1) * 128],
                                    rhs=Woh[:, hh, hf * 512:(hf + 1) * 512], start=(hh == 0), stop=(hh == 7)),
                                    r=[bWoh, bOGT], w=[bpb])
                            sch.op("dve", lambda e, b=b, hf=hf, pbk=pbk, xt=xt: e.tensor_tensor(
                                xt[:, b, hf * 512:(hf + 1) * 512], pbk[:, :], xt[:, b, hf * 512:(hf + 1) * 512], ALU.add),
                                r=[bpb, bx], w=[bx])
                    sch.dma(xsv[:, t * 4:(t + 1) * 4, :], xt[:, :, :], r=[bx], w=[b_xs[t]])
                    norm_tile(ns, xt, bx, C_FFNG1, OGT, bOGT, psT)
                    sch.dma(hTsv[:, :, t * 512:(t + 1) * 512], OGT[:, :, :], r=[bOGT], w=[b_hTs[t]])
            sch.barrier()

        if upto >= 5:
            hgrn_phase()
        if upto >= 6:
            ffn_phase(1, 0, final=True)

        sch.finish()
        print("sched: ops", sch.nops, "waits", sch.nwait, "cnt", sch.cnt, flush=True)
    return nc


def host_tables(inp):
    f = np.float32

    def colize(v):
        v = np.asarray(v, f)
        return np.ascontiguousarray(v.reshape(-1, 128).T)

    cols = np.zeros((128, NCOL), f)
    cols[:, C_ATTG:C_ATTG + 8] = colize(inp["att_norm_g"][0])
    cols[:, C_FFNG0:C_FFNG0 + 8] = colize(inp["ffn_norm_g"][0])
    cols[:, C_HGRNG:C_HGRNG + 8] = colize(inp["hgrn_norm_g"][0])
    cols[:, C_FFNG1:C_FFNG1 + 8] = colize(inp["ffn_norm_g"][1])
    cw = np.asarray(inp["ffn_conv_w"], f)
    cb = np.asarray(inp["ffn_conv_b"], f)
    for l in range(2):
        for tp in range(3):
            o = C_CONVW + (l * 3 + tp) * 44
            cols[:, o:o + 44] = colize(cw[l, tp])
        o = C_CONVB + l * 44
        cols[:, o:o + 44] = colize(cb[l])
    lbl = np.asarray(inp["hgrn_lb_logits"], f)
    for l in range(2):
        cols[:, C_LB + l * 8:C_LB + (l + 1) * 8] = colize(lbl[l])
    cols[:, C_ONORM:C_ONORM + 8] = colize(inp["hgrn_onorm_g"][0])
    cols[0:16, C_BF] = np.asarray(inp["att_b_f"], f)[0]
    cst = np.zeros((128, NCONST), f)
    cst[:, K_ID:K_ID + 128] = np.eye(128, dtype=f)
    ii = np.arange(128)
    cst[:, K_TRI:K_TRI + 128] = (ii[:, None] <= ii[None, :]).astype(f)
    bd = (ii[:, None] <= ii[None, :]) & ((ii[:, None] // 64) == (ii[None, :] // 64))
    cst[:, K_BD:K_BD + 128] = bd.astype(f)
    rm = np.ones((128, 512), f)
    rm[:, 0::64] = 0.0
    cst[:, K_RM:K_RM + 512] = rm
    return cols, cst


_NC_CACHE = {}


def make_in_maps(inp, n_cores=8):
    cols, cst = host_tables(inp)
    x = np.asarray(inp["x"], np.float32)
    shared = {
        "att_w_in": np.ascontiguousarray(np.asarray(inp["att_w_in"], np.float32)[0]),
        "att_w_out": np.ascontiguousarray(np.asarray(inp["att_w_out"], np.float32)[0]),
        "hgrn_w_in": np.ascontiguousarray(np.asarray(inp["hgrn_w_in"], np.float32)[0]),
        "hgrn_w_out": np.ascontiguousarray(np.asarray(inp["hgrn_w_out"], np.float32)[0]),
        "ffn_w_up": np.ascontiguousarray(np.asarray(inp["ffn_w_up"], np.float32)),
        "ffn_w_down": np.ascontiguousarray(np.asarray(inp["ffn_w_down"], np.float32)),
        "cols": cols, "consts": cst,
        "final_norm_g": np.ascontiguousarray(np.asarray(inp["final_norm_g"], np.float32).reshape(1, D)),
    }
    maps = []
    for c in range(n_cores):
        b, p = c // 2, c % 2
        m = dict(shared)
        m["x"] = np.ascontiguousarray(np.concatenate([x[b, 0:SO], x[b, p * SO:(p + 1) * SO]], axis=0))
        cc = cols.copy()
        cc[:, C_PM] = float(p)
        cc[:, C_M0] = float(1 - p)
        cc[:, C_NEG] = float((p - 1) * 30000)
        cc[:, C_NEG8] = float((p - 1) * 240000)
        m["cols"] = cc
        maps.append(m)
    return maps


def kernel(**inputs):
    if "nc" not in _NC_CACHE:
        _NC_CACHE["nc"] = build()
    nc = _NC_CACHE["nc"]
    maps = make_in_maps(inputs, 8)
    res = run_bass_kernel_spmd(nc, maps, core_ids=list(range(8)))
    full = np.empty((4, S, D), np.float32)
    for c in range(8):
        b, p = c // 2, c % 2
        full[b, p * SO:(p + 1) * SO] = np.asarray(res.results[c]["out"], np.float32)
    return full
```

```python
import contextlib
import numpy as np
import concourse.bass as bass
import concourse.mybir as mybir
from concourse.bass_utils import run_bass_kernel_spmd

F32 = mybir.dt.float32
BF16 = mybir.dt.bfloat16
AF = mybir.ActivationFunctionType
ALU = mybir.AluOpType

D = 1024
S = 8192
H = 16
HD = 64
FF = 2816
NFC = 22
EPS = 1e-6
NT = S // 512
NB = S // 128

C_ATTG, C_FFNG0, C_HGRNG, C_FFNG1 = 0, 8, 16, 24
C_CONVW = 32
C_CONVB = C_CONVW + 264
C_LB = C_CONVB + 88
C_ONORM = C_LB + 16
C_BF = C_ONORM + 8
C_PM = C_BF + 1
C_M0 = C_PM + 1
C_NEG = C_M0 + 1
C_NEG8 = C_NEG + 1
NCOL = C_NEG8 + 1
T0 = 8
SO = S // 2
PAIRS = [[0, 1], [2, 3], [4, 5], [6, 7]]
K_ID, K_TRI, K_BD, K_RM = 0, 128, 256, 384
NCONST = 896


class Buf:
    __slots__ = ("w", "r", "excl")

    def __init__(self, excl=False):
        self.w = None
        self.r = {}
        self.excl = excl


class Ring:
    def __init__(self, items):
        self.items = items
        self.i = 0

    def next(self):
        it = self.items[self.i % len(self.items)]
        self.i += 1
        return it


class Sched:
    def __init__(self, nc, es, nds=40):
        self.nc = nc
        self.E = {"pe": nc.tensor, "act": nc.scalar, "dve": nc.vector, "pool": nc.gpsimd, "sp": nc.sync}
        self.cnt = {e: 0 for e in ("pe", "act", "dve", "pool")}
        self.sem = {e: es.enter_context(nc.semaphore("sm_" + e)) for e in self.cnt}
        self.dsem = [es.enter_context(nc.semaphore("sd%d" % i)) for i in range(nds)]
        self.dval = [0] * nds
        self.dn = 0
        self.known = {e: {} for e in self.E}
        self.nwait = 0
        self.nops = 0
        self.ncc = 0
        self.ccev = []

    def _wait(self, eng, ev):
        sem, val = ev
        k = self.known[eng]
        key = id(sem)
        if k.get(key, 0) >= val:
            return
        k[key] = val
        self.E[eng].wait_ge(sem, val)
        self.nwait += 1

    def _deps(self, eng, r, w, own):
        skip_own = (eng == "pe")
        for b in r:
            ev = b.w
            if ev is not None and not (ev[0] is own and skip_own):
                self._wait(eng, ev)
        for b in w:
            ev = b.w
            if ev is not None and not (ev[0] is own and skip_own):
                self._wait(eng, ev)
            for ev in b.r.values():
                if not (ev[0] is own and skip_own):
                    self._wait(eng, ev)

    @staticmethod
    def _mark(ev, r, w):
        for b in r:
            b.r[id(ev[0])] = ev
        for b in w:
            b.w = ev
            b.r = {}

    def op(self, eng, fn, r=(), w=()):
        own = self.sem[eng]
        if any(b.excl for b in r):
            xs = [b for b in r if b.excl]
            r = [b for b in r if not b.excl]
            for b in xs:
                for ev in ([b.w] if b.w is not None else []) + list(b.r.values()):
                    if ev[0] is not own:
                        self._wait(eng, ev)
            w = list(w) + xs
            self._deps(eng, r, [b for b in w if b not in xs], own)
        else:
            self._deps(eng, r, w, own)
        self.cnt[eng] += 1
        ev = (own, self.cnt[eng])
        fn(self.E[eng]).then_inc(own, 1)
        self._mark(ev, r, w)
        self.nops += 1
        return ev

    def dma(self, out, in_, r=(), w=(), q="sp"):
        i = self.dn % len(self.dsem)
        self.dn += 1
        sem = self.dsem[i]
        if self.dval[i] > 0:
            self._wait(q, (sem, self.dval[i]))
        self._deps(q, r, w, None)
        self.dval[i] += 16
        ev = (sem, self.dval[i])
        self.E[q].dma_start(out=out, in_=in_).then_inc(sem, 16)
        self._mark(ev, r, w)
        self.nops += 1
        return ev

    def collective(self, es, in_ap, out_ap, r=(), w=()):
        sem = es.enter_context(self.nc.semaphore("cc%d" % self.ncc))
        self.ncc += 1
        self._deps("pool", r, w, None)
        self.nc.gpsimd.collective_compute("AllReduce", ALU.add, replica_groups=PAIRS,
                                          ins=[in_ap], outs=[out_ap]).then_inc(sem)
        ev = (sem, 1)
        self._mark(ev, r, w)
        self.ccev.append(ev)
        return ev

    def barrier(self):
        evs = [(self.sem[e], self.cnt[e]) for e in self.cnt if self.cnt[e] > 0]
        evs += [(sem, self.dval[i]) for i, sem in enumerate(self.dsem) if self.dval[i] > 0]
        evs += self.ccev
        for eng in self.E:
            own = self.sem.get(eng)
            for ev in evs:
                if ev[0] is not own:
                    self._wait(eng, ev)

    def finish(self):
        for i, sem in enumerate(self.dsem):
            if self.dval[i] > 0:
                self._wait("sp", (sem, self.dval[i]))


def build(dbg=False, upto=99):
    nc = bass.Bass("TRN2", target_bir_lowering=False)

    def I(n, shp):
        return nc.dram_tensor(n, shp, F32, kind="ExternalInput").ap()

    x_in = I("x", [S, D])
    w_att_in = I("att_w_in", [D, 3 * D + H])
    w_att_out = I("att_w_out", [D, D])
    w_h_in = I("hgrn_w_in", [D, 4 * D])
    w_h_out = I("hgrn_w_out", [D, D])
    w_up = I("ffn_w_up", [2, D, 2 * FF])
    w_dn = I("ffn_w_down", [2, FF, D])
    cols_in = I("cols", [128, NCOL])
    consts_in = I("consts", [128, NCONST])
    fgain_in = I("final_norm_g", [1, D])
    out = nc.dram_tensor("out", [SO, D], F32, kind="ExternalOutput").ap()
    sk = "ExternalOutput" if dbg else "Internal"

    def Sc(n, shp, dt):
        return nc.dram_tensor(n, shp, dt, kind=sk).ap()

    qT = Sc("qT", [D, S], BF16)
    kT = Sc("kT", [D, S], BF16)
    vv = Sc("vv", [S, D], BF16)
    caug = Sc("caug", [H, S], BF16)
    caugk = Sc("caugk", [3, H, S], BF16)
    oT = Sc("oT", [D, S], BF16)
    hTs = Sc("hTs", [D, S], BF16)
    xs = Sc("xs", [S, D], F32)
    cn_d = Sc("cn_d", [H, S], F32) if dbg else None
    halo_a = [nc.dram_tensor("halo_a%d" % l, [128, 88], F32).ap() for l in range(2)]
    halo_g = [nc.dram_tensor("halo_g%d" % l, [128, 88], F32).ap() for l in range(2)]
    eb_s = nc.dram_tensor("eb_s", [8 * 8, 128, 512], F32).ap()
    ki_s = nc.dram_tensor("ki_s", [8 * 8, 128, 512], BF16).ap()
    kit_s = nc.dram_tensor("kit_s", [8 * 8, 128, 512], BF16).ap()
    iv_s = nc.dram_tensor("iv_s", [8, 128, 4096], BF16).ap()
    st_a = nc.dram_tensor("st_a", [128, 1024], F32).ap()
    st_g = nc.dram_tensor("st_g", [128, 1024], F32).ap()

    b_qT, b_kT, b_vv, b_caug, b_oT = Buf(), Buf(), Buf(), Buf(), Buf()
    b_caugk = Buf()
    b_hTs = [Buf() for _ in range(NT)]
    b_xs = [Buf() for _ in range(NT)]
    b_out = Buf()

    with contextlib.ExitStack() as es:
        sch = Sched(nc, es)

        def TS(stack, name, shape, dt):
            return stack.enter_context(nc.sbuf_tensor(name, shape, dt))

        cols = TS(es, "cols_sb", [128, NCOL], F32)
        identb = TS(es, "identb", [128, 128], BF16)
        trib = TS(es, "trib", [128, 128], BF16)
        ones_b = TS(es, "ones_b", [128, 128], BF16)
        b_cols, b_cb = Buf(), Buf()
        sch.dma(cols[:, :], cols_in[:, :], w=[b_cols])
        with contextlib.ExitStack() as ph0:
            cst0 = TS(ph0, "cst0", [128, 256], F32)
            b_cst0 = Buf()
            sch.dma(cst0[:, :], consts_in[:, 0:256], w=[b_cst0])
            sch.op("dve", lambda e: e.tensor_copy(identb[:, :], cst0[:, K_ID:K_ID + 128]), r=[b_cst0], w=[b_cb])
            sch.op("dve", lambda e: e.tensor_copy(trib[:, :], cst0[:, K_TRI:K_TRI + 128]), r=[b_cst0], w=[b_cb])
            sch.op("dve", lambda e: e.memset(ones_b[:, :], 1.0), w=[b_cb])
            sch.barrier()

        def load_cst(stack, tag, c0, c1):
            t_ = TS(stack, "cst" + tag, [128, c1 - c0], F32)
            b_ = Buf()
            sch.dma(t_[:, :], consts_in[:, c0:c1], w=[b_])
            return t_, b_

        pbs = [es.enter_context(nc.psum_tensor("pb%d" % i, [128, 512], F32)) for i in range(8)]
        pbuf = [Buf(excl=True) for _ in range(8)]

        wcnt = [0]

        def load_w(dst, bdst, src2d, kch, ncols, stg, bstg):
            v = src2d.rearrange("(c p) n -> p c n", p=128)
            for k0 in range(0, kch, 4):
                kk = min(4, kch - k0)
                for c0 in range(0, ncols, 512):
                    wd = min(512, ncols - c0)
                    i = wcnt[0] % len(stg)
                    wcnt[0] += 1
                    sch.dma(stg[i][:, :kk, :wd], v[:, k0:k0 + kk, c0:c0 + wd], w=[bstg[i]])
                    eng = ("pool", "dve", "act")[(wcnt[0] - 1) % 3] if len(stg) > 2 else "pool"
                    if eng == "act":
                        sch.op("act", lambda e, i=i, k0=k0, kk=kk, c0=c0, wd=wd: e.copy(
                            out=dst[:, k0:k0 + kk, c0:c0 + wd], in_=stg[i][:, :kk, :wd]), r=[bstg[i]], w=[bdst])
                    else:
                        sch.op(eng, lambda e, i=i, k0=k0, kk=kk, c0=c0, wd=wd: e.tensor_copy(
                            dst[:, k0:k0 + kk, c0:c0 + wd], stg[i][:, :kk, :wd]), r=[bstg[i]], w=[bdst])

        def norm_tile(ns, xtile, bx, gc0, hT, bh, psr, nb=4):
            junk, bjunk, ssq, bssq, rstd, brstd, xn, bxn = ns
            TK = nb * 128
            for b in range(nb):
                sch.op("act", lambda e, b=b: e.activation(out=junk[:, :], in_=xtile[:, b, :], func=AF.Square,
                                                         accum_out=ssq[:, b:b + 1]), r=[bx], w=[bjunk, bssq])
            sch.op("act", lambda e: e.activation(out=rstd[:, 0:nb], in_=ssq[:, 0:nb], func=AF.Sqrt,
                                                 scale=1.0 / D, bias=EPS), r=[bssq], w=[brstd])
            sch.op("dve", lambda e: e.reciprocal(rstd[:, 0:nb], rstd[:, 0:nb]), r=[brstd], w=[brstd])
            for b in range(nb):
                sch.op("dve", lambda e, b=b: e.tensor_scalar(xn[:, b, :], xtile[:, b, :], rstd[:, b:b + 1], None,
                                                            ALU.mult), r=[bx, brstd], w=[bxn])
            for cp in range(4):
                pbk, bpb = psr.next()
                pv = pbk[:, :].bitcast(BF16)
                for b in range(nb):
                    for cc in range(2):
                        c = cp * 2 + cc
                        sch.op("pe", lambda e, b=b, cc=cc, c=c, pv=pv: e.transpose(
                            out=pv[:, cc * TK + b * 128:cc * TK + (b + 1) * 128],
                            in_=xn[:, b, c * 128:(c + 1) * 128], identity=identb[:, :]),
                            r=[bxn, b_cb], w=[bpb])
                for cc in range(2):
                    c = cp * 2 + cc
                    sch.op("dve", lambda e, cc=cc, c=c, pv=pv: e.tensor_scalar(
                        hT[:, c, :], pv[:, cc * TK:(cc + 1) * TK], cols[:, gc0 + c:gc0 + c + 1], None, ALU.mult),
                        r=[bpb, b_cols], w=[bh])

        def norm_scratch(stack, tag):
            junk = TS(stack, "junk" + tag, [128, 1024], BF16)
            ssq = TS(stack, "ssq" + tag, [128, 4], F32)
            rstd = TS(stack, "rstd" + tag, [128, 4], F32)
            xn = TS(stack, "xn" + tag, [128, 4, 1024], BF16)
            return (junk, Buf(), ssq, Buf(), rstd, Buf(), xn, Buf())

        ac_stack = contextlib.ExitStack()
        cnt_keep = TS(ac_stack, "CNT", [128, NB, H], F32)
        b_cnt = Buf()
        with contextlib.ExitStack() as phB, contextlib.ExitStack() as ph:
            NL = TS(phB, "NL", [16, S], F32)
            b_nl = Buf()
            Wqk = TS(ph, "Wqk", [128, 8, 2048], BF16)
            Wv = TS(ph, "Wv", [128, 8, 1024], BF16)
            Wf = TS(ph, "Wf", [128, 8, 16], BF16)
            bWqk, bWv, bWf = Buf(), Buf(), Buf()
            stg = [TS(ph, "stgA%d" % i, [128, 4, 512], F32) for i in range(2)]
            bstg = [Buf(), Buf()]
            negb = TS(ph, "negb", [16, 1], F32)
            b_negb = Buf()
            sch.op("dve", lambda e: e.tensor_scalar(negb[:, :], cols[0:16, C_BF:C_BF + 1], -1.0, None, ALU.mult),
                   r=[b_cols], w=[b_negb])
            load_w(Wqk, bWqk, w_att_in[:, 0:2048], 8, 2048, stg, bstg)
            load_w(Wv, bWv, w_att_in[:, 2048:3072], 8, 1024, stg, bstg)
            load_w(Wf, bWf, w_att_in[:, 3072:3088], 8, 16, stg, bstg)
            xts = Ring([(TS(ph, "xtA%d" % i, [128, 4, 1024], F32), Buf()) for i in range(2)])
            hTr = Ring([(TS(ph, "hTA%d" % i, [128, 8, 512], BF16), Buf()) for i in range(2)])
            ns = norm_scratch(ph, "A")
            qst = Ring([(TS(ph, "qst%d" % i, [128, 8, 512], BF16), Buf()) for i in range(1)])
            kst = Ring([(TS(ph, "kst%d" % i, [128, 8, 512], BF16), Buf()) for i in range(1)])
            vst = Ring([(TS(ph, "vst%d" % i, [128, 4, 1024], BF16), Buf()) for i in range(1)])
            tmpE = TS(ph, "tmpE", [16, 512], F32)
            b_tmpE = Buf()
            psT = Ring([(pbs[i], pbuf[i]) for i in (0, 1)])
            psM = Ring([(pbs[i], pbuf[i]) for i in (2, 3, 4, 5, 6, 7)])
            qTv = qT.rearrange("(g p) t -> p g t", p=128)
            kTv = kT.rearrange("(g p) t -> p g t", p=128)
            vvv = vv.rearrange("(n p) c -> p n c", p=128)
            xv = x_in.rearrange("(n p) d -> p n d", p=128)

            def load_x(t):
                xt, bx = xts.next()
                sch.dma(xt[:, :, :], xv[:, t * 4:(t + 1) * 4, :], w=[bx])
                return xt, bx

            nxt = load_x(0)
            evi = 0
            for t in range(NT):
                xt, bx = nxt
                if t + 1 < NT:
                    nxt = load_x(t + 1)
                hT, bh = hTr.next()
                norm_tile(ns, xt, bx, C_ATTG, hT, bh, psT)
                qs_, bqs = qst.next()
                ks_, bks = kst.next()
                for grp in range(16):
                    if grp < 8 and t < T0:
                        continue
                    pbk, bpb = psM.next()
                    for c in range(8):
                        sch.op("pe", lambda e, c=c, grp=grp, pbk=pbk, hT=hT: e.matmul(
                            pbk[:, :], lhsT=Wqk[:, c, grp * 128:(grp + 1) * 128], rhs=hT[:, c, :],
                            start=(c == 0), stop=(c == 7)), r=[bWqk, bh], w=[bpb])
                    dst, bd = (qs_, bqs) if grp < 8 else (ks_, bks)
                    sch.op("act", lambda e, dst=dst, grp=grp, pbk=pbk: e.copy(out=dst[:, grp % 8, :], in_=pbk[:, :]),
                           r=[bpb], w=[bd])
                if t >= T0:
                    sch.dma(qTv[:, :, t * 512:(t + 1) * 512], qs_[:, :, :], r=[bqs], w=[b_qT])
                sch.dma(kTv[:, :, t * 512:(t + 1) * 512], ks_[:, :, :], r=[bks], w=[b_kT])
                vs_, bvs = vst.next()
                for b in range(4):
                    for hf in range(2):
                        pbk, bpb = psM.next()
                        for c in range(8):
                            sch.op("pe", lambda e, c=c, b=b, hf=hf, pbk=pbk, hT=hT: e.matmul(
                                pbk[:, :], lhsT=hT[:, c, b * 128:(b + 1) * 128], rhs=Wv[:, c, hf * 512:(hf + 1) * 512],
                                start=(c == 0), stop=(c == 7)), r=[bWv, bh], w=[bpb])
                        eng = "dve" if (evi % 2 == 0) else "act"
                        evi += 1
                        if eng == "dve":
                            sch.op("dve", lambda e, b=b, hf=hf, pbk=pbk, vs_=vs_: e.tensor_copy(
                                vs_[:, b, hf * 512:(hf + 1) * 512], pbk[:, :]), r=[bpb], w=[bvs])
                        else:
                            sch.op("act", lambda e, b=b, hf=hf, pbk=pbk, vs_=vs_: e.copy(
                                out=vs_[:, b, hf * 512:(hf + 1) * 512], in_=pbk[:, :]), r=[bpb], w=[bvs])
                sch.dma(vvv[:, t * 4:(t + 1) * 4, :], vs_[:, :, :], r=[bvs], w=[b_vv])
                pbk, bpb = psM.next()
                for c in range(8):
                    sch.op("pe", lambda e, c=c, pbk=pbk, hT=hT: e.matmul(
                        pbk[0:16, :], lhsT=Wf[:, c, :], rhs=hT[:, c, :], start=(c == 0), stop=(c == 7)),
                        r=[bWf, bh], w=[bpb])
                sch.op("act", lambda e, pbk=pbk: e.activation(out=tmpE[:, :], in_=pbk[0:16, :], func=AF.Exp,
                                                            scale=-1.0, bias=negb[:, 0:1]),
                       r=[bpb, b_negb], w=[b_tmpE])
                sch.op("act", lambda e, t=t: e.activation(out=NL[:, t * 512:(t + 1) * 512], in_=tmpE[:, :],
                                                          func=AF.Ln, scale=1.0, bias=1.0), r=[b_tmpE], w=[b_nl])

            ph.close()
            ph = phB
            CN = NL
            b_cn = b_nl
            ones16 = TS(ph, "ones16", [16, 512], F32)
            b_o16 = Buf()
            qa = TS(ph, "qa", [16, S], BF16)
            b_qa = Buf()
            identf, b_cst = load_cst(ph, "B", K_ID, K_ID + 128)
            sch.op("dve", lambda e: e.memset(ones16[:, :], 1.0), w=[b_o16])
            for t in range(NT):
                init = 0.0 if t == 0 else CN[:, t * 512 - 1:t * 512]
                sch.op("dve", lambda e, t=t, init=init: e.tensor_tensor_scan(
                    CN[:, t * 512:(t + 1) * 512], ones16[:, :], NL[:, t * 512:(t + 1) * 512], init,
                    ALU.mult, ALU.add), r=[b_o16, b_cn], w=[b_cn])
            for t in range(4):
                sch.op("dve", lambda e, t=t: e.tensor_scalar(qa[:, t * 2048:(t + 1) * 2048], CN[:, t * 2048:(t + 1) * 2048],
                                                            -8.0, None, ALU.mult), r=[b_cn], w=[b_qa])
            sch.dma(caug[:, :], qa[:, :], r=[b_qa], w=[b_caug])
            if dbg:
                sch.dma(cn_d[:, :], CN[:, :], r=[b_cn])
            Wt = TS(ph, "Wt", [16, 2048], F32)
            Wr = [TS(ph, "Wr%d" % k, [16, 2048], BF16) for k in range(3)]
            b_wt, b_wr = Buf(), Buf()
            for t4 in range(4):
                sl = slice(t4 * 2048, (t4 + 1) * 2048)
                if t4 < 2:
                    sch.op("dve", lambda e, sl=sl: e.tensor_scalar(Wt[:, :], CN[:, sl], 8.0, cols[0:16, C_NEG8:C_NEG8 + 1],
                                                                  ALU.mult, ALU.add), r=[b_cn, b_cols], w=[b_wt])
                else:
                    sch.op("dve", lambda e, sl=sl: e.tensor_scalar(Wt[:, :], CN[:, sl], 8.0, None, ALU.mult),
                           r=[b_cn], w=[b_wt])
                for k in range(3):
                    sch.op("dve", lambda e, k=k: e.tensor_copy(Wr[k][:, :], Wt[:, :]), r=[b_wt], w=[b_wr])
                    if k < 2:
                        sch.op("dve", lambda e, k=k: e.tensor_tensor(Wt[:, :], Wt[:, :], Wr[k][:, :], ALU.subtract),
                               r=[b_wt, b_wr], w=[b_wt])
                    sch.dma(caugk[k, :, sl], Wr[k][:, :], r=[b_wr], w=[b_caugk])

        sch.barrier()
        if upto >= 2:
            with contextlib.ExitStack() as ph:
                KTr = Ring([(TS(ph, "KT%d" % i, [68, S], BF16), Buf()) for i in range(2)])
                QTr = Ring([(TS(ph, "QT%d" % i, [68, S], BF16), Buf()) for i in range(2)])
                VTr = Ring([(TS(ph, "VT%d" % i, [128, NB, 128], BF16), Buf()) for i in range(2)])
                OTr = Ring([(TS(ph, "OT%d" % i, [64, S], BF16), Buf()) for i in range(2)])
                PTr = Ring([(TS(ph, "PT%d" % i, [128, 512], BF16), Buf()) for i in range(6)])
                SrF = TS(ph, "SrF", [128, 512], F32)
                b_sr = Buf()
                Ssh = TS(ph, "Ssh", [64, 512], F32)
                b_ssh = Buf()
                shiftF = TS(ph, "shiftF", [128, 128], F32)
                b_shift = Buf()
                cstC = TS(ph, "cstC", [128, 128], F32)
                b_cstC = Buf()
                sch.dma(cstC[:, :], consts_in[:, K_ID:K_ID + 128], w=[b_cstC])
                sch.op("dve", lambda e: e.memset(shiftF[:, :], 0.0), w=[b_shift])
                sch.op("dve", lambda e: e.tensor_copy(shiftF[:, 0:64], cstC[:, 64:128]), r=[b_cstC], w=[b_shift])
                sch.op("dve", lambda e: e.memset(SrF[:, :], 0.0), w=[b_sr])
                for (kt, bk) in KTr.items:
                    sch.op("dve", lambda e, kt=kt: e.memset(kt[64:65, :], 1.0), w=[bk])
                for (qt_, bq_) in QTr.items:
                    sch.op("dve", lambda e, qt_=qt_: e.memset(qt_[64:68, SO:S], 1.0), w=[bq_])
                for (vt_, bv_) in VTr.items:
                    sch.op("pool", lambda e, vt_=vt_: e.memset(vt_[:, :, 64:128], 1.0), w=[bv_])
                psS = Ring([(pbs[i], pbuf[i]) for i in (0, 1, 2, 3, 4, 5)])
                psO = Ring([(pbs[i], pbuf[i]) for i in (6,)])
                psH, bpsH = pbs[7], pbuf[7]
                Ocp = TS(ph, "Ocp", [128, 512], F32)
                b_ocp = Buf()
                GK = 3
                vhv = vv.rearrange("(n p) (h d) -> p n h d", p=128, d=HD)

                def load_head(h):
                    kt, bk = KTr.next()
                    qt, bq = QTr.next()
                    vt, bv = VTr.next()
                    sch.dma(kt[0:64, :], kT[h * 64:(h + 1) * 64, :], r=[b_kT], w=[bk])
                    sch.dma(kt[65:68, :], caugk[:, h, :], r=[b_caugk], w=[bk])
                    sch.dma(qt[0:64, SO:S], qT[h * 64:(h + 1) * 64, SO:S], r=[b_qT], w=[bq])
                    sch.dma(qt[64:65, SO:S], caug[h:h + 1, SO:S], r=[b_caug], w=[bq])
                    for part in range(4):
                        sch.dma(vt[:, part * 16:(part + 1) * 16, 0:64], vhv[:, part * 16:(part + 1) * 16, h, :],
                                r=[b_vv], w=[bv])
                    return kt, bk, qt, bq, vt, bv

                nh = load_head(0)
                for h in range(H):
                    kt, bk, qt, bq, vt, bv = nh
                    if h + 1 < H:
                        nh = load_head(h + 1)
                    ot, bo = OTr.next()
                    for j in range(T0, NT):
                        nkb = 4 * j + 4
                        pso, bpo = psO.next()

                        def qk(i):
                            m = i - 4 * j
                            c0 = 128 * max(m, 0)
                            pss, bps = psS.next()
                            sch.op("pe", lambda e, i=i, c0=c0, pss=pss: e.matmul(
                                pss[:, c0:512], lhsT=kt[0:68, i * 128:(i + 1) * 128],
                                rhs=qt[0:68, j * 512 + c0:(j + 1) * 512], start=True, stop=True),
                                r=[bk, bq], w=[bps])
                            return pss, bps, c0, m

                        groups = [list(range(g0, min(g0 + GK, nkb))) for g0 in range(0, nkb, GK)]
                        cur = [qk(i) for i in groups[0]]
                        for gi, grp in enumerate(groups):
                            mine = cur
                            if gi + 1 < len(groups):
                                cur = [qk(i) for i in groups[gi + 1]]
                            pts = []
                            for i, (pss, bps, c0, m) in zip(grp, mine):
                                pt, bpt = PTr.next()
                                sch.op("act", lambda e, i=i, c0=c0, pss=pss, pt=pt: e.activation(
                                    out=pt[:, c0:512], in_=pss[:, c0:512], func=AF.Exp, scale=0.125),
                                    r=[bps], w=[bpt])
                                if m >= 0:
                                    sch.op("dve", lambda e, c0=c0, pt=pt: e.tensor_tensor(
                                        pt[:, c0:c0 + 128], pt[:, c0:c0 + 128], trib[:, :], ALU.mult),
                                        r=[bpt, b_cb], w=[bpt])
                                pts.append((i, c0, pt, bpt))
                            for (i, c0, pt, bpt) in pts:
                                sch.op("pe", lambda e, i=i, c0=c0, pt=pt: e.matmul(
                                    pso[:, c0:512], lhsT=vt[:, i, :], rhs=pt[:, c0:512],
                                    start=(i == 0), stop=(i == nkb - 1)), r=[bv, bpt], w=[bpo])
                        sch.op("act", lambda e, pso=pso: e.copy(out=Ocp[:, :], in_=pso[:, :]), r=[bpo], w=[b_ocp])
                        sch.op("dve", lambda e: e.reciprocal(SrF[64:128, :], Ocp[64:128, :]), r=[b_ocp], w=[b_sr])
                        sch.dma(Ssh[:, :], SrF[64:128, :], r=[b_sr], w=[b_ssh])
                        sch.op("dve", lambda e, j=j, ot=ot: e.tensor_tensor(
                            ot[:, j * 512:(j + 1) * 512], Ocp[0:64, :], Ssh[:, :], ALU.mult),
                            r=[b_ocp, b_ssh], w=[bo])
                    sch.dma(oT[h * 64:(h + 1) * 64, SO:S], ot[:, SO:S], r=[bo], w=[b_oT])

        sch.barrier()
        ac_stack.close()

        hTsv = hTs.rearrange("(c p) t -> p c t", p=128)
        xsv = xs.rearrange("(n p) d -> p n d", p=128)
        outv = out.rearrange("(n p) d -> p n d", p=128)

        def out_proj_phase(src_x_view, lhs_loader, w_src, gc0, tagp):
            with contextlib.ExitStack() as ph:
                Wo = TS(ph, "Wo" + tagp, [128, 8, 1024], BF16)
                bWo = Buf()
                stg = [TS(ph, "stgD%s%d" % (tagp, i), [128, 4, 512], F32) for i in range(2)]
                bstg = [Buf(), Buf()]
                load_w(Wo, bWo, w_src, 8, 1024, stg, bstg)
                lts = Ring([(TS(ph, "lt%s%d" % (tagp, i), [128, 8, 512], BF16), Buf()) for i in range(2)])
                xts = Ring([(TS(ph, "xtD%s%d" % (tagp, i), [128, 4, 1024], F32), Buf()) for i in range(2)])
                hto = Ring([(TS(ph, "hto%s%d" % (tagp, i), [128, 8, 512], BF16), Buf()) for i in range(2)])
                ns = norm_scratch(ph, "D" + tagp)
                psT = Ring([(pbs[i], pbuf[i]) for i in (0, 1)])
                psM = Ring([(pbs[i], pbuf[i]) for i in (2, 3, 4, 5)])

                def loads(t):
                    lt, bl = lts.next()
                    xt, bx = xts.next()
                    lhs_loader(t, lt, bl)
                    sch.dma(xt[:, :, :], src_x_view[:, t * 4:(t + 1) * 4, :], w=[bx])
                    return lt, bl, xt, bx

                nxt = loads(T0)
                for t in range(T0, NT):
                    lt, bl, xt, bx = nxt
                    if t + 1 < NT:
                        nxt = loads(t + 1)
                    for b in range(4):
                        for hf in range(2):
                            pbk, bpb = psM.next()
                            for c in range(8):
                                sch.op("pe", lambda e, c=c, b=b, hf=hf, pbk=pbk, lt=lt: e.matmul(
                                    pbk[:, :], lhsT=lt[:, c, b * 128:(b + 1) * 128],
                                    rhs=Wo[:, c, hf * 512:(hf + 1) * 512], start=(c == 0), stop=(c == 7)),
                                    r=[bWo, bl], w=[bpb])
                            sch.op("dve", lambda e, b=b, hf=hf, pbk=pbk, xt=xt: e.tensor_tensor(
                                xt[:, b, hf * 512:(hf + 1) * 512], pbk[:, :], xt[:, b, hf * 512:(hf + 1) * 512], ALU.add),
                                r=[bpb, bx], w=[bx])
                    sch.dma(xsv[:, t * 4:(t + 1) * 4, :], xt[:, :, :], r=[bx], w=[b_xs[t]])
                    ho, bho = hto.next()
                    norm_tile(ns, xt, bx, gc0, ho, bho, psT)
                    sch.dma(hTsv[:, :, t * 512:(t + 1) * 512], ho[:, :, :], r=[bho], w=[b_hTs[t]])
            sch.barrier()

        def ffn_phase(l, gc_next, final):
            TF = 512
            with contextlib.ExitStack() as ph:
                Wup = TS(ph, "Wup%d" % l, [128, 8, 2 * FF], BF16)
                Wdn = TS(ph, "Wdn%d" % l, [128, NFC, 1024], BF16)
                bWup, bWdn = Buf(), Buf()
                with contextlib.ExitStack() as ph2:
                    stg = [TS(ph2, "stgF%d%d" % (l, i), [128, 4, 512], F32) for i in range(6)]
                    bstg = [Buf() for _ in range(6)]
                    load_w(Wup, bWup, w_up[l], 8, 2 * FF, stg, bstg)
                    load_w(Wdn, bWdn, w_dn[l], NFC, 1024, stg, bstg)
                    sch.barrier()
                hin = Ring([(TS(ph, "hin%d%d" % (l, i), [128, 8, TF], BF16), Buf()) for i in range(2)])
                xt = TS(ph, "xtF%d" % l, [128, 4, 1024], F32)
                bx = Buf()
                R = TS(ph, "R%d" % l, [128, NFC * TF], BF16)
                bR = Buf()
                actv = R[:, :].rearrange("p (c t) -> p c t", t=TF)
                xn = R[:, 0:4096].rearrange("p (b d) -> p b d", d=1024)
                hto = R[:, 4096:8192].rearrange("p (c t) -> p c t", t=TF)
                junk = R[:, 8192:9216]
                ssq = TS(ph, "ssqF%d" % l, [128, 4], F32)
                rstd = TS(ph, "rstdF%d" % l, [128, 4], F32)
                ns = (junk, bR, ssq, Buf(), rstd, Buf(), xn, bR)
                Ur = Ring([(TS(ph, "U%d%d" % (l, i), [128, TF + 2], BF16), Buf()) for i in range(4)])
                yr = Ring([(TS(ph, "y%d%d" % (l, i), [128, TF], F32), Buf()) for i in range(4)])
                carry = TS(ph, "carry%d" % l, [128, 2 * NFC, 2], BF16)
                bcar = [Buf() for _ in range(2 * NFC)]
                if final:
                    gfin = TS(ph, "gfin", [128, 1024], F32)
                    b_gfin = Buf()
                    sch.dma(gfin[:, :], fgain_in.partition_broadcast(128), w=[b_gfin])
                psT = Ring([(pbs[i], pbuf[i]) for i in (0, 1)])
                psU = Ring([(pbs[i], pbuf[i]) for i in (2, 3, 4, 5)])
                psD = Ring([(pbs[i], pbuf[i]) for i in (6, 7)])
                NTF = S // TF
                cw0 = C_CONVW + l * 3 * 44
                cb0 = C_CONVB + l * 44
                hl = TS(ph, "hl%d" % l, [128, 8, 128], BF16)
                b_hl = Buf()
                halo = TS(ph, "halo%d" % l, [128, 2 * NFC, 2], F32)
                b_halo, b_ha, b_hg = Buf(), Buf(), Buf()
                sch.dma(hl[:, :, :], hTsv[:, :, S - 128:S], r=[b_hTs[NT - 1]], w=[b_hl])
                for ci in range(2 * NFC):
                    pbk, bpb = psU.next()
                    for c in range(8):
                        sch.op("pe", lambda e, c=c, ci=ci, pbk=pbk: e.matmul(
                            pbk[:, 0:128], lhsT=Wup[:, c, ci * 128:(ci + 1) * 128], rhs=hl[:, c, :],
                            start=(c == 0), stop=(c == 7)), r=[bWup, b_hl], w=[bpb])
                    sch.op("act", lambda e, ci=ci, pbk=pbk: e.copy(out=halo[:, ci, :], in_=pbk[:, 126:128]),
                           r=[bpb], w=[b_halo])
                sch.op("dve", lambda e: e.tensor_scalar(halo[:, :, :], halo[:, :, :], cols[:, C_M0:C_M0 + 1], None, ALU.mult),
                       r=[b_halo, b_cols], w=[b_halo])
                sch.dma(halo_a[l][:, :], halo[:, :, :].rearrange("p c k -> p (c k)"), r=[b_halo], w=[b_ha])
                sch.collective(es, halo_a[l][:, :], halo_g[l][:, :], r=[b_ha], w=[b_hg])
                sch.dma(halo[:, :, :].rearrange("p c k -> p (c k)"), halo_g[l][:, :], r=[b_hg], w=[b_halo])
                sch.op("dve", lambda e: e.tensor_scalar(carry[:, :, :], halo[:, :, :], cols[:, C_PM:C_PM + 1], None, ALU.mult),
                       r=[b_halo, b_cols], w=bcar)

                def load_h(t):
                    hi, bhi = hin.next()
                    sch.dma(hi[:, :, :], hTsv[:, :, t * TF:(t + 1) * TF], r=[b_hTs[t]], w=[bhi])
                    return hi, bhi

                def up(j, hi, bhi):
                    res = []
                    for ci in (j, NFC + j):
                        pbk, bpb = psU.next()
                        for c in range(8):
                            sch.op("pe", lambda e, c=c, ci=ci, pbk=pbk: e.matmul(
                                pbk[:, :], lhsT=Wup[:, c, ci * 128:(ci + 1) * 128], rhs=hi[:, c, :],
                                start=(c == 0), stop=(c == 7)), r=[bWup, bhi], w=[bpb])
                        res.append((ci, pbk, bpb))
                    return res

                def conv(ci, pbk, bpb):
                    U, bU = Ur.next()
                    y, by = yr.next()
                    wc = lambda tp: cols[:, cw0 + tp * 44 + ci:cw0 + tp * 44 + ci + 1]
                    sch.op("pool", lambda e: e.tensor_copy(U[:, 0:2], carry[:, ci, :]), r=[bcar[ci]], w=[bU])
                    sch.op("act", lambda e: e.copy(out=U[:, 2:TF + 2], in_=pbk[:, :]), r=[bpb], w=[bU])
                    sch.op("act", lambda e: e.activation(out=y[:, :], in_=pbk[:, :], func=AF.Identity, scale=wc(2),
                                                         bias=cols[:, cb0 + ci:cb0 + ci + 1]), r=[bpb, b_cols], w=[by])
                    sch.op("pool", lambda e: e.tensor_copy(carry[:, ci, :], U[:, TF:TF + 2]), r=[bU], w=[bcar[ci]])
                    sch.op("dve", lambda e: e.scalar_tensor_tensor(y[:, :], U[:, 1:TF + 1], wc(1), y[:, :],
                                                                   ALU.mult, ALU.add), r=[bU, by, b_cols], w=[by])
                    sch.op("dve", lambda e: e.scalar_tensor_tensor(y[:, :], U[:, 0:TF], wc(0), y[:, :],
                                                                   ALU.mult, ALU.add), r=[bU, by, b_cols], w=[by])
                    return y, by

                nxt = load_h(T0)
                sch.dma(xt[:, :, :], xsv[:, T0 * 4:(T0 + 1) * 4, :], r=[b_xs[T0]], w=[bx])
                for t in range(T0, NTF):
                    hi, bhi = nxt
                    if t + 1 < NTF:
                        nxt = load_h(t + 1)
                    cur = up(0, hi, bhi)
                    for j in range(NFC):
                        mine = cur
                        if j + 1 < NFC:
                            cur = up(j + 1, hi, bhi)
                        (cg, pg, bpg), (cv, pv_, bpv) = mine
                        yg, byg = conv(cg, pg, bpg)
                        yv, byv = conv(cv, pv_, bpv)
                        sch.op("act", lambda e, yg=yg: e.activation(out=yg[:, :], in_=yg[:, :], func=AF.Silu),
                               r=[byg], w=[byg])
                        sch.op("dve", lambda e, j=j, yg=yg, yv=yv: e.tensor_tensor(actv[:, j, :], yg[:, :], yv[:, :], ALU.mult),
                               r=[byg, byv], w=[bR])
                    for b in range(4):
                        for hf in range(2):
                            pbk, bpb = psD.next()
                            for j in range(NFC):
                                sch.op("pe", lambda e, j=j, b=b, hf=hf, pbk=pbk: e.matmul(
                                    pbk[:, :], lhsT=actv[:, j, b * 128:(b + 1) * 128], rhs=Wdn[:, j, hf * 512:(hf + 1) * 512],
                                    start=(j == 0), stop=(j == NFC - 1)), r=[bWdn, bR], w=[bpb])
                            sch.op("dve", lambda e, b=b, hf=hf, pbk=pbk: e.tensor_tensor(
                                xt[:, b, hf * 512:(hf + 1) * 512], pbk[:, :], xt[:, b, hf * 512:(hf + 1) * 512], ALU.add),
                                r=[bpb, bx], w=[bx])
                    if not final:
                        sch.dma(xsv[:, t * 4:(t + 1) * 4, :], xt[:, :, :], r=[bx], w=[b_xs[t]])
                        norm_tile(ns, xt, bx, gc_next, hto, bR, psT)
                        sch.dma(hTsv[:, :, t * TF:(t + 1) * TF], hto, r=[bR], w=[b_hTs[t]])
                    else:
                        for b in range(4):
                            sch.op("act", lambda e, b=b: e.activation(out=junk, in_=xt[:, b, :], func=AF.Square,
                                                                      accum_out=ssq[:, b:b + 1]), r=[bx], w=[bR, ns[3]])
                        sch.op("act", lambda e: e.activation(out=rstd[:, 0:4], in_=ssq[:, 0:4], func=AF.Sqrt,
                                                             scale=1.0 / D, bias=EPS), r=[ns[3]], w=[ns[5]])
                        sch.op("dve", lambda e: e.reciprocal(rstd[:, 0:4], rstd[:, 0:4]), r=[ns[5]], w=[ns[5]])
                        for b in range(4):
                            sch.op("dve", lambda e, b=b: e.scalar_tensor_tensor(
                                xt[:, b, :], xt[:, b, :], rstd[:, b:b + 1], gfin[:, :], ALU.mult, ALU.mult),
                                r=[bx, ns[5], b_gfin], w=[bx])
                        sch.dma(outv[:, (t - T0) * 4:(t - T0 + 1) * 4, :], xt[:, :, :], r=[bx], w=[b_out])
                    if t + 1 < NTF:
                        sch.dma(xt[:, :, :], xsv[:, (t + 1) * 4:(t + 2) * 4, :], r=[b_xs[t + 1]], w=[bx])
            sch.barrier()

        if upto >= 3:
            oTv = oT.rearrange("(c p) t -> p c t", p=128)
            xv0 = x_in.rearrange("(n p) d -> p n d", p=128)
            out_proj_phase(xv0, lambda t, lt, bl: sch.dma(lt[:, :, :], oTv[:, :, t * 512:(t + 1) * 512], r=[b_oT], w=[bl]),
                           w_att_out, C_FFNG0, "a")
        if upto >= 4:
            ffn_phase(0, C_HGRNG, final=(upto == 4))

        def hgrn_phase():
            with contextlib.ExitStack() as ph:
                Wh = TS(ph, "Wh", [128, 8, 4096], BF16)
                Woh = TS(ph, "Woh", [128, 8, 1024], BF16)
                bWh, bWoh = Buf(), Buf()
                with contextlib.ExitStack() as ph2:
                    stg = [TS(ph2, "stgH%d" % i, [128, 4, 512], F32) for i in range(6)]
                    bstg = [Buf() for _ in range(6)]
                    load_w(Wh, bWh, w_h_in, 8, 4096, stg, bstg)
                    load_w(Woh, bWoh, w_h_out, 8, 1024, stg, bstg)
                    sch.barrier()
                hin = Ring([(TS(ph, "hinH%d" % i, [128, 8, 512], BF16), Buf()) for i in range(2)])
                xtl = Ring([(TS(ph, "xtH%d" % i, [128, 4, 1024], F32), Buf()) for i in range(1)])
                IV = TS(ph, "IV", [128, 4, 1024], BF16)
                bIV = Buf()
                OGT = TS(ph, "OGT", [128, 8, 512], BF16)
                bOGT = Buf()
                junk = TS(ph, "junkH", [128, 1024], BF16)
                ssq = TS(ph, "ssqH", [128, 4], F32)
                rstd = TS(ph, "rstdH", [128, 4], F32)
                ns = (junk, Buf(), ssq, Buf(), rstd, Buf(), IV, bIV)
                NSET = 2
                A = [[(TS(ph, "A%d_%d" % (k, i), [128, 512], F32), Buf()) for k in range(5)] for i in range(NSET)]
                Bt = [[(TS(ph, "B%d_%d" % (k, i), [128, 512], BF16), Buf()) for k in range(6)] for i in range(NSET)]
                Ct = [[(TS(ph, "C%d_%d" % (k, i), [128, 512], F32), Buf()) for k in range(2)] for i in range(NSET)]
                Tst = TS(ph, "Tst", [128, 8, 128], F32)
                bT = [Buf() for _ in range(8)]
                Sbf = TS(ph, "Sbf", [128, 8, 2, 128], BF16)
                bS = [[Buf(), Buf()] for _ in range(8)]
                dprev = TS(ph, "dprev", [128, 8], F32)
                bdp = [Buf() for _ in range(8)]
                lbc = TS(ph, "lbc", [128, 8], F32)
                omlc = TS(ph, "omlc", [128, 8], F32)
                nomlc = TS(ph, "nomlc", [128, 8], F32)
                b_lb = Buf()
                bd4 = TS(ph, "bd4", [128, 512], BF16)
                b_bd4 = Buf()
                sch.op("dve", lambda e: e.tensor_tensor(lbc[:, :], cols[:, C_LB + 8:C_LB + 16], cols[:, C_LB:C_LB + 8],
                                                        ALU.subtract), r=[b_cols], w=[b_lb])
                sch.op("act", lambda e: e.activation(out=lbc[:, :], in_=lbc[:, :], func=AF.Sigmoid), r=[b_lb], w=[b_lb])
                sch.op("dve", lambda e: e.tensor_scalar(omlc[:, :], lbc[:, :], -1.0, 1.0, ALU.mult, ALU.add),
                       r=[b_lb], w=[b_lb])
                sch.op("dve", lambda e: e.tensor_scalar(omlc[:, :], omlc[:, :], 0.5, None, ALU.mult),
                       r=[b_lb], w=[b_lb])
                sch.op("dve", lambda e: e.tensor_tensor(lbc[:, :], lbc[:, :], omlc[:, :], ALU.add),
                       r=[b_lb], w=[b_lb])
                sch.op("dve", lambda e: e.tensor_scalar(nomlc[:, :], omlc[:, :], -1.0, None, ALU.mult),
                       r=[b_lb], w=[b_lb])
                cstH, b_cst = load_cst(ph, "H", K_BD, K_RM + 512)
                for k in range(4):
                    sch.op("dve", lambda e, k=k: e.tensor_copy(bd4[:, k * 128:(k + 1) * 128], cstH[:, 0:128]),
                           r=[b_cst], w=[b_bd4])
                rmask = cstH[:, K_RM - K_BD:K_RM - K_BD + 512]
                psP = Ring([(pbs[i], pbuf[i]) for i in (0, 1, 2, 3)])
                psA, bpsA = pbs[5], pbuf[5]
                psO, bpsO = pbs[6], pbuf[6]
                psXv = pbs[5][:, 0:256].bitcast(BF16)
                bpsX = pbuf[5]
                psUr = Ring([(pbs[4][:, 0:128], pbuf[4]), (pbs[7][:, 0:128], pbuf[7])])
                psT = Ring([(pbs[i], pbuf[i]) for i in (5, 6)])
                evi = [0]

                def loads(t):
                    hi, bhi = hin.next()
                    sch.dma(hi[:, :, :], hTsv[:, :, t * 512:(t + 1) * 512], r=[b_hTs[t]], w=[bhi])
                    return hi, bhi

                def proj(hi, bhi, hh, pre):
                    res = []
                    for base in ((1024,) if pre else (0, 3072)):
                        pbk, bpb = psP.next()
                        for c in range(8):
                            sch.op("pe", lambda e, c=c, base=base, pbk=pbk: e.matmul(
                                pbk[:, :], lhsT=Wh[:, c, base + hh * 128:base + (hh + 1) * 128], rhs=hi[:, c, :],
                                start=(c == 0), stop=(c == 7)), r=[bWh, bhi], w=[bpb])
                        res.append((pbk, bpb))
                    return res

                def elem_gen(pr, hh, st, pre, t):
                    (A1, b1), (A2, b2), (A3, b3), (A4, b4), (A5, b5) = A[st]
                    (QD, bqd), (KI, bki), (GS, bgs), (KIT, bkit) = Bt[st][0], Bt[st][1], Bt[st][2], Bt[st][3]
                    slot = (t - T0) * 8 + hh
                    if not pre:
                        (pq, bpq), (pg, bpg) = pr[0], pr[1]
                        sch.dma(A2[:, :], eb_s[slot], r=[b_ebs], w=[b2])
                        sch.dma(KI[:, :], ki_s[slot], r=[b_kis], w=[bki])
                        sch.dma(KIT[:, :], kit_s[slot], r=[b_kits], w=[bkit])
                        sch.op("act", lambda e: e.activation(out=A5[:, :], in_=pq[:, :], func=AF.Silu), r=[bpq], w=[b5])
                        sch.op("act", lambda e: e.activation(out=GS[:, :], in_=pg[:, :], func=AF.Silu), r=[bpg], w=[bgs])
                        yield
                        sch.op("dve", lambda e: e.tensor_tensor(QD[:, :], A5[:, :], A2[:, :], ALU.mult), r=[b5, b2], w=[bqd])
                        return
                    (pf, bpf) = pr[0]
                    sch.op("act", lambda e: e.activation(out=A1[:, :], in_=pf[:, :], func=AF.Tanh, scale=0.5), r=[bpf], w=[b1])
                    yield
                    sch.op("dve", lambda e: e.tensor_scalar(A2[:, :], A1[:, :], omlc[:, hh:hh + 1], lbc[:, hh:hh + 1],
                                                            ALU.mult, ALU.add), r=[b1, b_lb], w=[b2])
                    sch.op("dve", lambda e: e.tensor_scalar(A1[:, :], A1[:, :], nomlc[:, hh:hh + 1], omlc[:, hh:hh + 1],
                                                            ALU.mult, ALU.add), r=[b1, b_lb], w=[b1])
                    yield
                    sch.op("act", lambda e: e.activation(out=A2[:, :], in_=A2[:, :], func=AF.Ln), r=[b2], w=[b2])
                    yield
                    sch.op("dve", lambda e: e.tensor_tensor_scan(A3[:, :], rmask, A2[:, :], 0.0, ALU.mult, ALU.add),
                           r=[b2, b_cst], w=[b3])
                    yield
                    sch.op("act", lambda e: e.activation(out=A2[:, :], in_=A3[:, :], func=AF.Exp), r=[b3], w=[b2])
                    sch.op("act", lambda e: e.activation(out=A4[:, :], in_=A3[:, :], func=AF.Exp, scale=-1.0),
                           r=[b3], w=[b4])
                    yield
                    sch.op("dve", lambda e: e.tensor_tensor(KI[:, :], A1[:, :], A4[:, :], ALU.mult), r=[b1, b4], w=[bki])
                    sch.dma(eb_s[slot], A2[:, :], r=[b2], w=[b_ebs])
                    sch.dma(ki_s[slot], KI[:, :], r=[bki], w=[b_kis])

                def chunks(t, hh, st, pre, g=None):
                    def pull(k=1):
                        for _ in range(k):
                            if g is not None:
                                next(g, None)

                    (A2, b2) = A[st][1]
                    (QD, bqd), (KI, bki), (GS, bgs), (KIT, bkit), (AT, bat), (SQ, bsq) = Bt[st]
                    (RS, brs), (OG, bog) = Ct[st]
                    if pre:
                        for b in range(4):
                            sch.op("pe", lambda e, b=b: e.transpose(out=psXv[:, b * 128:(b + 1) * 128],
                                                                    in_=KI[:, b * 128:(b + 1) * 128], identity=identb[:, :]),
                                   r=[bki, b_cb], w=[bpsX])
                        sch.op("act", lambda e: e.copy(out=KIT[:, :], in_=psXv[:, :]), r=[bpsX], w=[bkit])
                        sch.dma(kit_s[(t - T0) * 8 + hh], KIT[:, :], r=[bkit], w=[b_kits])
                        pull()
                    if not pre:
                        for b in range(4):
                            sch.op("pe", lambda e, b=b: e.matmul(psA[:, b * 128:(b + 1) * 128], lhsT=KI[:, b * 128:(b + 1) * 128],
                                                                 rhs=QD[:, b * 128:(b + 1) * 128], start=True, stop=True),
                                   r=[bki, bqd], w=[bpsA])
                        sch.op("dve", lambda e: e.tensor_tensor(AT[:, :], psA[:, :], bd4[:, :], ALU.mult),
                               r=[bpsA, b_bd4], w=[bat])
                    for idx in range(8):
                        b, ee = idx // 2, idx % 2
                        n = 8 * (t - T0) + idx
                        loc = idx * 64
                        pu, bpu = psUr.next()
                        sch.op("pe", lambda e, b=b, ee=ee, pu=pu: e.matmul(
                            pu, lhsT=KIT[ee * 64:(ee + 1) * 64, b * 128:(b + 1) * 128],
                            rhs=IV[ee * 64:(ee + 1) * 64, b, hh * 128:(hh + 1) * 128], start=True, stop=True),
                            r=[bkit, bIV], w=[bpu])
                        if ee == 0 and not pre:
                            sch.op("pe", lambda e, b=b: e.matmul(psO[:, b * 128:(b + 1) * 128],
                                                                 lhsT=IV[:, b, hh * 128:(hh + 1) * 128],
                                                                 rhs=AT[:, b * 128:(b + 1) * 128],
                                                                 start=True, stop=False), r=[bIV, bat], w=[bpsO])
                        if not pre:
                            c0 = loc
                            sch.op("pe", lambda e, c0=c0, n=n, ee=ee: e.matmul(
                                psO[:, c0:c0 + 64], lhsT=Sbf[:, hh, (n - 1) % 2, :], rhs=QD[:, c0:c0 + 64],
                                start=False, stop=(ee == 1)), r=[bS[hh][(n - 1) % 2], bqd], w=[bpsO])
                        if n == 0 and pre:
                            sch.op("dve", lambda e, pu=pu: e.tensor_copy(Tst[:, hh, :], pu), r=[bpu], w=[bT[hh]])
                        elif n == 0:
                            sch.op("dve", lambda e, pu=pu: e.tensor_tensor(Tst[:, hh, :], pu, Sinit[:, hh, :], ALU.add),
                                   r=[bpu, b_sinit], w=[bT[hh]])
                        else:
                            dec = A2[:, loc - 1:loc] if loc > 0 else dprev[:, hh:hh + 1]
                            rr = [bpu, bT[hh], b2] if loc > 0 else [bpu, bT[hh], bdp[hh]]
                            sch.op("dve", lambda e, pu=pu, dec=dec: e.scalar_tensor_tensor(
                                Tst[:, hh, :], Tst[:, hh, :], dec, pu, ALU.mult, ALU.add), r=rr, w=[bT[hh]])
                        if not pre:
                            sch.op("dve", lambda e, loc=loc, n=n: e.tensor_scalar(
                                Sbf[:, hh, n % 2, :], Tst[:, hh, :], A2[:, loc + 63:loc + 64], None, ALU.mult),
                                r=[bT[hh], b2], w=[bS[hh][n % 2]])
                        if pre and idx % 2 == 1:
                            pull()
                        if (not pre) and idx in (3, 7):
                            pull()
                    if pre:
                        if t == NT - 1:
                            sch.op("dve", lambda e: e.tensor_scalar(Sinit[:, hh, :], Tst[:, hh, :], A2[:, 511:512], None,
                                                                    ALU.mult), r=[bT[hh], b2], w=[b_sinit])
                        sch.op("pool", lambda e: e.tensor_copy(dprev[:, hh:hh + 1], A2[:, 511:512]), r=[b2], w=[bdp[hh]])
                        return
                    sch.op("act", lambda e: e.activation(out=SQ[:, :], in_=psO[:, :], func=AF.Square), r=[bpsO], w=[bsq])
                    sch.op("pe", lambda e: e.matmul(psA[:, :], lhsT=ones_b[:, :], rhs=SQ[:, :], start=True, stop=True),
                           r=[bsq, b_cb], w=[bpsA])
                    sch.op("act", lambda e: e.activation(out=RS[:, :], in_=psA[:, :], func=AF.Ln, scale=1.0 / 128.0,
                                                         bias=EPS), r=[bpsA], w=[brs])
                    sch.op("act", lambda e: e.activation(out=RS[:, :], in_=RS[:, :], func=AF.Exp, scale=-0.5),
                           r=[brs], w=[brs])
                    sch.op("dve", lambda e: e.scalar_tensor_tensor(OG[:, :], psO[:, :], cols[:, C_ONORM + hh:C_ONORM + hh + 1],
                                                                   RS[:, :], ALU.mult, ALU.mult),
                           r=[bpsO, brs, b_cols], w=[bog])
                    sch.op("dve", lambda e: e.tensor_tensor(OGT[:, hh, :], OG[:, :], GS[:, :], ALU.mult),
                           r=[bog, bgs], w=[bOGT])
                    sch.op("pool", lambda e: e.tensor_copy(dprev[:, hh:hh + 1], A2[:, 511:512]), r=[b2], w=[bdp[hh]])

                b_ebs, b_kis, b_kits, b_ivs = Buf(), Buf(), Buf(), Buf()
                Sinit = TS(ph, "Sinit", [128, 8, 128], F32)
                b_sinit, b_sta, b_stg = Buf(), Buf(), Buf()

                def iv_proj(hi, bhi):
                    for b in range(4):
                        for hf in range(2):
                            pbk, bpb = psP.next()
                            for c in range(8):
                                sch.op("pe", lambda e, c=c, b=b, hf=hf, pbk=pbk: e.matmul(
                                    pbk[:, :], lhsT=hi[:, c, b * 128:(b + 1) * 128],
                                    rhs=Wh[:, c, 2048 + hf * 512:2048 + (hf + 1) * 512], start=(c == 0), stop=(c == 7)),
                                    r=[bWh, bhi], w=[bpb])
                            if evi[0] % 2 == 0:
                                sch.op("dve", lambda e, b=b, hf=hf, pbk=pbk: e.tensor_copy(
                                    IV[:, b, hf * 512:(hf + 1) * 512], pbk[:, :]), r=[bpb], w=[bIV])
                            else:
                                sch.op("act", lambda e, b=b, hf=hf, pbk=pbk: e.copy(
                                    out=IV[:, b, hf * 512:(hf + 1) * 512], in_=pbk[:, :]), r=[bpb], w=[bIV])
                            evi[0] += 1

                def heads(t, hi, bhi, pre):
                    pr = proj(hi, bhi, 0, pre)
                    for _ in elem_gen(pr, 0, 0, pre, t):
                        pass
                    for hh in range(8):
                        st = hh % NSET
                        g = None
                        if hh + 1 < 8:
                            pr = proj(hi, bhi, hh + 1, pre)
                            g = elem_gen(pr, hh + 1, (hh + 1) % NSET, pre, t)
                        chunks(t, hh, st, pre, g)
                        if g is not None:
                            for _ in g:
                                pass

                nxt = loads(T0)
                for t in range(T0, NT):
                    hi, bhi = nxt
                    if t + 1 < NT:
                        nxt = loads(t + 1)
                    iv_proj(hi, bhi)
                    sch.dma(iv_s[t - T0], IV[:, :, :].rearrange("p b c -> p (b c)"), r=[bIV], w=[b_ivs])
                    heads(t, hi, bhi, True)
                sch.op("dve", lambda e: e.tensor_scalar(Sinit[:, :, :], Sinit[:, :, :], cols[:, C_M0:C_M0 + 1], None, ALU.mult),
                       r=[b_sinit, b_cols], w=[b_sinit])
                sch.dma(st_a[:, :], Sinit[:, :, :].rearrange("p h v -> p (h v)"), r=[b_sinit], w=[b_sta])
                sch.collective(es, st_a[:, :], st_g[:, :], r=[b_sta], w=[b_stg])
                sch.dma(Sinit[:, :, :].rearrange("p h v -> p (h v)"), st_g[:, :], r=[b_stg], w=[b_sinit])
                sch.op("dve", lambda e: e.tensor_scalar(Sinit[:, :, :], Sinit[:, :, :], cols[:, C_PM:C_PM + 1], None, ALU.mult),
                       r=[b_sinit, b_cols], w=[b_sinit])
                sch.op("dve", lambda e: e.tensor_copy(Sbf[:, :, 1, :], Sinit[:, :, :]), r=[b_sinit], w=[bS[hh][1] for hh in range(8)])

                nxt = loads(T0)
                for t in range(T0, NT):
                    hi, bhi = nxt
                    if t + 1 < NT:
                        nxt = loads(t + 1)
                    xt, bx = xtl.next()
                    sch.dma(xt[:, :, :], xsv[:, t * 4:(t + 1) * 4, :], r=[b_xs[t]], w=[bx])
                    sch.dma(IV[:, :, :].rearrange("p b c -> p (b c)"), iv_s[t - T0], r=[b_ivs], w=[bIV])
                    heads(t, hi, bhi, False)
                    for b in range(4):
                        for hf in range(2):
                            pbk, bpb = psP.next()
                            for hh in range(8):
                                sch.op("pe", lambda e, hh=hh, b=b, hf=hf, pbk=pbk: e.matmul(
                                    pbk[:, :], lhsT=OGT[:, hh, b * 128:(b + 1) * 128],
                                    rhs=Woh[:, hh, hf * 512:(hf + 1) * 512], start=(hh == 0), stop=(hh == 7)),
                                    r=[bWoh, bOGT], w=[bpb])
                            sch.op("dve", lambda e, b=b, hf=hf, pbk=pbk, xt=xt: e.tensor_tensor(
                                xt[:, b, hf * 512:(hf + 1) * 512], pbk[:, :], xt[:, b, hf * 512:(hf + 1) * 512], ALU.add),
                                r=[bpb, bx], w=[bx])
                    sch.dma(xsv[:, t * 4:(t + 1) * 4, :], xt[:, :, :], r=[bx], w=[b_xs[t]])
                    norm_tile(ns, xt, bx, C_FFNG1, OGT, bOGT, psT)
                    sch.dma(hTsv[:, :, t * 512:(t + 1) * 512], OGT[:, :, :], r=[bOGT], w=[b_hTs[t]])
            sch.barrier()

        if upto >= 5:
            hgrn_phase()
        if upto >= 6:
            ffn_phase(1, 0, final=True)

        sch.finish()
        print("sched: ops", sch.nops, "waits", sch.nwait, "cnt", sch.cnt, flush=True)
    return nc


def host_tables(inp):
    f = np.float32

    def colize(v):
        v = np.asarray(v, f)
        return np.ascontiguousarray(v.reshape(-1, 128).T)

    cols = np.zeros((128, NCOL), f)
    cols[:, C_ATTG:C_ATTG + 8] = colize(inp["att_norm_g"][0])
    cols[:, C_FFNG0:C_FFNG0 + 8] = colize(inp["ffn_norm_g"][0])
    cols[:, C_HGRNG:C_HGRNG + 8] = colize(inp["hgrn_norm_g"][0])
    cols[:, C_FFNG1:C_FFNG1 + 8] = colize(inp["ffn_norm_g"][1])
    cw = np.asarray(inp["ffn_conv_w"], f)
    cb = np.asarray(inp["ffn_conv_b"], f)
    for l in range(2):
        for tp in range(3):
            o = C_CONVW + (l * 3 + tp) * 44
            cols[:, o:o + 44] = colize(cw[l, tp])
        o = C_CONVB + l * 44
        cols[:, o:o + 44] = colize(cb[l])
    lbl = np.asarray(inp["hgrn_lb_logits"], f)
    for l in range(2):
        cols[:, C_LB + l * 8:C_LB + (l + 1) * 8] = colize(lbl[l])
    cols[:, C_ONORM:C_ONORM + 8] = colize(inp["hgrn_onorm_g"][0])
    cols[0:16, C_BF] = np.asarray(inp["att_b_f"], f)[0]
    cst = np.zeros((128, NCONST), f)
    cst[:, K_ID:K_ID + 128] = np.eye(128, dtype=f)
    ii = np.arange(128)
    cst[:, K_TRI:K_TRI + 128] = (ii[:, None] <= ii[None, :]).astype(f)
    bd = (ii[:, None] <= ii[None, :]) & ((ii[:, None] // 64) == (ii[None, :] // 64))
    cst[:, K_BD:K_BD + 128] = bd.astype(f)
    rm = np.ones((128, 512), f)
    rm[:, 0::64] = 0.0
    cst[:, K_RM:K_RM + 512] = rm
    return cols, cst


_NC_CACHE = {}


def make_in_maps(inp, n_cores=8):
    cols, cst = host_tables(inp)
    x = np.asarray(inp["x"], np.float32)
    shared = {
        "att_w_in": np.ascontiguousarray(np.asarray(inp["att_w_in"], np.float32)[0]),
        "att_w_out": np.ascontiguousarray(np.asarray(inp["att_w_out"], np.float32)[0]),
        "hgrn_w_in": np.ascontiguousarray(np.asarray(inp["hgrn_w_in"], np.float32)[0]),
        "hgrn_w_out": np.ascontiguousarray(np.asarray(inp["hgrn_w_out"], np.float32)[0]),
        "ffn_w_up": np.ascontiguousarray(np.asarray(inp["ffn_w_up"], np.float32)),
        "ffn_w_down": np.ascontiguousarray(np.asarray(inp["ffn_w_down"], np.float32)),
        "cols": cols, "consts": cst,
        "final_norm_g": np.ascontiguousarray(np.asarray(inp["final_norm_g"], np.float32).reshape(1, D)),
    }
    maps = []
    for c in range(n_cores):
        b, p = c // 2, c % 2
        m = dict(shared)
        m["x"] = np.ascontiguousarray(np.concatenate([x[b, 0:SO], x[b, p * SO:(p + 1) * SO]], axis=0))
        cc = cols.copy()
        cc[:, C_PM] = float(p)
        cc[:, C_M0] = float(1 - p)
        cc[:, C_NEG] = float((p - 1) * 30000)
        cc[:, C_NEG8] = float((p - 1) * 240000)
        m["cols"] = cc
        maps.append(m)
    return maps


def kernel(**inputs):
    if "nc" not in _NC_CACHE:
        _NC_CACHE["nc"] = build()
    nc = _NC_CACHE["nc"]
    maps = make_in_maps(inputs, 8)
    res = run_bass_kernel_spmd(nc, maps, core_ids=list(range(8)))
    full = np.empty((4, S, D), np.float32)
    for c in range(8):
        b, p = c // 2, c % 2
        full[b, p * SO:(p + 1) * SO] = np.asarray(res.results[c]["out"], np.float32)
    return full
```
